# Optimizing a Trainium2 kernel written in Bass

```python
import math
import jax, jax.numpy as jnp
from jax import lax
import numpy as np

D_MODEL = 1024
BATCH = 4
SEQ = 8192
DEPTH = 1

HEAD_DIM = 64
ATTN_WIDTH = D_MODEL // 2
N_ATTN_HEADS = ATTN_WIDTH // HEAD_DIM
SG_WIDTH = D_MODEL // 2
SG_GROUP_DIM = 64
N_SG_GROUPS = SG_WIDTH // SG_GROUP_DIM
SG_CHUNK = 128
MEM_WIDTH = D_MODEL // 2
N_MEM_HEADS = 4
MEM_HEAD_DIM = MEM_WIDTH // N_MEM_HEADS
N_MEM = 256
D_MIX = ATTN_WIDTH + SG_WIDTH + MEM_WIDTH
IN_WIDTHS = (ATTN_WIDTH, ATTN_WIDTH, ATTN_WIDTH, ATTN_WIDTH,
             2 * SG_WIDTH, SG_WIDTH,
             MEM_WIDTH, MEM_WIDTH)
IN_WIDTH = sum(IN_WIDTHS)
DILATED_CONFIGS = ((128, 1), (512, 4), (2048, 16))
BAND_BLOCK = 64
N_REL_BUCKETS = 32
REL_MAX_DISTANCE = 1024
EPS = 1e-6
NEG_BIG = -1e30

kernel_name = "hybrid_dilated_sgu_memory_layer"


def rms_norm(x, g):
    xf = x.astype(jnp.float32)
    y = xf * lax.rsqrt(jnp.mean(xf * xf, axis=-1, keepdims=True) + EPS)
    return (y * g.astype(jnp.float32)).astype(x.dtype)


def layer_norm(x, g, b):
    xf = x.astype(jnp.float32)
    mu = jnp.mean(xf, axis=-1, keepdims=True)
    var = jnp.mean(jnp.square(xf - mu), axis=-1, keepdims=True)
    y = (xf - mu) * lax.rsqrt(var + EPS)
    return (y * g.astype(jnp.float32) + b.astype(jnp.float32)).astype(x.dtype)


def t5_bucket(rel):
    half = N_REL_BUCKETS // 2
    max_exact = half // 2
    ret = (rel > 0).astype(np.int32) * half
    n = np.abs(rel)
    large = max_exact + (np.log(np.maximum(n, 1).astype(np.float32) / max_exact)
                         / math.log(REL_MAX_DISTANCE / max_exact)
                         * (half - max_exact)).astype(np.int32)
    large = np.minimum(large, half - 1)
    return (ret + np.where(n < max_exact, n, large)).astype(np.int32)


def dilated_band_attention(q, k, v, rel_bias, window, dilation):
    B, S, H, Dh = q.shape
    half = window // (2 * dilation)
    blk = BAND_BLOCK
    seg = dilation * blk
    Sp = -(-S // seg) * seg
    pad = Sp - S
    M = Sp // dilation
    nb = M // blk

    def to_blocks(t):
        t = jnp.pad(t, ((0, 0), (0, pad), (0, 0), (0, 0)))
        t = t.reshape(B, M, dilation, H, Dh).transpose(0, 2, 1, 3, 4)
        return t.reshape(B, dilation, nb, blk, H, Dh)

    def neighbours(t):
        tp = jnp.pad(t, ((0, 0), (0, 0), (1, 1), (0, 0), (0, 0), (0, 0)))
        return jnp.concatenate([tp[:, :, :-2], tp[:, :, 1:-1], tp[:, :, 2:]], axis=3)

    qb = to_blocks(q)
    kn = neighbours(to_blocks(k))
    vn = neighbours(to_blocks(v))

    qi = np.arange(blk)[:, None]
    kj = np.arange(3 * blk)[None, :]
    rel = kj - blk - qi
    band = np.abs(rel) <= half
    bucket = t5_bucket(rel * dilation)
    pos = np.arange(M)[None, :] * dilation + np.arange(dilation)[:, None]
    valid = (pos < S).reshape(dilation, nb, blk)
    valid = np.pad(valid, ((0, 0), (1, 1), (0, 0)))
    valid = np.concatenate([valid[:, :-2], valid[:, 1:-1], valid[:, 2:]], axis=2)
    allowed = band[None, None] & valid[:, :, None, :]

    bias = jnp.transpose(rel_bias.astype(jnp.float32)[bucket], (2, 0, 1))
    s = jnp.einsum('brnqhe,brnkhe->brnhqk', qb, kn).astype(jnp.float32) * (Dh ** -0.5)
    s = s + bias
    s = jnp.where(allowed[None, :, :, None], s, NEG_BIG)
    m = jnp.max(s, axis=-1, keepdims=True)
    p = jnp.exp(s - m)
    l = jnp.sum(p, axis=-1, keepdims=True)
    o = jnp.einsum('brnhqk,brnkhe->brnqhe', p / l, vn.astype(jnp.float32))
    lse = jnp.transpose((m + jnp.log(l))[..., 0], (0, 1, 2, 4, 3))

    o = o.reshape(B, dilation, M, H, Dh).transpose(0, 2, 1, 3, 4).reshape(B, Sp, H, Dh)[:, :S]
    lse = lse.reshape(B, dilation, M, H).transpose(0, 2, 1, 3).reshape(B, Sp, H)[:, :S]
    return o, lse


def setup_inputs(seed: int = 0) -> dict:
    key = jax.random.key(seed)
    ks = jax.random.split(key, 14)
    f32 = jnp.float32
    x = jax.random.normal(ks[0], (BATCH, SEQ, D_MODEL), f32)
    mem = jax.random.normal(ks[1], (BATCH, N_MEM, D_MODEL), f32)
    norm_g = 1.0 + 0.02 * jax.random.normal(ks[2], (DEPTH, D_MODEL), f32)
    mem_norm_g = 1.0 + 0.02 * jax.random.normal(ks[3], (DEPTH, D_MODEL), f32)
    w_in = jax.random.normal(ks[4], (DEPTH, D_MODEL, IN_WIDTH), f32) * D_MODEL ** -0.5
    sg_ln_g = 1.0 + 0.02 * jax.random.normal(ks[5], (DEPTH, SG_WIDTH), f32)
    sg_ln_b = 0.02 * jax.random.normal(ks[6], (DEPTH, SG_WIDTH), f32)
    sg_w = jax.random.normal(ks[7], (DEPTH, N_SG_GROUPS, SG_CHUNK, SG_CHUNK), f32) * SG_CHUNK ** -0.5
    sg_b = 1.0 + 0.02 * jax.random.normal(ks[8], (DEPTH, N_SG_GROUPS, SG_CHUNK), f32)
    w_mem_kv = jax.random.normal(ks[9], (DEPTH, D_MODEL, 2 * MEM_WIDTH), f32) * D_MODEL ** -0.5
    w_out = jax.random.normal(ks[10], (DEPTH, D_MIX, D_MODEL), f32) * D_MIX ** -0.5
    rel_bias = 0.1 * jax.random.normal(ks[11], (N_REL_BUCKETS, N_ATTN_HEADS), f32)
    final_norm_g = 1.0 + 0.02 * jax.random.normal(ks[12], (D_MODEL,), f32)
    return {"x": x, "mem": mem, "norm_g": norm_g, "mem_norm_g": mem_norm_g,
            "w_in": w_in, "sg_ln_g": sg_ln_g, "sg_ln_b": sg_ln_b, "sg_w": sg_w,
            "sg_b": sg_b, "w_mem_kv": w_mem_kv, "w_out": w_out,
            "rel_bias": rel_bias, "final_norm_g": final_norm_g}


def reference(x, mem, norm_g, mem_norm_g, w_in, sg_ln_g, sg_ln_b, sg_w, sg_b,
              w_mem_kv, w_out, rel_bias, final_norm_g):
    B, S, _ = x.shape
    split_points = list(np.cumsum(IN_WIDTHS)[:-1])
    for layer in range(DEPTH):
        h = rms_norm(x, norm_g[layer])
        proj = h @ w_in[layer]
        q, k, v, g_att, sg_uv, g_sg, q_mem, g_mem = jnp.split(proj, split_points, axis=-1)

        qh = q.reshape(B, S, N_ATTN_HEADS, HEAD_DIM)
        kh = k.reshape(B, S, N_ATTN_HEADS, HEAD_DIM)
        vh = v.reshape(B, S, N_ATTN_HEADS, HEAD_DIM)
        outs, lses = [], []
        for window, dilation in DILATED_CONFIGS:
            o, l = dilated_band_attention(qh, kh, vh, rel_bias, window, dilation)
            outs.append(o)
            lses.append(l)
        wts = jax.nn.softmax(jnp.stack(lses), axis=0)
        att = jnp.sum(wts[..., None] * jnp.stack(outs), axis=0)
        att = att.reshape(B, S, ATTN_WIDTH).astype(x.dtype)
        y_att = att * jax.nn.silu(g_att)

        uv = jax.nn.gelu(sg_uv)
        u, vv = jnp.split(uv, 2, axis=-1)
        vv = layer_norm(vv, sg_ln_g[layer], sg_ln_b[layer])
        vv = vv.reshape(B, S // SG_CHUNK, SG_CHUNK, N_SG_GROUPS, SG_GROUP_DIM)
        mixed = jnp.einsum('gpq,bcqge->bcpge', sg_w[layer], vv)
        mixed = mixed + jnp.transpose(sg_b[layer])[None, None, :, :, None]
        y_sg = (u * mixed.reshape(B, S, SG_WIDTH)) * jax.nn.silu(g_sg)

        mem_kv = rms_norm(mem, mem_norm_g[layer]) @ w_mem_kv[layer]
        km, vm = jnp.split(mem_kv, 2, axis=-1)
        km = km.reshape(B, N_MEM, N_MEM_HEADS, MEM_HEAD_DIM)
        vm = vm.reshape(B, N_MEM, N_MEM_HEADS, MEM_HEAD_DIM)
        qm = q_mem.reshape(B, S, N_MEM_HEADS, MEM_HEAD_DIM)
        sm = jnp.einsum('bshe,bmhe->bhsm', qm, km).astype(jnp.float32) * (MEM_HEAD_DIM ** -0.5)
        pm = jax.nn.softmax(sm, axis=-1)
        om = jnp.einsum('bhsm,bmhe->bshe', pm, vm.astype(jnp.float32))
        y_mem = om.reshape(B, S, MEM_WIDTH).astype(x.dtype) * jax.nn.silu(g_mem)

        y = jnp.concatenate([y_att, y_sg, y_mem], axis=-1) @ w_out[layer]
        x = x + y
    return rms_norm(x, final_norm_g)
```

```python
import contextlib
import math
import numpy as np
import ml_dtypes
import concourse.bass as bass
import concourse.mybir as mybir
from concourse.bass_utils import run_bass_kernel_spmd

F32 = mybir.dt.float32
BF16 = mybir.dt.bfloat16
AF = mybir.ActivationFunctionType
ALU = mybir.AluOpType

ENGS = ("pe", "act", "dve", "pool", "sp")
NDMA = 12
DM = 1024
SBT = 2048
HALO = 1024
EPS = 1e-6
NEG = -30000.0
CFGS = (1, 4, 16)


class Sched:
    def __init__(self, nc):
        self.nc = nc
        self.ops = {e: [] for e in ENGS}
        self.last_writer = {}
        self.readers = {}

    def _add(self, eng, fn, reads, writes, dma, extra=None, nobar=False):
        idx = len(self.ops[eng])
        me = (eng, idx)
        deps = set()
        raw = set()
        for k in reads:
            w = self.last_writer.get(k)
            if w is not None:
                deps.add(w)
                raw.add(w)
        for k in writes:
            w = self.last_writer.get(k)
            if w is not None:
                deps.add(w)
            for r in self.readers.get(k, ()):
                deps.add(r)
        deps.discard(me)
        if not dma:
            if eng == "pe":
                deps = {d for d in deps if d[0] != "pe" or self.ops["pe"][d[1]]["dma"]}
        if extra:
            deps |= set(extra)
        self.ops[eng].append(dict(fn=fn, deps=deps, dma=dma, signal=False, nobar=nobar))
        for k in reads:
            self.readers.setdefault(k, []).append(me)
        for k in writes:
            self.last_writer[k] = me
            self.readers[k] = []
        return me

    def op(self, eng, fn, reads=(), writes=()):
        return self._add(eng, fn, tuple(reads), tuple(writes), False)

    def dma(self, q, fn, reads=(), writes=(), nobar=False):
        return self._add(q, fn, tuple(reads), tuple(writes), True, nobar=nobar)

    def barrier(self):
        deps = set()
        for e in ENGS:
            ndma = 0
            seen_c = False
            for i in range(len(self.ops[e]) - 1, -1, -1):
                o = self.ops[e][i]
                if o["dma"]:
                    if ndma < NDMA:
                        if not o["nobar"]:
                            deps.add((e, i))
                        ndma += 1
                elif o["fn"] is not None and not seen_c:
                    deps.add((e, i))
                    seen_c = True
                if seen_c and ndma >= NDMA:
                    break
        for e in ENGS:
            self._add(e, None, (), (), False, extra={d for d in deps})

    def emit(self):
        nc = self.nc
        for e in ENGS:
            for o in self.ops[e]:
                for (de, di) in o["deps"]:
                    self.ops[de][di]["signal"] = True
        for e in ENGS:
            c = 0
            nd = 0
            for o in self.ops[e]:
                if o["dma"]:
                    o["dslot"] = nd % NDMA
                    o["dval"] = 16 * (nd // NDMA + 1)
                    nd += 1
                elif o["signal"]:
                    assert o["fn"] is not None
                    c += 1
                    o["cnt"] = c
        with contextlib.ExitStack() as st:
            csem = {e: st.enter_context(nc.semaphore("c_" + e)) for e in ENGS}
            dsem = {e: [st.enter_context(nc.semaphore("d_%s_%d" % (e, i))) for i in range(NDMA)]
                    for e in ("sp", "pool", "act")}
            block = st.enter_context(nc.Block())

            def run(e, eng):
                waited = {}
                for o in self.ops[e]:
                    need = {}
                    for (de, di) in o["deps"]:
                        d = self.ops[de][di]
                        if d["dma"]:
                            sem, val = dsem[de][d["dslot"]], d["dval"]
                        else:
                            sem, val = csem[de], d["cnt"]
                        k = id(sem)
                        if k not in need or need[k][1] < val:
                            need[k] = (sem, val)
                    if o["dma"]:
                        sem = dsem[e][o["dslot"]]
                        if o["dval"] > 16:
                            k = id(sem)
                            if k not in need or need[k][1] < o["dval"] - 16:
                                need[k] = (sem, o["dval"] - 16)
                    for k, (sem, val) in need.items():
                        if waited.get(k, 0) >= val:
                            continue
                        waited[k] = val
                        eng.wait_ge(sem, val)
                    if o["fn"] is None:
                        continue
                    ins = o["fn"](eng)
                    if o["dma"]:
                        ins.then_inc(dsem[e][o["dslot"]], 16)
                    elif o["signal"]:
                        ins.then_inc(csem[e], 1)

            @block.sync
            def _(eng):
                run("sp", eng)

            @block.tensor
            def _(eng):
                run("pe", eng)

            @block.scalar
            def _(eng):
                run("act", eng)

            @block.vector
            def _(eng):
                run("dve", eng)

            @block.gpsimd
            def _(eng):
                run("pool", eng)


class Arena:
    def __init__(self, t, n):
        self.t, self.n, self.off = t, n, 0

    def reset(self, off=0):
        self.off = off

    def alloc(self, cols):
        v = self.t[:, self.off:self.off + cols]
        self.off += cols
        assert self.off <= self.n, (self.off, self.n)
        return v


def build_nc(NSB):
    nc = bass.Bass("TRN2", target_bir_lowering=False)
    NTOK = NSB * SBT

    def din(name, shape):
        return nc.dram_tensor(name, list(shape), F32, kind="ExternalInput").ap()

    xs = din("xs", [NTOK + 2 * HALO, DM])
    mem = din("mem", [256, DM])
    w_in = din("w_in", [DM, 4608])
    w_mkv = din("w_mem_kv", [DM, 1024])
    w_out = din("w_out", [1536, DM])
    norm_g = din("norm_g", [1, DM])
    mem_norm_g = din("mem_norm_g", [1, DM])
    final_g = din("final_norm_g", [1, DM])
    sg_ln_g = din("sg_ln_g", [1, 512])
    sg_ln_b = din("sg_ln_b", [1, 512])
    sg_w = din("sg_w", [8, 128, 128])
    sg_b = din("sg_b", [8, 128])
    biasM = din("biasM", [128, 4, 1536])
    emask = din("emask", [128, 2 * NSB])
    ident_d = din("ident", [128, 128])
    out = nc.dram_tensor("out", [NTOK, DM], F32, kind="ExternalOutput").ap()

    wb_in = nc.dram_tensor("wb_in", [DM, 4608], BF16, kind="Internal").ap()
    wb_out = nc.dram_tensor("wb_out", [1536, DM], BF16, kind="Internal").ap()
    wb_bias = nc.dram_tensor("wb_bias", [128, 4, 1536], BF16, kind="Internal").ap()
    kv_scr = nc.dram_tensor("kv_scr", [4, 2, 128, 2048], BF16, kind="Internal").ap()
    w_in_v = wb_in.rearrange("(kc p) c -> p kc c", p=128)
    w_mkv_v = w_mkv.rearrange("(kc p) c -> p kc c", p=128)
    w_out_v = wb_out.rearrange("(kc p) c -> p kc c", p=128)

    with contextlib.ExitStack() as st:
        def sb(name, shape, dt):
            return st.enter_context(nc.sbuf_tensor("s_" + name, list(shape), dt))

        HCOLS = 55296
        FCOLS = 10240
        hT_own = sb("hT_own", [128, 8, SBT], BF16)
        attT = sb("attT", [128, 4, SBT], BF16)
        ident = sb("ident", [128, 128], BF16)
        ones = sb("ones", [128, 128], BF16)
        WsT = sb("WsT", [128, 8, 128], BF16)
        Bp = sb("Bp", [128, 4, 128], F32)
        lng = sb("lng", [128, 4], F32)
        kmT = sb("kmT", [128, 4, 256], BF16)
        vm = sb("vm", [128, 2, 512], BF16)
        emk = sb("emk", [128, 2 * NSB], F32)
        small = sb("small", [128, 64], F32)
        bigH_t = sb("bigH", [128, HCOLS], BF16)
        bigF_t = sb("bigF", [128, FCOLS], F32)
        pball = st.enter_context(nc.psum_tensor("pball", [128, 8, 512], F32))
        AH = Arena(bigH_t, HCOLS)
        AF_ = Arena(bigF_t, FCOLS)

        S = Sched(nc)
        PB = lambda k: ("pb", k)

        def bank(k):
            return pball[:, k, :]

        def bank_bf(k):
            return pball[:, k, :].bitcast(BF16)

        def ACT(out_, in_, func, rd, wr, **kw):
            S.op("act", lambda e: e.activation(out=out_, in_=in_, func=func, **kw), rd, wr)

        def MM(out_, lhsT, rhs, start, stop, rd, wr):
            S.op("pe", lambda e: e.matmul(out_, lhsT, rhs, start=start, stop=stop), rd, wr)

        def TR(out_, in_, rd, wr):
            S.op("pe", lambda e: e.transpose(out_, in_, ident[:]), list(rd) + ["ident"], wr)

        def TT(eng, out_, in0, in1, op, rd, wr):
            S.op(eng, lambda e: e.tensor_tensor(out=out_, in0=in0, in1=in1, op=op), rd, wr)

        def STT(eng, out_, in0, scalar, in1, op0, op1, rd, wr):
            S.op(eng, lambda e: e.scalar_tensor_tensor(out=out_, in0=in0, scalar=scalar, in1=in1,
                                                       op0=op0, op1=op1), rd, wr)

        def TS(eng, out_, in0, s1, s2, op0, op1, rd, wr):
            if op1 is None:
                S.op(eng, lambda e: e.tensor_scalar(out=out_, in0=in0, scalar1=s1, scalar2=None,
                                                    op0=op0), rd, wr)
            else:
                S.op(eng, lambda e: e.tensor_scalar(out=out_, in0=in0, scalar1=s1, scalar2=s2,
                                                    op0=op0, op1=op1), rd, wr)

        def CP(eng, out_, in_, rd, wr):
            S.op(eng, lambda e: e.tensor_copy(out=out_, in_=in_), rd, wr)

        def RCP(out_, in_, rd, wr):
            S.op("dve", lambda e: e.reciprocal(out=out_, in_=in_), rd, wr)

        def RCPF(out_, in_, rd, wr):
            S.op("dve", lambda e: e.reciprocal_approx_fast(out=out_, in_=in_), rd, wr)

        def MSET(eng, ap, val, wr):
            S.op(eng, lambda e: e.memset(ap, val), (), wr)

        def DMA(q, out_, in_, rd, wr, nobar=False):
            S.dma(q, lambda e: e.dma_start(out=out_, in_=in_), rd, wr, nobar=nobar)

        small_i = [0]

        def scol():
            i = small_i[0] % 64
            small_i[0] += 1
            return small[:, i:i + 1], ("small", i)

        def rms_rows(x_ap, x_key, sq_ap, sq_key):
            ss, kss = scol()
            ACT(sq_ap, x_ap, AF.Square, [x_key], [sq_key, kss], accum_out=ss)
            sd, ksd = scol()
            ACT(sd, ss, AF.Sqrt, [kss], [ksd], bias=EPS, scale=1.0 / DM)
            rs, krs = scol()
            RCP(rs, sd, [ksd], [krs])
            return rs, krs

        evac_rr = [0]

        def norm_transpose_pipe(n, src_fn, gbc, gkey, xa, hb, sq, dst_fn, tag="", tiles=None):
            pend = {}

            def stage1(t):
                i = t % len(xa)
                DMA("sp", xa[i], src_fn(t), [], [(tag + "xa", i)])
                ss, kss = scol()
                ACT(sq, xa[i], AF.Square, [(tag + "xa", i)], [tag + "sq", kss], accum_out=ss)
                sd, ksd = scol()
                ACT(sd, ss, AF.Sqrt, [kss], [ksd], bias=EPS, scale=1.0 / DM)
                pend[t] = (sd, ksd)

            def stage2(t):
                i = t % len(xa)
                ih = t % len(hb)
                sd, ksd = pend.pop(t)
                rs, krs = scol()
                RCP(rs, sd, [ksd], [krs])
                STT("dve", hb[ih], xa[i], rs, gbc, ALU.mult, ALU.mult,
                    [(tag + "xa", i), krs, gkey], [(tag + "hb", ih)])

            def stage3(t):
                i = t % len(hb)
                k = t % 3
                pv = bank_bf(k).rearrange("p (a b) -> p a b", b=128)
                for kc in range(8):
                    TR(pv[:, kc, :], hb[i][:, kc * 128:(kc + 1) * 128], [(tag + "hb", i)], [PB(k)])

            def stage4(t):
                k = t % 3
                pv = bank_bf(k).rearrange("p (a b) -> p a b", b=128)
                dst, dkey = dst_fn(t)
                eng = "act" if (evac_rr[0] % 2 == 0) else "dve"
                evac_rr[0] += 1
                if eng == "act":
                    ACT(dst, pv, AF.Copy, [PB(k)], [dkey])
                else:
                    CP("dve", dst, pv, [PB(k)], [dkey])

            tl = list(range(n)) if tiles is None else list(tiles)
            n = len(tl)
            for i in range(n + 3):
                if i < n:
                    stage1(tl[i])
                if 0 <= i - 1 < n:
                    stage2(tl[i - 1])
                if 0 <= i - 2 < n:
                    stage3(tl[i - 2])
                if 0 <= i - 3 < n:
                    stage4(tl[i - 3])

        DMA("pool", ident[:], ident_d, [], ["ident"])
        MSET("pool", ones[:], 1.0, ["ones"])
        DMA("sp", emk[:], emask, [], ["emk"])
        WBIN = [("wbin", kc) for kc in range(8)]
        WBOUT = [("wbout", kc) for kc in range(12)]
        WBIN2 = [("wbin2", kc) for kc in range(8)]
        w_in_f = w_in.rearrange("(kc p) c -> p kc c", p=128)

        def convert_hp(hp):
            c0 = hp * 128
            for grp in range(4):
                c = grp * 512 + c0
                DMA("pool", w_in_v[:, :, c:c + 128], w_in_f[:, :, c:c + 128], [], [("wbin_hp", hp, grp)],
                    nobar=True)
            DMA("pool", wb_bias[:, hp, :], biasM[:, hp, :], [], [("wbb", hp)], nobar=True)

        convert_hp(0)

        def convert_rest():
            for kc in range(8):
                DMA("pool", wb_in[kc * 128:(kc + 1) * 128, 2048:4608],
                    w_in[kc * 128:(kc + 1) * 128, 2048:4608], [], [("wbin2", kc)], nobar=True)
            for kc in range(12):
                DMA("pool", wb_out[kc * 128:(kc + 1) * 128, :], w_out[kc * 128:(kc + 1) * 128, :], [],
                    [("wbout", kc)], nobar=True)

        sst = {}

        def emit_setup_loads():
            sst["xa"] = [AF_.alloc(1024), AF_.alloc(1024)]
            sst["gbc"] = AF_.alloc(1024)
            sst["hb"] = [AH.alloc(1024), AH.alloc(1024)]
            sst["sq"] = AH.alloc(1024)
            sst["wm"] = AH.alloc(8192).rearrange("p (a b) -> p a b", b=1024)
            sst["memnT"] = AH.alloc(2048).rearrange("p (a b) -> p a b", b=256)
            sst["lnb_bc"] = AH.alloc(512)
            sst["sgw"] = AH.alloc(1024).rearrange("p (a b) -> p a b", b=128)
            sst["sgbb"] = AF_.alloc(512).rearrange("p (a b) -> p a b", b=128)
            DMA("pool", sst["gbc"], mem_norm_g.partition_broadcast(128)[:, 0, :], [], ["gbc_s"])
            sst["wm_loads"] = lambda: [DMA("pool", sst["wm"][:, kc, :], w_mkv_v[:, kc, :], [("hT", 16 + 2 * kc)],
                                           [("wm", kc)]) for kc in range(8)]
            DMA("pool", sst["sgw"], sg_w.rearrange("g p q -> p g q"), [], ["sgw"])
            DMA("pool", sst["lnb_bc"], sg_ln_b.partition_broadcast(128)[:, 0, :], [], ["lnb"])
            for j in range(4):
                DMA("pool", lng[:, j:j + 1], sg_ln_g[0:1, j * 128:(j + 1) * 128].rearrange("a p -> p a"),
                    [], ["lng"])
                for hh in range(2):
                    g = 2 * j + hh
                    DMA("pool", sst["sgbb"][hh * 64:(hh + 1) * 64, j, :],
                        sg_b[g:g + 1, :].partition_broadcast(64)[:, 0, :], [], ["sgbb"])

        def emit_setup():
            xa, gbc, hb, sq = sst["xa"], sst["gbc"], sst["hb"], sst["sq"]
            wm, memnT, lnb_bc, sgw, sgbb = (sst["wm"], sst["memnT"], sst["lnb_bc"], sst["sgw"],
                                            sst["sgbb"])
            norm_transpose_pipe(2, lambda t: mem[t * 128:(t + 1) * 128, :], gbc, "gbc_s", xa, hb, sq,
                                lambda t: (memnT[:, :, t * 128:(t + 1) * 128], ("memnT", t)), tag="s_")
            wmk = [("wm", kc) for kc in range(8)]
            mnk = [("memnT", 0), ("memnT", 1)]
            for h in range(4):
                k = 2 + h % 2
                for kc in range(8):
                    MM(bank(k)[:, 0:256], wm[:, kc, h * 128:(h + 1) * 128], memnT[:, kc, :],
                       kc == 0, kc == 7, wmk + mnk, [PB(k)])
                ACT(kmT[:, h, :], bank(k)[:, 0:256], AF.Copy, [PB(k)], ["kmT"])
            for mt in range(2):
                k = 4 + mt
                for kc in range(8):
                    MM(bank(k), memnT[:, kc, mt * 128:(mt + 1) * 128], wm[:, kc, 512:1024],
                       kc == 0, kc == 7, wmk + mnk, [PB(k)])
                ACT(vm[:, mt, :], bank(k), AF.Copy, [PB(k)], ["vm"])
            pv6 = bank_bf(6).rearrange("p (a b) -> p a b", b=128)
            for g in range(8):
                TR(pv6[:, g, :], sgw[:, g, :], ["sgw"], [PB(6)])
            CP("dve", WsT[:], pv6, [PB(6)], ["WsT"])
            for j in range(4):
                for hh in range(2):
                    g = 2 * j + hh
                    MM(bank(7)[hh * 64:(hh + 1) * 64, j * 128:(j + 1) * 128],
                       lnb_bc[:, g * 64:(g + 1) * 64], WsT[:, g, :], True, True,
                       ["lnb", "WsT"], [PB(7)])
            TT("dve", Bp[:], bank(7).rearrange("p (a b) -> p a b", b=128), sgbb, ALU.add,
               [PB(7), "sgbb"], ["Bp"])


        AH.reset()
        hT_halo = AH.alloc(16384).rearrange("p (a b) -> p a b", b=2048)
        wsets = []
        for i in range(2):
            ws = dict(
                wq=AH.alloc(1024).rearrange("p (a b) -> p a b", b=128),
                wk=AH.alloc(1024).rearrange("p (a b) -> p a b", b=128),
                wv=AH.alloc(1024).rearrange("p (a b) -> p a b", b=128),
                wg=AH.alloc(1024).rearrange("p (a b) -> p a b", b=128),
                mbf=AH.alloc(1536), i=i)
            ws["mb"] = ws["mbf"].rearrange("p (c h w) -> p c h w", c=3, h=2)
            wsets.append(ws)
        AB_H0 = AH.off

        def load_wset(ws, hp):
            c0 = hp * 128
            i = ws["i"]
            DMA("sp", ws["wk"], w_in_v[:, :, 512 + c0:512 + c0 + 128], [("wbin_hp", hp, 1)], [("wk", i)])
            DMA("sp", ws["wv"], w_in_v[:, :, 1024 + c0:1024 + c0 + 128], [("wbin_hp", hp, 2)], [("wv", i)])
            DMA("sp", ws["wq"], w_in_v[:, :, c0:c0 + 128], [("wbin_hp", hp, 0)], [("wq", i)])
            DMA("sp", ws["wg"], w_in_v[:, :, 1536 + c0:1536 + c0 + 128], [("wbin_hp", hp, 3)], [("wg", i)])
            DMA("sp", ws["mbf"], wb_bias[:, hp, :], [("wbb", hp)], [("mb", i)])

        def hT_blk(b):
            if b in (0, 1):
                v = hT_halo[:, :, b * 512:(b + 1) * 512]
            elif b in (6, 7):
                v = hT_halo[:, :, 1024 + (b - 6) * 512:1024 + (b - 5) * 512]
            else:
                v = hT_own[:, :, (b - 2) * 512:(b - 1) * 512]
            return v, [("hT", 4 * b + i) for i in range(4)]

        ALLT = []
        for ci, d in enumerate(CFGS):
            A = HALO // d
            nq = SBT // (128 * d)
            for r in range(d):
                for j in range(nq + 1):
                    ALLT.append((ci, d, r, j, nq, A))
        NT = len(ALLT)

        for s in range(NSB):
            AH.reset(AB_H0); AF_.reset()
            hb = [AH.alloc(1024) for _ in range(4)]
            sq = AH.alloc(1024)
            xa = [AF_.alloc(1024) for _ in range(4)]
            gbc = AF_.alloc(1024)
            DMA("sp", gbc, norm_g.partition_broadcast(128)[:, 0, :], [], ["gbc"])
            if s == 0:
                emit_setup_loads()

            if s == 0:
                pass

            def dstA(t):
                v, _ = hT_blk(t // 4)
                o = (t % 4) * 128
                return v[:, :, o:o + 128], ("hT", t)
            tiles_a = None if s == 0 else list(range(8, 32))
            norm_transpose_pipe(32, lambda t: xs[s * SBT + t * 128:s * SBT + (t + 1) * 128, :],
                                gbc, "gbc", xa, hb, sq, dstA, tiles=tiles_a)
            if s == 0:
                sst["wm_loads"]()
                emit_setup()
            load_wset(wsets[0], 0)
            S.barrier()

            AH.reset(AB_H0); AF_.reset()
            Q2 = AF_.alloc(2048).bitcast(BF16).rearrange("p (h t) -> p h t", h=2)
            QA = Q2[:, 0, :]
            QB = Q2[:, 1, :]
            KT = AH.alloc(4096)
            VT = AH.alloc(4096)
            Vt_flat = AH.alloc(NT * 256)
            Vt = Vt_flat.rearrange("p (a b) -> p a b", b=256)
            PT = [AH.alloc(512).rearrange("p (a b) -> p a b", b=256) for _ in range(3)]
            GT = AF_.alloc(2048)
            acc = AF_.alloc(4096).rearrange("p (a b) -> p a b", b=2048)
            rl = AF_.alloc(2048)
            MSET("pool", Vt[:, :, 64:192], 1.0, ["Vt1"])
            MSET("pool", QA[64:128, :], 0.0, ["QTz0"])
            MSET("pool", QB[0:64, :], 0.0, ["QTz1"])
            KTK = [("KT", b_) for b_ in range(8)]
            VTK = [("VT", b_) for b_ in range(8)]
            QTK = [("QT", b_, h_) for b_ in range(4) for h_ in range(2)] + ["QTz0", "QTz1"]
            GTK = [("GT", b_) for b_ in range(4)]
            if s == 0:
                for hp_ in (1, 2, 3):
                    convert_hp(hp_)
                convert_rest()
            ACCK = [("acc", i) for i in range(16)]

            for hp in range(4):
                ws = wsets[hp % 2]
                wi = ws["i"]
                if hp + 1 < 4:
                    load_wset(wsets[(hp + 1) % 2], hp + 1)
                pk = 0
                if s >= 1:
                    DMA("sp", KT[:, 0:2048], kv_scr[hp, 0], [("kvs", hp, 0)], [("KT", b_) for b_ in range(4)])
                    DMA("sp", VT[:, 0:2048], kv_scr[hp, 1], [("kvs", hp, 1)], [("VT", b_) for b_ in range(4)])
                for b in list(range(8)) + [12, 13, 14, 15]:
                    if b < 8:
                        hv, hk = hT_blk(b)
                        jobs = [(("wk", wi), ws["wk"], KT[:, b * 512:(b + 1) * 512], ("KT", b), AF.Copy, 1.0),
                                (("wv", wi), ws["wv"], VT[:, b * 512:(b + 1) * 512], ("VT", b), AF.Copy, 1.0)]
                        if s >= 1 and b < 4:
                            jobs = []
                        if 2 <= b <= 5:
                            o = (b - 2) * 512
                            jobs += [(("wq", wi), ws["wq"], None, ("QT", b - 2), AF.Copy, 0.125)]
                    else:
                        hv, hk = hT_blk(b - 10)
                        o = (b - 12) * 512
                        jobs = [(("wg", wi), ws["wg"], GT[:, o:o + 512], ("GT", b - 12), AF.Silu, 1.0)]
                    for (wkey, wt, dst, dkey, fn, sc) in jobs:
                        k = pk % 4
                        pk += 1
                        for kc in range(8):
                            MM(bank(k), wt[:, kc, :], hv[:, kc, :], kc == 0, kc == 7,
                               [wkey] + hk, [PB(k)])
                        if dst is None:
                            ACT(QA[0:64, o:o + 512], bank(k)[0:64, :], fn, [PB(k)], [dkey + (0,)], scale=sc)
                            ACT(QB[64:128, o:o + 512], bank(k)[64:128, :], fn, [PB(k)], [dkey + (1,)], scale=sc)
                        else:
                            ACT(dst, bank(k), fn, [PB(k)], [dkey], scale=sc)
                if s + 1 < NSB:
                    DMA("sp", kv_scr[hp, 0], KT[:, 2048:4096], [("KT", b_) for b_ in range(4, 8)],
                        [("kvs", hp, 0)])
                    DMA("sp", kv_scr[hp, 1], VT[:, 2048:4096], [("VT", b_) for b_ in range(4, 8)],
                        [("kvs", hp, 1)])
                for g0 in range(0, NT, 8):
                    grp = ALLT[g0:g0 + 8]
                    k = (g0 // 8) % 4
                    pvv = bank_bf(k).rearrange("p (a b) -> p a b", b=128)
                    for i, (ci, d, r, j, nq, A) in enumerate(grp):
                        u0 = r + d * (A + 128 * j - 64)
                        TR(pvv[:, i, :], VT[:, u0:u0 + d * 127 + 1:d], VTK, [PB(k)])
                    n = len(grp)
                    CP("dve", Vt[:, g0:g0 + n, 0:64], pvv[:, 0:n, 0:64], [PB(k)], [("Vt", g0, 0)])
                    CP("dve", Vt[:, g0:g0 + n, 192:256], pvv[:, 0:n, 64:128], [PB(k)], [("Vt", g0, 1)])

                def geom(ti):
                    ci, d, r, j, nq, A = ALLT[ti]
                    halves = []
                    if j >= 1:
                        halves.append((0, j - 1))
                    if j <= nq - 1:
                        halves.append((1, j))
                    return ci, d, r, j, nq, A, halves

                def stage1(ti):
                    ci, d, r, j, nq, A, halves = geom(ti)
                    u0 = r + d * (A + 128 * j - 64)
                    W = 128 * len(halves)
                    mcol0 = 128 * halves[0][0]
                    q0 = r + d * 128 * halves[0][1]
                    sk = ti % 4
                    so = pball[:, sk, 0:2 * W]
                    MM(so, ident[:], ws["mb"][:, ci, :, mcol0:mcol0 + W],
                       True, False, ["ident", ("mb", wi)], [PB(sk)])
                    MM(so, KT[:, u0:u0 + d * 127 + 1:d], Q2[:, :, q0:q0 + d * (W - 1) + 1:d],
                       False, True, KTK + QTK, [PB(sk)])

                def stage2(ti):
                    ci, d, r, j, nq, A, halves = geom(ti)
                    W = 128 * len(halves)
                    sk = ti % 4
                    if j == 0:
                        bias = emk[:, 2 * s:2 * s + 1]
                    elif j == nq:
                        bias = emk[:, 2 * s + 1:2 * s + 2]
                    else:
                        bias = 0.0
                    ACT(PT[ti % 3][:, :, 0:W],
                        pball[:, sk, 0:2 * W].rearrange("p (h w) -> p h w", h=2), AF.Exp,
                        [PB(sk), "emk"], [("PT", ti % 3)], bias=bias)

                st3 = dict(ctr=0, prev=None, cur=None)

                def stage3(ti):
                    ci, d, r, j, nq, A, halves = geom(ti)
                    pt = PT[ti % 3]
                    for hi_, (hf, cj) in enumerate(halves):
                        if hf == 1:
                            slot = st3["ctr"] % 2
                            st3["ctr"] += 1
                            st3["cur"] = slot
                            opening = True
                        else:
                            slot = st3["prev"]
                            opening = False
                        ko = 4 + 2 * slot
                        cs = hi_ * 128
                        vk = ["Vt1", ("Vt", (ti // 8) * 8, 0), ("Vt", (ti // 8) * 8, 1)]
                        MM(pball[:, ko, 0:128], Vt[:, ti, 0:128], pt[:, 0, cs:cs + 128],
                           opening, not opening, vk + [("PT", ti % 3)], [PB(ko)])
                        MM(pball[:, ko + 1, 0:128], Vt[:, ti, 128:256], pt[:, 1, cs:cs + 128],
                           opening, not opening, vk + [("PT", ti % 3)], [PB(ko + 1)])
                        if not opening:
                            t0 = r + d * 128 * cj
                            dst = acc[:, :, t0:t0 + d * 127 + 1:d]
                            src = pball[:, ko:ko + 2, 0:128]
                            blk0 = (d * 128 * cj) // 128
                            aks = ACCK[blk0:blk0 + d]
                            if ci == 0:
                                CP("dve", dst, src, [PB(ko), PB(ko + 1)], aks)
                            else:
                                TT("dve", dst, src, dst, ALU.add, [PB(ko), PB(ko + 1)] + aks, aks)
                    if j <= nq - 1:
                        st3["prev"] = st3["cur"]

                for i in range(NT + 2):
                    if i < NT:
                        stage1(i)
                    if 0 <= i - 1 < NT:
                        stage2(i - 1)
                    if 0 <= i - 2 < NT:
                        stage3(i - 2)
                if hp == 3:
                    S.barrier()
                RCP(rl[0:64, :], acc[64:128, 0, :], ACCK + ["FT"], ["rl"])
                RCP(rl[64:128, :], acc[0:64, 1, :], ACCK + ["FT"], ["rl"])
                TT("pool", rl, rl, GT, ALU.mult, ["rl", "FT"] + GTK, ["rl"])
                TT("pool", attT[0:64, hp, :], acc[0:64, 0, :], rl[0:64, :], ALU.mult,
                   ACCK + ["rl", "FT"], [("attT", hp)])
                TT("pool", attT[64:128, hp, :], acc[64:128, 1, :], rl[64:128, :], ALU.mult,
                   ACCK + ["rl", "FT"], [("attT", hp)])

            AH.reset(); AF_.reset()
            wC = [AH.alloc(4096).rearrange("p (a b) -> p a b", b=512) for _ in range(5)]
            wo = AH.alloc(12288).rearrange("p (a b) -> p a b", b=1024)
            uT = AH.alloc(2048).rearrange("p (a b) -> p a b", b=512)
            sgT = AH.alloc(2048).rearrange("p (a b) -> p a b", b=512)
            sgmT = AH.alloc(2048).rearrange("p (a b) -> p a b", b=512)
            vvn = [AH.alloc(512) for _ in range(4)]
            ysgT = AH.alloc(2048).rearrange("p (a b) -> p a b", b=512)
            qmT = AH.alloc(2048).rearrange("p (a b) -> p a b", b=512)
            PTm = [AH.alloc(512) for _ in range(4)]
            ymemT = AH.alloc(2048).rearrange("p (a b) -> p a b", b=512)
            sqj = AH.alloc(1024)
            vv = [AF_.alloc(512) for _ in range(4)]
            rlm = [AF_.alloc(512), AF_.alloc(512)]
            xr = [AF_.alloc(1024) for _ in range(4)]
            fgbc = AF_.alloc(1024)
            ms = [AF_.alloc(512).rearrange("p (a b) -> p a b", b=128) for _ in range(2)]
            usg = AH.alloc(2048).rearrange("p (a b) -> p a b", b=512)

            cgrp = [2048, 2560, 3072, 3584, 4096]
            for gi in (1, 0, 2, 4, 3):
                cc = cgrp[gi]
                for kc in range(8):
                    DMA("sp", wC[gi][:, kc, :], w_in_v[:, kc, cc:cc + 512], WBIN2, [("wC", gi, kc)])
            for kc in range(12):
                DMA("sp", wo[:, kc, :], w_out_v[:, kc, :], WBOUT, [("wo", kc)])
            DMA("sp", fgbc, final_g.partition_broadcast(128)[:, 0, :], [], ["fgbc", "FT"])
            w_u, w_v2, w_gs, w_qm, w_gm = wC
            free_banks = list(range(8))

            def balloc():
                assert free_banks, "PSUM banks exhausted"
                return free_banks.pop(0)

            def bfree(k):
                free_banks.append(k)

            def balloc_pair():
                best = None
                for a_ in free_banks:
                    if a_ < 7 and (a_ + 1) in free_banks:
                        sc = max(free_banks.index(a_), free_banks.index(a_ + 1))
                        if best is None or sc < best[0]:
                            best = (sc, a_)
                assert best is not None, "no adjacent PSUM bank pair free"
                a_ = best[1]
                free_banks.remove(a_)
                free_banks.remove(a_ + 1)
                return a_

            bst = [dict() for _ in range(4)]

            def groups_P(bb):
                hv = hT_own[:, :, bb * 512:(bb + 1) * 512]
                hk = [("hT", 8 + 4 * bb + i) for i in range(4)]
                stt_ = [dict() for _ in range(4)]
                bst[bb]["stt"] = stt_
                gl = []

                def vv_group(tt):
                    d_ = stt_[tt]
                    k = balloc()
                    for kc in range(8):
                        MM(bank(k), hv[:, kc, tt * 128:(tt + 1) * 128], w_v2[:, kc, :],
                           kc == 0, kc == 7, [("wC", 1, kc)] + hk, [PB(k)])
                    d_["sm"], d_["ksm"] = scol()
                    ACT(vv[tt], bank(k), AF.Gelu_apprx_tanh, [PB(k)],
                        [("vv", tt), d_["ksm"]], accum_out=d_["sm"])
                    bfree(k)

                def fm_group(wt, gi, dst, dkey, fn, ft):
                    k = balloc()
                    for kc in range(8):
                        MM(bank(k), wt[:, kc, ft * 128:(ft + 1) * 128], hv[:, kc, :],
                           kc == 0, kc == 7, [("wC", gi, kc)] + hk, [PB(k)])
                    ACT(dst[:, ft, :], bank(k), fn, [PB(k)], [(dkey, ft)])
                    bfree(k)

                for tt in range(4):
                    gl.append(lambda tt=tt: vv_group(tt))
                for (wt, gi, dst, dkey, fn) in ((w_u, 0, uT, "uT", AF.Gelu_apprx_tanh),
                                                (w_gs, 2, sgT, "sgT", AF.Silu),
                                                (w_qm, 3, qmT, "qmT", AF.Copy),
                                                (w_gm, 4, sgmT, "sgmT", AF.Silu)):
                    for ft in range(4):
                        gl.append(lambda wt=wt, gi=gi, dst=dst, dkey=dkey, fn=fn, ft=ft:
                                  fm_group(wt, gi, dst, dkey, fn, ft))
                return gl

            filler = []

            def fill(n):
                for _ in range(n):
                    if filler:
                        filler.pop(0)()

            def stage_M(bb):
                ft_ = ["FT"] if bb == 0 else []
                stt_ = bst[bb]["stt"]
                for tt in range(4):
                    d_ = stt_[tt]
                    d_["sq2"], d_["ksq"] = scol()
                    ACT(sqj[:, 0:512], vv[tt], AF.Square, [("vv", tt)], ["sqj", d_["ksq"]],
                        accum_out=d_["sq2"])
                fen, kfen = scol()
                ACT(fen, stt_[3]["sq2"], AF.Copy, [stt_[i_]["ksq"] for i_ in range(4)], [kfen])
                for tt in range(4):
                    d_ = stt_[tt]
                    d_["mean"], d_["kmean"] = scol()
                    TS("dve", d_["mean"], d_["sm"], 1.0 / 512, None, ALU.mult, None,
                       [d_["ksm"], kfen], [d_["kmean"]])
                for tt in range(4):
                    d_ = stt_[tt]
                    d_["m2"], d_["km2"] = scol()
                    TT("dve", d_["m2"], d_["mean"], d_["mean"], ALU.mult, [d_["kmean"]], [d_["km2"]])
                for tt in range(4):
                    d_ = stt_[tt]
                    d_["var"], d_["kvar"] = scol()
                    STT("dve", d_["var"], d_["sq2"], 1.0 / 512, d_["m2"], ALU.mult, ALU.subtract,
                        [d_["ksq"], d_["km2"], kfen], [d_["kvar"]])
                for tt in range(4):
                    d_ = stt_[tt]
                    d_["sd"], d_["ksd"] = scol()
                    ACT(d_["sd"], d_["var"], AF.Sqrt, [d_["kvar"]], [d_["ksd"]], bias=EPS, scale=1.0)
                for tt in range(4):
                    d_ = stt_[tt]
                    d_["rs"], d_["krs"] = scol()
                    RCP(d_["rs"], d_["sd"], [d_["ksd"]], [d_["krs"]])
                for tt in range(4):
                    d_ = stt_[tt]
                    d_["nb"], d_["knb"] = scol()
                    STT("dve", d_["nb"], d_["mean"], -1.0, d_["rs"], ALU.mult, ALU.mult,
                        [d_["kmean"], d_["krs"]], [d_["knb"]])
                for tt in range(4):
                    d_ = stt_[tt]
                    ACT(vvn[tt], vv[tt], AF.Identity, [("vv", tt), d_["krs"], d_["knb"]],
                        [("vvn", tt)], bias=d_["nb"], scale=d_["rs"])
                TT("pool", usg, uT, sgT, ALU.mult,
                   [("uT", f_) for f_ in range(4)] + [("sgT", f_) for f_ in range(4)], ["usg"])
                fill(8)
                mixb = []
                for tt in range(4):
                    km_ = balloc()
                    mixb.append(km_)
                    for j in range(4):
                        for hh in range(2):
                            g = 2 * j + hh
                            MM(bank(km_)[hh * 64:(hh + 1) * 64, j * 128:(j + 1) * 128],
                               vvn[tt][:, g * 64:(g + 1) * 64], WsT[:, g, :], True, True,
                               [("vvn", tt), "WsT"], [PB(km_)])
                fill(4)
                for tt in range(4):
                    i = tt % 2
                    bmv = bank(mixb[tt]).rearrange("p (a b) -> p a b", b=128)
                    for j in range(4):
                        STT("dve", ms[i][:, j, :], bmv[:, j, :], lng[:, j:j + 1], Bp[:, j, :],
                            ALU.mult, ALU.add, [PB(mixb[tt]), "lng", "Bp"], [("ms", i, j)] + ft_)
                    bfree(mixb[tt])
                    TT("pool", ysgT[:, :, tt * 128:(tt + 1) * 128], ms[i],
                       usg[:, :, tt * 128:(tt + 1) * 128], ALU.mult,
                       [("ms", i, j_) for j_ in range(4)] + ["usg"], [("ysgT", tt)])
                sbk = {}

                def scores(h):
                    for mt in range(2):
                        k = balloc()
                        sbk[(h, mt)] = k
                        MM(bank(k), kmT[:, h, mt * 128:(mt + 1) * 128], qmT[:, h, :], True, True,
                           ["kmT", ("qmT", h)], [PB(k)])

                bst[bb]["ky0"] = balloc_pair()
                scores(0)
                scores(1)
                for h in range(4):
                    if h + 2 < 4:
                        scores(h + 2)
                    for mt in range(2):
                        pi = (2 * h + mt) % 4
                        ACT(PTm[pi], bank(sbk[(h, mt)]), AF.Exp, [PB(sbk[(h, mt)])], [("PTm", pi)],
                            scale=1.0 / math.sqrt(128.0))
                        bfree(sbk[(h, mt)])
                    kO = balloc()
                    kL = balloc()
                    for mt in range(2):
                        pi = (2 * h + mt) % 4
                        MM(bank(kL), ones[:], PTm[pi], mt == 0, mt == 1,
                           ["ones", ("PTm", pi)], [PB(kL)])
                    for mt in range(2):
                        pi = (2 * h + mt) % 4
                        MM(bank(kO), vm[:, mt, h * 128:(h + 1) * 128], PTm[pi], mt == 0, mt == 1,
                           ["vm", ("PTm", pi)], [PB(kO)])
                    i = h % 2
                    RCP(rlm[i], bank(kL), [PB(kL)], [("rlm", i)] + ft_)
                    bfree(kL)
                    TT("pool", rlm[i], rlm[i], sgmT[:, h, :], ALU.mult, [("rlm", i), ("sgmT", h)],
                       [("rlm", i)])
                    TT("dve", ymemT[:, h, :], bank(kO), rlm[i], ALU.mult, [PB(kO), ("rlm", i)],
                       [("ymemT", h)])
                    bfree(kO)
                    if h == 1:
                        fill(4)
                    if h == 3:
                        fill(4)

            def stage_O(bb):
                ft_ = ["FT"] if bb == 0 else []
                for tt in range(4):
                    own0 = bb * 512 + tt * 128
                    r0 = s * SBT + HALO + own0
                    DMA("sp", xr[tt], xs[r0:r0 + 128, :], [], [("xr", tt)] + ft_)
                for tt in range(4):
                    own0 = bb * 512 + tt * 128
                    k0_ = bst[bb].pop("ky0") if tt == 0 else balloc_pair()
                    ky = [k0_, k0_ + 1]
                    for half in range(2):
                        for kc in range(12):
                            if kc < 4:
                                lt, lk = attT[:, kc, own0:own0 + 128], ("attT", kc)
                            elif kc < 8:
                                lt, lk = ysgT[:, kc - 4, tt * 128:(tt + 1) * 128], ("ysgT", tt)
                            else:
                                lt, lk = ymemT[:, kc - 8, tt * 128:(tt + 1) * 128], ("ymemT", kc - 8)
                            MM(bank(ky[half]), lt, wo[:, kc, half * 512:(half + 1) * 512],
                               kc == 0, kc == 11, [lk, ("wo", kc)], [PB(ky[half])])
                    xv = xr[tt].rearrange("p (a b) -> p a b", b=512)
                    TT("dve", xv, pball[:, ky[0]:ky[0] + 2, :], xv, ALU.add,
                       [PB(ky[0]), PB(ky[1]), ("xr", tt)], [("xr", tt)])
                    bfree(ky[0])
                    bfree(ky[1])
                rss = []
                for tt in range(4):
                    ss, kss = scol()
                    ACT(sqj, xr[tt], AF.Square, [("xr", tt)], ["sqj", kss], accum_out=ss)
                    rss.append([ss, kss])
                for tt in range(4):
                    sd, ksd = scol()
                    ACT(sd, rss[tt][0], AF.Sqrt, [rss[tt][1]], [ksd], bias=EPS, scale=1.0 / DM)
                    rss[tt] += [sd, ksd]
                for tt in range(4):
                    rs, krs = scol()
                    RCP(rs, rss[tt][2], [rss[tt][3]], [krs])
                    rss[tt] += [rs, krs]
                for tt in range(4):
                    rs, krs = rss[tt][4], rss[tt][5]
                    STT("dve", xr[tt], xr[tt], rs, fgbc, ALU.mult, ALU.mult,
                        [("xr", tt), krs, "fgbc"], [("xr", tt)])
                    o0 = s * SBT + bb * 512 + tt * 128
                    DMA("sp", out[o0:o0 + 128, :], xr[tt], [("xr", tt)], [("out", s, bb, tt)])

            filler.extend(groups_P(0))
            fill(20)
            for bb in range(4):
                if bb + 1 < 4:
                    filler.extend(groups_P(bb + 1))
                stage_M(bb)
                fill(20)
                stage_O(bb)
            S.barrier()
        S.emit()
    return nc


def _t5_bucket(rel):
    nb_, md = 32, 1024
    half = nb_ // 2
    max_exact = half // 2
    ret = (rel > 0).astype(np.int32) * half
    n = np.abs(rel)
    large = max_exact + (np.log(np.maximum(n, 1).astype(np.float32) / max_exact)
                         / math.log(md / max_exact) * (half - max_exact)).astype(np.int32)
    large = np.minimum(large, half - 1)
    return (ret + np.where(n < max_exact, n, large)).astype(np.int32)


def _bias_layout(rel_bias):
    rel_bias = np.asarray(rel_bias, np.float32)
    kk = np.arange(128)[:, None]
    qq = np.arange(256)[None, :]
    rel = kk - qq + 64
    band = np.abs(rel) <= 64
    outp = np.empty((128, 4, 3, 2, 256), np.float32)
    for ci, d in enumerate(CFGS):
        bucket = _t5_bucket(rel * d)
        g = rel_bias[bucket]
        for h in range(8):
            m = g[:, :, h].copy()
            m[~band] = NEG
            outp[:, h // 2, ci, h % 2, :] = m
    return np.ascontiguousarray(outp.reshape(128, 4, 1536))


_NC_CACHE = {}


def _run(inputs, NSB):
    x = np.asarray(inputs["x"], np.float32)
    B, S, _ = x.shape
    per_core = NSB * SBT
    cps = S // per_core
    n_cores = B * cps
    if NSB not in _NC_CACHE:
        _NC_CACHE[NSB] = build_nc(NSB)
    nc = _NC_CACHE[NSB]
    biasM = _bias_layout(inputs["rel_bias"])
    ident = np.eye(128, dtype=np.float32)
    f = lambda k: np.ascontiguousarray(np.asarray(inputs[k], np.float32))
    w_in = f("w_in")[0]
    w_mkv = f("w_mem_kv")[0]
    w_out = f("w_out")[0]
    sg_w = f("sg_w")[0]
    sg_b = f("sg_b")[0]
    in_maps = []
    for c in range(n_cores):
        b, part = divmod(c, cps)
        t0 = part * per_core
        slab = np.zeros((per_core + 2 * HALO, DM), np.float32)
        lo, hi = t0 - HALO, t0 + per_core + HALO
        slo, shi = max(lo, 0), min(hi, S)
        slab[slo - lo:shi - lo] = x[b, slo:shi]
        em = np.zeros((128, 2 * NSB), np.float32)
        for s in range(NSB):
            g0 = t0 + s * SBT
            if g0 == 0:
                em[0:64, 2 * s] = NEG
            if g0 + SBT == S:
                em[64:128, 2 * s + 1] = NEG
        in_maps.append({
            "xs": slab, "mem": f("mem")[b], "w_in": w_in, "w_mem_kv": w_mkv, "w_out": w_out,
            "norm_g": f("norm_g").reshape(1, DM), "mem_norm_g": f("mem_norm_g").reshape(1, DM),
            "final_norm_g": f("final_norm_g").reshape(1, DM),
            "sg_ln_g": f("sg_ln_g").reshape(1, 512), "sg_ln_b": f("sg_ln_b").reshape(1, 512),
            "sg_w": sg_w, "sg_b": sg_b, "biasM": biasM, "emask": em, "ident": ident,
        })
    res = run_bass_kernel_spmd(nc, in_maps, core_ids=list(range(n_cores)))
    outp = np.empty((B, S, DM), np.float32)
    for c in range(n_cores):
        b, part = divmod(c, cps)
        t0 = part * per_core
        outp[b, t0:t0 + per_core] = res.results[c]["out"]
    return outp


def kernel(**inputs):
    return _run(inputs, 2)
```

```python
import contextlib
import math
import numpy as np
import ml_dtypes
import concourse.bass as bass
import concourse.mybir as mybir
from concourse.bass_utils import run_bass_kernel_spmd

F32 = mybir.dt.float32
BF16 = mybir.dt.bfloat16
AF = mybir.ActivationFunctionType
ALU = mybir.AluOpType

ENGS = ("pe", "act", "dve", "pool", "sp")
NDMA = 12
DM = 1024
SBT = 2048
HALO = 1024
EPS = 1e-6
NEG = -30000.0
CFGS = (1, 4, 16)


class Sched:
    def __init__(self, nc):
        self.nc = nc
        self.ops = {e: [] for e in ENGS}
        self.last_writer = {}
        self.readers = {}

    def _add(self, eng, fn, reads, writes, dma, extra=None, nobar=False):
        idx = len(self.ops[eng])
        me = (eng, idx)
        deps = set()
        raw = set()
        for k in reads:
            w = self.last_writer.get(k)
            if w is not None:
                deps.add(w)
                raw.add(w)
        for k in writes:
            w = self.last_writer.get(k)
            if w is not None:
                deps.add(w)
            for r in self.readers.get(k, ()):
                deps.add(r)
        deps.discard(me)
        if not dma:
            if eng == "pe":
                deps = {d for d in deps if d[0] != "pe" or self.ops["pe"][d[1]]["dma"]}
        if extra:
            deps |= set(extra)
        self.ops[eng].append(dict(fn=fn, deps=deps, dma=dma, signal=False, nobar=nobar))
        for k in reads:
            self.readers.setdefault(k, []).append(me)
        for k in writes:
            self.last_writer[k] = me
            self.readers[k] = []
        return me

    def op(self, eng, fn, reads=(), writes=()):
        return self._add(eng, fn, tuple(reads), tuple(writes), False)

    def dma(self, q, fn, reads=(), writes=(), nobar=False):
        return self._add(q, fn, tuple(reads), tuple(writes), True, nobar=nobar)

    def barrier(self):
        deps = set()
        for e in ENGS:
            ndma = 0
            seen_c = False
            for i in range(len(self.ops[e]) - 1, -1, -1):
                o = self.ops[e][i]
                if o["dma"]:
                    if ndma < NDMA:
                        if not o["nobar"]:
                            deps.add((e, i))
                        ndma += 1
                elif o["fn"] is not None and not seen_c:
                    deps.add((e, i))
                    seen_c = True
                if seen_c and ndma >= NDMA:
                    break
        for e in ENGS:
            self._add(e, None, (), (), False, extra={d for d in deps})

    def emit(self):
        nc = self.nc
        for e in ENGS:
            for o in self.ops[e]:
                for (de, di) in o["deps"]:
                    self.ops[de][di]["signal"] = True
        for e in ENGS:
            c = 0
            nd = 0
            for o in self.ops[e]:
                if o["dma"]:
                    o["dslot"] = nd % NDMA
                    o["dval"] = 16 * (nd // NDMA + 1)
                    nd += 1
                elif o["signal"]:
                    assert o["fn"] is not None
                    c += 1
                    o["cnt"] = c
        with contextlib.ExitStack() as st:
            csem = {e: st.enter_context(nc.semaphore("c_" + e)) for e in ENGS}
            dsem = {e: [st.enter_context(nc.semaphore("d_%s_%d" % (e, i))) for i in range(NDMA)]
                    for e in ("sp", "pool", "act")}
            block = st.enter_context(nc.Block())

            def run(e, eng):
                waited = {}
                for o in self.ops[e]:
                    need = {}
                    for (de, di) in o["deps"]:
                        d = self.ops[de][di]
                        if d["dma"]:
                            sem, val = dsem[de][d["dslot"]], d["dval"]
                        else:
                            sem, val = csem[de], d["cnt"]
                        k = id(sem)
                        if k not in need or need[k][1] < val:
                            need[k] = (sem, val)
                    if o["dma"]:
                        sem = dsem[e][o["dslot"]]
                        if o["dval"] > 16:
                            k = id(sem)
                            if k not in need or need[k][1] < o["dval"] - 16:
                                need[k] = (sem, o["dval"] - 16)
                    for k, (sem, val) in need.items():
                        if waited.get(k, 0) >= val:
                            continue
                        waited[k] = val
                        eng.wait_ge(sem, val)
                    if o["fn"] is None:
                        continue
                    ins = o["fn"](eng)
                    if o["dma"]:
                        ins.then_inc(dsem[e][o["dslot"]], 16)
                    elif o["signal"]:
                        ins.then_inc(csem[e], 1)

            @block.sync
            def _(eng):
                run("sp", eng)

            @block.tensor
            def _(eng):
                run("pe", eng)

            @block.scalar
            def _(eng):
                run("act", eng)

            @block.vector
            def _(eng):
                run("dve", eng)

            @block.gpsimd
            def _(eng):
                run("pool", eng)


class Arena:
    def __init__(self, t, n):
        self.t, self.n, self.off = t, n, 0

    def reset(self, off=0):
        self.off = off

    def alloc(self, cols):
        v = self.t[:, self.off:self.off + cols]
        self.off += cols
        assert self.off <= self.n, (self.off, self.n)
        return v


def build_nc(NSB):
    nc = bass.Bass("TRN2", target_bir_lowering=False)
    NTOK = NSB * SBT

    def din(name, shape):
        return nc.dram_tensor(name, list(shape), F32, kind="ExternalInput").ap()

    xs = din("xs", [NTOK + 2 * HALO, DM])
    mem = din("mem", [256, DM])
    w_in = din("w_in", [DM, 4608])
    w_mkv = din("w_mem_kv", [DM, 1024])
    w_out = din("w_out", [1536, DM])
    norm_g = din("norm_g", [1, DM])
    mem_norm_g = din("mem_norm_g", [1, DM])
    final_g = din("final_norm_g", [1, DM])
    sg_ln_g = din("sg_ln_g", [1, 512])
    sg_ln_b = din("sg_ln_b", [1, 512])
    sg_w = din("sg_w", [8, 128, 128])
    sg_b = din("sg_b", [8, 128])
    biasM = din("biasM", [128, 4, 1536])
    emask = din("emask", [128, 2 * NSB])
    ident_d = din("ident", [128, 128])
    out = nc.dram_tensor("out", [NTOK, DM], F32, kind="ExternalOutput").ap()

    wb_in = nc.dram_tensor("wb_in", [DM, 4608], BF16, kind="Internal").ap()
    wb_out = nc.dram_tensor("wb_out", [1536, DM], BF16, kind="Internal").ap()
    wb_bias = nc.dram_tensor("wb_bias", [128, 4, 1536], BF16, kind="Internal").ap()
    kv_scr = nc.dram_tensor("kv_scr", [4, 2, 128, 2048], BF16, kind="Internal").ap()
    w_in_v = wb_in.rearrange("(kc p) c -> p kc c", p=128)
    w_mkv_v = w_mkv.rearrange("(kc p) c -> p kc c", p=128)
    w_out_v = wb_out.rearrange("(kc p) c -> p kc c", p=128)

    with contextlib.ExitStack() as st:
        def sb(name, shape, dt):
            return st.enter_context(nc.sbuf_tensor("s_" + name, list(shape), dt))

        HCOLS = 55296
        FCOLS = 10240
        hT_own = sb("hT_own", [128, 8, SBT], BF16)
        attT = sb("attT", [128, 4, SBT], BF16)
        ident = sb("ident", [128, 128], BF16)
        ones = sb("ones", [128, 128], BF16)
        WsT = sb("WsT", [128, 8, 128], BF16)
        Bp = sb("Bp", [128, 4, 128], F32)
        lng = sb("lng", [128, 4], F32)
        kmT = sb("kmT", [128, 4, 256], BF16)
        vm = sb("vm", [128, 2, 512], BF16)
        emk = sb("emk", [128, 2 * NSB], F32)
        small = sb("small", [128, 64], F32)
        bigH_t = sb("bigH", [128, HCOLS], BF16)
        bigF_t = sb("bigF", [128, FCOLS], F32)
        pball = st.enter_context(nc.psum_tensor("pball", [128, 8, 512], F32))
        AH = Arena(bigH_t, HCOLS)
        AF_ = Arena(bigF_t, FCOLS)

        S = Sched(nc)
        PB = lambda k: ("pb", k)

        def bank(k):
            return pball[:, k, :]

        def bank_bf(k):
            return pball[:, k, :].bitcast(BF16)

        def ACT(out_, in_, func, rd, wr, **kw):
            S.op("act", lambda e: e.activation(out=out_, in_=in_, func=func, **kw), rd, wr)

        def MM(out_, lhsT, rhs, start, stop, rd, wr):
            S.op("pe", lambda e: e.matmul(out_, lhsT, rhs, start=start, stop=stop), rd, wr)

        def TR(out_, in_, rd, wr):
            S.op("pe", lambda e: e.transpose(out_, in_, ident[:]), list(rd) + ["ident"], wr)

        def TT(eng, out_, in0, in1, op, rd, wr):
            S.op(eng, lambda e: e.tensor_tensor(out=out_, in0=in0, in1=in1, op=op), rd, wr)

        def STT(eng, out_, in0, scalar, in1, op0, op1, rd, wr):
            S.op(eng, lambda e: e.scalar_tensor_tensor(out=out_, in0=in0, scalar=scalar, in1=in1,
                                                       op0=op0, op1=op1), rd, wr)

        def TS(eng, out_, in0, s1, s2, op0, op1, rd, wr):
            if op1 is None:
                S.op(eng, lambda e: e.tensor_scalar(out=out_, in0=in0, scalar1=s1, scalar2=None,
                                                    op0=op0), rd, wr)
            else:
                S.op(eng, lambda e: e.tensor_scalar(out=out_, in0=in0, scalar1=s1, scalar2=s2,
                                                    op0=op0, op1=op1), rd, wr)

        def CP(eng, out_, in_, rd, wr):
            S.op(eng, lambda e: e.tensor_copy(out=out_, in_=in_), rd, wr)

        def RCP(out_, in_, rd, wr):
            S.op("dve", lambda e: e.reciprocal(out=out_, in_=in_), rd, wr)

        def RCPF(out_, in_, rd, wr):
            S.op("dve", lambda e: e.reciprocal_approx_fast(out=out_, in_=in_), rd, wr)

        def MSET(eng, ap, val, wr):
            S.op(eng, lambda e: e.memset(ap, val), (), wr)

        def DMA(q, out_, in_, rd, wr, nobar=False):
            S.dma(q, lambda e: e.dma_start(out=out_, in_=in_), rd, wr, nobar=nobar)

        small_i = [0]

        def scol():
            i = small_i[0] % 64
            small_i[0] += 1
            return small[:, i:i + 1], ("small", i)

        def rms_rows(x_ap, x_key, sq_ap, sq_key):
            ss, kss = scol()
            ACT(sq_ap, x_ap, AF.Square, [x_key], [sq_key, kss], accum_out=ss)
            sd, ksd = scol()
            ACT(sd, ss, AF.Sqrt, [kss], [ksd], bias=EPS, scale=1.0 / DM)
            rs, krs = scol()
            RCP(rs, sd, [ksd], [krs])
            return rs, krs

        evac_rr = [0]

        def norm_transpose_pipe(n, src_fn, gbc, gkey, xa, hb, sq, dst_fn, tag="", tiles=None):
            pend = {}

            def stage1(t):
                i = t % len(xa)
                DMA("sp", xa[i], src_fn(t), [], [(tag + "xa", i)])
                ss, kss = scol()
                ACT(sq, xa[i], AF.Square, [(tag + "xa", i)], [tag + "sq", kss], accum_out=ss)
                sd, ksd = scol()
                ACT(sd, ss, AF.Sqrt, [kss], [ksd], bias=EPS, scale=1.0 / DM)
                pend[t] = (sd, ksd)

            def stage2(t):
                i = t % len(xa)
                ih = t % len(hb)
                sd, ksd = pend.pop(t)
                rs, krs = scol()
                RCP(rs, sd, [ksd], [krs])
                STT("dve", hb[ih], xa[i], rs, gbc, ALU.mult, ALU.mult,
                    [(tag + "xa", i), krs, gkey], [(tag + "hb", ih)])

            def stage3(t):
                i = t % len(hb)
                k = t % 3
                pv = bank_bf(k).rearrange("p (a b) -> p a b", b=128)
                for kc in range(8):
                    TR(pv[:, kc, :], hb[i][:, kc * 128:(kc + 1) * 128], [(tag + "hb", i)], [PB(k)])

            def stage4(t):
                k = t % 3
                pv = bank_bf(k).rearrange("p (a b) -> p a b", b=128)
                dst, dkey = dst_fn(t)
                eng = "act" if (evac_rr[0] % 2 == 0) else "dve"
                evac_rr[0] += 1
                if eng == "act":
                    ACT(dst, pv, AF.Copy, [PB(k)], [dkey])
                else:
                    CP("dve", dst, pv, [PB(k)], [dkey])

            tl = list(range(n)) if tiles is None else list(tiles)
            n = len(tl)
            for i in range(n + 3):
                if i < n:
                    stage1(tl[i])
                if 0 <= i - 1 < n:
                    stage2(tl[i - 1])
                if 0 <= i - 2 < n:
                    stage3(tl[i - 2])
                if 0 <= i - 3 < n:
                    stage4(tl[i - 3])

        DMA("pool", ident[:], ident_d, [], ["ident"])
        MSET("pool", ones[:], 1.0, ["ones"])
        DMA("sp", emk[:], emask, [], ["emk"])
        WBIN = [("wbin", kc) for kc in range(8)]
        WBOUT = [("wbout", kc) for kc in range(12)]
        WBIN2 = [("wbin2", kc) for kc in range(8)]
        w_in_f = w_in.rearrange("(kc p) c -> p kc c", p=128)

        def convert_hp(hp):
            c0 = hp * 128
            for grp in range(4):
                c = grp * 512 + c0
                DMA("pool", w_in_v[:, :, c:c + 128], w_in_f[:, :, c:c + 128], [], [("wbin_hp", hp, grp)],
                    nobar=True)
            DMA("pool", wb_bias[:, hp, :], biasM[:, hp, :], [], [("wbb", hp)], nobar=True)

        convert_hp(0)

        def convert_rest():
            for kc in range(8):
                DMA("pool", wb_in[kc * 128:(kc + 1) * 128, 2048:4608],
                    w_in[kc * 128:(kc + 1) * 128, 2048:4608], [], [("wbin2", kc)], nobar=True)
            for kc in range(12):
                DMA("pool", wb_out[kc * 128:(kc + 1) * 128, :], w_out[kc * 128:(kc + 1) * 128, :], [],
                    [("wbout", kc)], nobar=True)

        sst = {}

        def emit_setup_loads():
            sst["xa"] = [AF_.alloc(1024), AF_.alloc(1024)]
            sst["gbc"] = AF_.alloc(1024)
            sst["hb"] = [AH.alloc(1024), AH.alloc(1024)]
            sst["sq"] = AH.alloc(1024)
            sst["wm"] = AH.alloc(8192).rearrange("p (a b) -> p a b", b=1024)
            sst["memnT"] = AH.alloc(2048).rearrange("p (a b) -> p a b", b=256)
            sst["lnb_bc"] = AH.alloc(512)
            sst["sgw"] = AH.alloc(1024).rearrange("p (a b) -> p a b", b=128)
            sst["sgbb"] = AF_.alloc(512).rearrange("p (a b) -> p a b", b=128)
            DMA("pool", sst["gbc"], mem_norm_g.partition_broadcast(128)[:, 0, :], [], ["gbc_s"])
            sst["wm_loads"] = lambda: [DMA("pool", sst["wm"][:, kc, :], w_mkv_v[:, kc, :], [("hT", 16 + 2 * kc)],
                                           [("wm", kc)]) for kc in range(8)]
            DMA("pool", sst["sgw"], sg_w.rearrange("g p q -> p g q"), [], ["sgw"])
            DMA("pool", sst["lnb_bc"], sg_ln_b.partition_broadcast(128)[:, 0, :], [], ["lnb"])
            for j in range(4):
                DMA("pool", lng[:, j:j + 1], sg_ln_g[0:1, j * 128:(j + 1) * 128].rearrange("a p -> p a"),
                    [], ["lng"])
                for hh in range(2):
                    g = 2 * j + hh
                    DMA("pool", sst["sgbb"][hh * 64:(hh + 1) * 64, j, :],
                        sg_b[g:g + 1, :].partition_broadcast(64)[:, 0, :], [], ["sgbb"])

        def emit_setup():
            xa, gbc, hb, sq = sst["xa"], sst["gbc"], sst["hb"], sst["sq"]
            wm, memnT, lnb_bc, sgw, sgbb = (sst["wm"], sst["memnT"], sst["lnb_bc"], sst["sgw"],
                                            sst["sgbb"])
            norm_transpose_pipe(2, lambda t: mem[t * 128:(t + 1) * 128, :], gbc, "gbc_s", xa, hb, sq,
                                lambda t: (memnT[:, :, t * 128:(t + 1) * 128], ("memnT", t)), tag="s_")
            wmk = [("wm", kc) for kc in range(8)]
            mnk = [("memnT", 0), ("memnT", 1)]
            for h in range(4):
                k = 2 + h % 2
                for kc in range(8):
                    MM(bank(k)[:, 0:256], wm[:, kc, h * 128:(h + 1) * 128], memnT[:, kc, :],
                       kc == 0, kc == 7, wmk + mnk, [PB(k)])
                ACT(kmT[:, h, :], bank(k)[:, 0:256], AF.Copy, [PB(k)], ["kmT"])
            for mt in range(2):
                k = 4 + mt
                for kc in range(8):
                    MM(bank(k), memnT[:, kc, mt * 128:(mt + 1) * 128], wm[:, kc, 512:1024],
                       kc == 0, kc == 7, wmk + mnk, [PB(k)])
                ACT(vm[:, mt, :], bank(k), AF.Copy, [PB(k)], ["vm"])
            pv6 = bank_bf(6).rearrange("p (a b) -> p a b", b=128)
            for g in range(8):
                TR(pv6[:, g, :], sgw[:, g, :], ["sgw"], [PB(6)])
            CP("dve", WsT[:], pv6, [PB(6)], ["WsT"])
            for j in range(4):
                for hh in range(2):
                    g = 2 * j + hh
                    MM(bank(7)[hh * 64:(hh + 1) * 64, j * 128:(j + 1) * 128],
                       lnb_bc[:, g * 64:(g + 1) * 64], WsT[:, g, :], True, True,
                       ["lnb", "WsT"], [PB(7)])
            TT("dve", Bp[:], bank(7).rearrange("p (a b) -> p a b", b=128), sgbb, ALU.add,
               [PB(7), "sgbb"], ["Bp"])


        AH.reset()
        hT_halo = AH.alloc(16384).rearrange("p (a b) -> p a b", b=2048)
        wsets = []
        for i in range(2):
            ws = dict(
                wq=AH.alloc(1024).rearrange("p (a b) -> p a b", b=128),
                wk=AH.alloc(1024).rearrange("p (a b) -> p a b", b=128),
                wv=AH.alloc(1024).rearrange("p (a b) -> p a b", b=128),
                wg=AH.alloc(1024).rearrange("p (a b) -> p a b", b=128),
                mbf=AH.alloc(1536), i=i)
            ws["mb"] = ws["mbf"].rearrange("p (c h w) -> p c h w", c=3, h=2)
            wsets.append(ws)
        AB_H0 = AH.off

        def load_wset(ws, hp):
            c0 = hp * 128
            i = ws["i"]
            DMA("sp", ws["wk"], w_in_v[:, :, 512 + c0:512 + c0 + 128], [("wbin_hp", hp, 1)], [("wk", i)])
            DMA("sp", ws["wv"], w_in_v[:, :, 1024 + c0:1024 + c0 + 128], [("wbin_hp", hp, 2)], [("wv", i)])
            DMA("sp", ws["wq"], w_in_v[:, :, c0:c0 + 128], [("wbin_hp", hp, 0)], [("wq", i)])
            DMA("sp", ws["wg"], w_in_v[:, :, 1536 + c0:1536 + c0 + 128], [("wbin_hp", hp, 3)], [("wg", i)])
            DMA("sp", ws["mbf"], wb_bias[:, hp, :], [("wbb", hp)], [("mb", i)])

        def hT_blk(b):
            if b in (0, 1):
                v = hT_halo[:, :, b * 512:(b + 1) * 512]
            elif b in (6, 7):
                v = hT_halo[:, :, 1024 + (b - 6) * 512:1024 + (b - 5) * 512]
            else:
                v = hT_own[:, :, (b - 2) * 512:(b - 1) * 512]
            return v, [("hT", 4 * b + i) for i in range(4)]

        ALLT = []
        for ci, d in enumerate(CFGS):
            A = HALO // d
            nq = SBT // (128 * d)
            for r in range(d):
                for j in range(nq + 1):
                    ALLT.append((ci, d, r, j, nq, A))
        NT = len(ALLT)

        for s in range(NSB):
            AH.reset(AB_H0); AF_.reset()
            hb = [AH.alloc(1024) for _ in range(4)]
            sq = AH.alloc(1024)
            xa = [AF_.alloc(1024) for _ in range(4)]
            gbc = AF_.alloc(1024)
            DMA("sp", gbc, norm_g.partition_broadcast(128)[:, 0, :], [], ["gbc"])
            if s == 0:
                emit_setup_loads()

            if s == 0:
                pass

            def dstA(t):
                v, _ = hT_blk(t // 4)
                o = (t % 4) * 128
                return v[:, :, o:o + 128], ("hT", t)
            tiles_a = None if s == 0 else list(range(8, 32))
            norm_transpose_pipe(32, lambda t: xs[s * SBT + t * 128:s * SBT + (t + 1) * 128, :],
                                gbc, "gbc", xa, hb, sq, dstA, tiles=tiles_a)
            if s == 0:
                sst["wm_loads"]()
                emit_setup()
            load_wset(wsets[0], 0)
            S.barrier()

            AH.reset(AB_H0); AF_.reset()
            Q2 = AF_.alloc(2048).bitcast(BF16).rearrange("p (h t) -> p h t", h=2)
            QA = Q2[:, 0, :]
            QB = Q2[:, 1, :]
            KT = AH.alloc(4096)
            VT = AH.alloc(4096)
            Vt_flat = AH.alloc(NT * 256)
            Vt = Vt_flat.rearrange("p (a b) -> p a b", b=256)
            PT = [AH.alloc(512).rearrange("p (a b) -> p a b", b=256) for _ in range(3)]
            GT = AF_.alloc(2048)
            acc = AF_.alloc(4096).rearrange("p (a b) -> p a b", b=2048)
            rl = AF_.alloc(2048)
            MSET("pool", Vt[:, :, 64:192], 1.0, ["Vt1"])
            MSET("pool", QA[64:128, :], 0.0, ["QTz0"])
            MSET("pool", QB[0:64, :], 0.0, ["QTz1"])
            KTK = [("KT", b_) for b_ in range(8)]
            VTK = [("VT", b_) for b_ in range(8)]
            QTK = [("QT", b_, h_) for b_ in range(4) for h_ in range(2)] + ["QTz0", "QTz1"]
            GTK = [("GT", b_) for b_ in range(4)]
            if s == 0:
                for hp_ in (1, 2, 3):
                    convert_hp(hp_)
                convert_rest()
            ACCK = [("acc", i) for i in range(16)]

            for hp in range(4):
                ws = wsets[hp % 2]
                wi = ws["i"]
                if hp + 1 < 4:
                    load_wset(wsets[(hp + 1) % 2], hp + 1)
                pk = 0
                if s >= 1:
                    DMA("sp", KT[:, 0:2048], kv_scr[hp, 0], [("kvs", hp, 0)], [("KT", b_) for b_ in range(4)])
                    DMA("sp", VT[:, 0:2048], kv_scr[hp, 1], [("kvs", hp, 1)], [("VT", b_) for b_ in range(4)])
                for b in list(range(8)) + [12, 13, 14, 15]:
                    if b < 8:
                        hv, hk = hT_blk(b)
                        jobs = [(("wk", wi), ws["wk"], KT[:, b * 512:(b + 1) * 512], ("KT", b), AF.Copy, 1.0),
                                (("wv", wi), ws["wv"], VT[:, b * 512:(b + 1) * 512], ("VT", b), AF.Copy, 1.0)]
                        if s >= 1 and b < 4:
                            jobs = []
                        if 2 <= b <= 5:
                            o = (b - 2) * 512
                            jobs += [(("wq", wi), ws["wq"], None, ("QT", b - 2), AF.Copy, 0.125)]
                    else:
                        hv, hk = hT_blk(b - 10)
                        o = (b - 12) * 512
                        jobs = [(("wg", wi), ws["wg"], GT[:, o:o + 512], ("GT", b - 12), AF.Silu, 1.0)]
                    for (wkey, wt, dst, dkey, fn, sc) in jobs:
                        k = pk % 8
                        pk += 1
                        for kc in range(8):
                            MM(bank(k), wt[:, kc, :], hv[:, kc, :], kc == 0, kc == 7,
                               [wkey] + hk, [PB(k)])
                        if dst is None:
                            ACT(QA[0:64, o:o + 512], bank(k)[0:64, :], fn, [PB(k)], [dkey + (0,)], scale=sc)
                            ACT(QB[64:128, o:o + 512], bank(k)[64:128, :], fn, [PB(k)], [dkey + (1,)], scale=sc)
                        else:
                            ACT(dst, bank(k), fn, [PB(k)], [dkey], scale=sc)
                if s + 1 < NSB:
                    DMA("sp", kv_scr[hp, 0], KT[:, 2048:4096], [("KT", b_) for b_ in range(4, 8)],
                        [("kvs", hp, 0)])
                    DMA("sp", kv_scr[hp, 1], VT[:, 2048:4096], [("VT", b_) for b_ in range(4, 8)],
                        [("kvs", hp, 1)])
                for g0 in range(0, NT, 8):
                    grp = ALLT[g0:g0 + 8]
                    k = (g0 // 8) % 8
                    pvv = bank_bf(k).rearrange("p (a b) -> p a b", b=128)
                    for i, (ci, d, r, j, nq, A) in enumerate(grp):
                        u0 = r + d * (A + 128 * j - 64)
                        TR(pvv[:, i, :], VT[:, u0:u0 + d * 127 + 1:d], VTK, [PB(k)])
                    n = len(grp)
                    CP("dve", Vt[:, g0:g0 + n, 0:64], pvv[:, 0:n, 0:64], [PB(k)], [("Vt", g0, 0)])
                    CP("dve", Vt[:, g0:g0 + n, 192:256], pvv[:, 0:n, 64:128], [PB(k)], [("Vt", g0, 1)])

                def geom(ti):
                    ci, d, r, j, nq, A = ALLT[ti]
                    halves = []
                    if j >= 1:
                        halves.append((0, j - 1))
                    if j <= nq - 1:
                        halves.append((1, j))
                    return ci, d, r, j, nq, A, halves

                def stage1(ti):
                    ci, d, r, j, nq, A, halves = geom(ti)
                    u0 = r + d * (A + 128 * j - 64)
                    W = 128 * len(halves)
                    mcol0 = 128 * halves[0][0]
                    q0 = r + d * 128 * halves[0][1]
                    sk = ti % 4
                    so = pball[:, sk, 0:2 * W]
                    MM(so, ident[:], ws["mb"][:, ci, :, mcol0:mcol0 + W],
                       True, False, ["ident", ("mb", wi)], [PB(sk)])
                    MM(so, KT[:, u0:u0 + d * 127 + 1:d], Q2[:, :, q0:q0 + d * (W - 1) + 1:d],
                       False, True, KTK + QTK, [PB(sk)])

                def stage2(ti):
                    ci, d, r, j, nq, A, halves = geom(ti)
                    W = 128 * len(halves)
                    sk = ti % 4
                    if j == 0:
                        bias = emk[:, 2 * s:2 * s + 1]
                    elif j == nq:
                        bias = emk[:, 2 * s + 1:2 * s + 2]
                    else:
                        bias = 0.0
                    ACT(PT[ti % 3][:, :, 0:W],
                        pball[:, sk, 0:2 * W].rearrange("p (h w) -> p h w", h=2), AF.Exp,
                        [PB(sk), "emk"], [("PT", ti % 3)], bias=bias)

                st3 = dict(ctr=0, prev=None, cur=None)

                def stage3(ti):
                    ci, d, r, j, nq, A, halves = geom(ti)
                    pt = PT[ti % 3]
                    for hi_, (hf, cj) in enumerate(halves):
                        if hf == 1:
                            slot = st3["ctr"] % 2
                            st3["ctr"] += 1
                            st3["cur"] = slot
                            opening = True
                        else:
                            slot = st3["prev"]
                            opening = False
                        ko = 4 + 2 * slot
                        cs = hi_ * 128
                        vk = ["Vt1", ("Vt", (ti // 8) * 8, 0), ("Vt", (ti // 8) * 8, 1)]
                        MM(pball[:, ko, 0:128], Vt[:, ti, 0:128], pt[:, 0, cs:cs + 128],
                           opening, not opening, vk + [("PT", ti % 3)], [PB(ko)])
                        MM(pball[:, ko + 1, 0:128], Vt[:, ti, 128:256], pt[:, 1, cs:cs + 128],
                           opening, not opening, vk + [("PT", ti % 3)], [PB(ko + 1)])
                        if not opening:
                            t0 = r + d * 128 * cj
                            dst = acc[:, :, t0:t0 + d * 127 + 1:d]
                            src = pball[:, ko:ko + 2, 0:128]
                            blk0 = (d * 128 * cj) // 128
                            aks = ACCK[blk0:blk0 + d]
                            if ci == 0:
                                CP("dve", dst, src, [PB(ko), PB(ko + 1)], aks)
                            else:
                                TT("dve", dst, src, dst, ALU.add, [PB(ko), PB(ko + 1)] + aks, aks)
                    if j <= nq - 1:
                        st3["prev"] = st3["cur"]

                for i in range(NT + 2):
                    if i < NT:
                        stage1(i)
                    if 0 <= i - 1 < NT:
                        stage2(i - 1)
                    if 0 <= i - 2 < NT:
                        stage3(i - 2)
                if hp == 3:
                    S.barrier()
                RCP(rl[0:64, :], acc[64:128, 0, :], ACCK + ["FT"], ["rl"])
                RCP(rl[64:128, :], acc[0:64, 1, :], ACCK + ["FT"], ["rl"])
                TT("pool", rl, rl, GT, ALU.mult, ["rl", "FT"] + GTK, ["rl"])
                TT("pool", attT[0:64, hp, :], acc[0:64, 0, :], rl[0:64, :], ALU.mult,
                   ACCK + ["rl", "FT"], [("attT", hp)])
                TT("pool", attT[64:128, hp, :], acc[64:128, 1, :], rl[64:128, :], ALU.mult,
                   ACCK + ["rl", "FT"], [("attT", hp)])

            AH.reset(); AF_.reset()
            wC = [AH.alloc(4096).rearrange("p (a b) -> p a b", b=512) for _ in range(5)]
            wo = AH.alloc(12288).rearrange("p (a b) -> p a b", b=1024)
            uT = AH.alloc(2048).rearrange("p (a b) -> p a b", b=512)
            sgT = AH.alloc(2048).rearrange("p (a b) -> p a b", b=512)
            sgmT = AH.alloc(2048).rearrange("p (a b) -> p a b", b=512)
            vvn = [AH.alloc(512) for _ in range(4)]
            ysgT = AH.alloc(2048).rearrange("p (a b) -> p a b", b=512)
            qmT = AH.alloc(2048).rearrange("p (a b) -> p a b", b=512)
            PTm = [AH.alloc(512) for _ in range(4)]
            ymemT = AH.alloc(2048).rearrange("p (a b) -> p a b", b=512)
            sqj = AH.alloc(1024)
            vv = [AF_.alloc(512) for _ in range(4)]
            rlm = [AF_.alloc(512), AF_.alloc(512)]
            xr = [AF_.alloc(1024) for _ in range(4)]
            fgbc = AF_.alloc(1024)
            ms = [AF_.alloc(512).rearrange("p (a b) -> p a b", b=128) for _ in range(2)]
            usg = AH.alloc(2048).rearrange("p (a b) -> p a b", b=512)

            cgrp = [2048, 2560, 3072, 3584, 4096]
            for gi in (1, 0, 2, 4, 3):
                cc = cgrp[gi]
                for kc in range(8):
                    DMA("sp", wC[gi][:, kc, :], w_in_v[:, kc, cc:cc + 512], WBIN2, [("wC", gi, kc)])
            for kc in range(12):
                DMA("sp", wo[:, kc, :], w_out_v[:, kc, :], WBOUT, [("wo", kc)])
            DMA("sp", fgbc, final_g.partition_broadcast(128)[:, 0, :], [], ["fgbc", "FT"])
            w_u, w_v2, w_gs, w_qm, w_gm = wC
            free_banks = list(range(8))

            def balloc():
                assert free_banks, "PSUM banks exhausted"
                return free_banks.pop(0)

            def bfree(k):
                free_banks.append(k)

            def balloc_pair():
                best = None
                for a_ in free_banks:
                    if a_ < 7 and (a_ + 1) in free_banks:
                        sc = max(free_banks.index(a_), free_banks.index(a_ + 1))
                        if best is None or sc < best[0]:
                            best = (sc, a_)
                assert best is not None, "no adjacent PSUM bank pair free"
                a_ = best[1]
                free_banks.remove(a_)
                free_banks.remove(a_ + 1)
                return a_

            bst = [dict() for _ in range(4)]

            def groups_P(bb):
                hv = hT_own[:, :, bb * 512:(bb + 1) * 512]
                hk = [("hT", 8 + 4 * bb + i) for i in range(4)]
                stt_ = [dict() for _ in range(4)]
                bst[bb]["stt"] = stt_
                gl = []

                def vv_group(tt):
                    d_ = stt_[tt]
                    k = balloc()
                    for kc in range(8):
                        MM(bank(k), hv[:, kc, tt * 128:(tt + 1) * 128], w_v2[:, kc, :],
                           kc == 0, kc == 7, [("wC", 1, kc)] + hk, [PB(k)])
                    d_["sm"], d_["ksm"] = scol()
                    ACT(vv[tt], bank(k), AF.Gelu_apprx_tanh, [PB(k)],
                        [("vv", tt), d_["ksm"]], accum_out=d_["sm"])
                    bfree(k)

                def fm_group(wt, gi, dst, dkey, fn, ft):
                    k = balloc()
                    for kc in range(8):
                        MM(bank(k), wt[:, kc, ft * 128:(ft + 1) * 128], hv[:, kc, :],
                           kc == 0, kc == 7, [("wC", gi, kc)] + hk, [PB(k)])
                    ACT(dst[:, ft, :], bank(k), fn, [PB(k)], [(dkey, ft)])
                    bfree(k)

                for tt in range(4):
                    gl.append(lambda tt=tt: vv_group(tt))
                for (wt, gi, dst, dkey, fn) in ((w_u, 0, uT, "uT", AF.Gelu_apprx_tanh),
                                                (w_gs, 2, sgT, "sgT", AF.Silu),
                                                (w_qm, 3, qmT, "qmT", AF.Copy),
                                                (w_gm, 4, sgmT, "sgmT", AF.Silu)):
                    for ft in range(4):
                        gl.append(lambda wt=wt, gi=gi, dst=dst, dkey=dkey, fn=fn, ft=ft:
                                  fm_group(wt, gi, dst, dkey, fn, ft))
                return gl

            filler = []

            def fill(n):
                for _ in range(n):
                    if filler:
                        filler.pop(0)()

            def stage_M(bb):
                ft_ = ["FT"] if bb == 0 else []
                stt_ = bst[bb]["stt"]
                for tt in range(4):
                    d_ = stt_[tt]
                    d_["sq2"], d_["ksq"] = scol()
                    ACT(sqj[:, 0:512], vv[tt], AF.Square, [("vv", tt)], ["sqj", d_["ksq"]],
                        accum_out=d_["sq2"])
                fen, kfen = scol()
                ACT(fen, stt_[3]["sq2"], AF.Copy, [stt_[i_]["ksq"] for i_ in range(4)], [kfen])
                for tt in range(4):
                    d_ = stt_[tt]
                    d_["mean"], d_["kmean"] = scol()
                    TS("dve", d_["mean"], d_["sm"], 1.0 / 512, None, ALU.mult, None,
                       [d_["ksm"], kfen], [d_["kmean"]])
                for tt in range(4):
                    d_ = stt_[tt]
                    d_["m2"], d_["km2"] = scol()
                    TT("dve", d_["m2"], d_["mean"], d_["mean"], ALU.mult, [d_["kmean"]], [d_["km2"]])
                for tt in range(4):
                    d_ = stt_[tt]
                    d_["var"], d_["kvar"] = scol()
                    STT("dve", d_["var"], d_["sq2"], 1.0 / 512, d_["m2"], ALU.mult, ALU.subtract,
                        [d_["ksq"], d_["km2"], kfen], [d_["kvar"]])
                for tt in range(4):
                    d_ = stt_[tt]
                    d_["sd"], d_["ksd"] = scol()
                    ACT(d_["sd"], d_["var"], AF.Sqrt, [d_["kvar"]], [d_["ksd"]], bias=EPS, scale=1.0)
                for tt in range(4):
                    d_ = stt_[tt]
                    d_["rs"], d_["krs"] = scol()
                    RCP(d_["rs"], d_["sd"], [d_["ksd"]], [d_["krs"]])
                for tt in range(4):
                    d_ = stt_[tt]
                    d_["nb"], d_["knb"] = scol()
                    STT("dve", d_["nb"], d_["mean"], -1.0, d_["rs"], ALU.mult, ALU.mult,
                        [d_["kmean"], d_["krs"]], [d_["knb"]])
                for tt in range(4):
                    d_ = stt_[tt]
                    ACT(vvn[tt], vv[tt], AF.Identity, [("vv", tt), d_["krs"], d_["knb"]],
                        [("vvn", tt)], bias=d_["nb"], scale=d_["rs"])
                TT("pool", usg, uT, sgT, ALU.mult,
                   [("uT", f_) for f_ in range(4)] + [("sgT", f_) for f_ in range(4)], ["usg"])
                fill(8)
                mixb = []
                for tt in range(4):
                    km_ = balloc()
                    mixb.append(km_)
                    for j in range(4):
                        for hh in range(2):
                            g = 2 * j + hh
                            MM(bank(km_)[hh * 64:(hh + 1) * 64, j * 128:(j + 1) * 128],
                               vvn[tt][:, g * 64:(g + 1) * 64], WsT[:, g, :], True, True,
                               [("vvn", tt), "WsT"], [PB(km_)])
                fill(4)
                for tt in range(4):
                    i = tt % 2
                    bmv = bank(mixb[tt]).rearrange("p (a b) -> p a b", b=128)
                    for j in range(4):
                        STT("dve", ms[i][:, j, :], bmv[:, j, :], lng[:, j:j + 1], Bp[:, j, :],
                            ALU.mult, ALU.add, [PB(mixb[tt]), "lng", "Bp"], [("ms", i, j)] + ft_)
                    bfree(mixb[tt])
                    TT("pool", ysgT[:, :, tt * 128:(tt + 1) * 128], ms[i],
                       usg[:, :, tt * 128:(tt + 1) * 128], ALU.mult,
                       [("ms", i, j_) for j_ in range(4)] + ["usg"], [("ysgT", tt)])
                sbk = {}

                def scores(h):
                    for mt in range(2):
                        k = balloc()
                        sbk[(h, mt)] = k
                        MM(bank(k), kmT[:, h, mt * 128:(mt + 1) * 128], qmT[:, h, :], True, True,
                           ["kmT", ("qmT", h)], [PB(k)])

                scores(0)
                scores(1)
                for h in range(4):
                    if h + 2 < 4:
                        scores(h + 2)
                    for mt in range(2):
                        pi = (2 * h + mt) % 4
                        ACT(PTm[pi], bank(sbk[(h, mt)]), AF.Exp, [PB(sbk[(h, mt)])], [("PTm", pi)],
                            scale=1.0 / math.sqrt(128.0))
                        bfree(sbk[(h, mt)])
                    kO = balloc()
                    kL = balloc()
                    for mt in range(2):
                        pi = (2 * h + mt) % 4
                        MM(bank(kL), ones[:], PTm[pi], mt == 0, mt == 1,
                           ["ones", ("PTm", pi)], [PB(kL)])
                    for mt in range(2):
                        pi = (2 * h + mt) % 4
                        MM(bank(kO), vm[:, mt, h * 128:(h + 1) * 128], PTm[pi], mt == 0, mt == 1,
                           ["vm", ("PTm", pi)], [PB(kO)])
                    i = h % 2
                    RCP(rlm[i], bank(kL), [PB(kL)], [("rlm", i)] + ft_)
                    bfree(kL)
                    TT("pool", rlm[i], rlm[i], sgmT[:, h, :], ALU.mult, [("rlm", i), ("sgmT", h)],
                       [("rlm", i)])
                    TT("dve", ymemT[:, h, :], bank(kO), rlm[i], ALU.mult, [PB(kO), ("rlm", i)],
                       [("ymemT", h)])
                    bfree(kO)
                    if h == 1:
                        fill(4)
                    if h == 3:
                        fill(4)

            def stage_O(bb):
                ft_ = ["FT"] if bb == 0 else []
                for tt in range(4):
                    own0 = bb * 512 + tt * 128
                    r0 = s * SBT + HALO + own0
                    DMA("sp", xr[tt], xs[r0:r0 + 128, :], [], [("xr", tt)] + ft_)
                for tt in range(4):
                    own0 = bb * 512 + tt * 128
                    k0_ = balloc_pair()
                    ky = [k0_, k0_ + 1]
                    for half in range(2):
                        for kc in range(12):
                            if kc < 4:
                                lt, lk = attT[:, kc, own0:own0 + 128], ("attT", kc)
                            elif kc < 8:
                                lt, lk = ysgT[:, kc - 4, tt * 128:(tt + 1) * 128], ("ysgT", tt)
                            else:
                                lt, lk = ymemT[:, kc - 8, tt * 128:(tt + 1) * 128], ("ymemT", kc - 8)
                            MM(bank(ky[half]), lt, wo[:, kc, half * 512:(half + 1) * 512],
                               kc == 0, kc == 11, [lk, ("wo", kc)], [PB(ky[half])])
                    xv = xr[tt].rearrange("p (a b) -> p a b", b=512)
                    TT("dve", xv, pball[:, ky[0]:ky[0] + 2, :], xv, ALU.add,
                       [PB(ky[0]), PB(ky[1]), ("xr", tt)], [("xr", tt)])
                    bfree(ky[0])
                    bfree(ky[1])
                rss = []
                for tt in range(4):
                    ss, kss = scol()
                    ACT(sqj, xr[tt], AF.Square, [("xr", tt)], ["sqj", kss], accum_out=ss)
                    rss.append([ss, kss])
                for tt in range(4):
                    sd, ksd = scol()
                    ACT(sd, rss[tt][0], AF.Sqrt, [rss[tt][1]], [ksd], bias=EPS, scale=1.0 / DM)
                    rss[tt] += [sd, ksd]
                for tt in range(4):
                    rs, krs = scol()
                    RCP(rs, rss[tt][2], [rss[tt][3]], [krs])
                    rss[tt] += [rs, krs]
                for tt in range(4):
                    rs, krs = rss[tt][4], rss[tt][5]
                    STT("dve", xr[tt], xr[tt], rs, fgbc, ALU.mult, ALU.mult,
                        [("xr", tt), krs, "fgbc"], [("xr", tt)])
                    o0 = s * SBT + bb * 512 + tt * 128
                    DMA("sp", out[o0:o0 + 128, :], xr[tt], [("xr", tt)], [("out", s, bb, tt)])

            filler.extend(groups_P(0))
            fill(20)
            for bb in range(4):
                if bb + 1 < 4:
                    filler.extend(groups_P(bb + 1))
                stage_M(bb)
                fill(20)
                stage_O(bb)
            S.barrier()
        S.emit()
    return nc


def _t5_bucket(rel):
    nb_, md = 32, 1024
    half = nb_ // 2
    max_exact = half // 2
    ret = (rel > 0).astype(np.int32) * half
    n = np.abs(rel)
    large = max_exact + (np.log(np.maximum(n, 1).astype(np.float32) / max_exact)
                         / math.log(md / max_exact) * (half - max_exact)).astype(np.int32)
    large = np.minimum(large, half - 1)
    return (ret + np.where(n < max_exact, n, large)).astype(np.int32)


def _bias_layout(rel_bias):
    rel_bias = np.asarray(rel_bias, np.float32)
    kk = np.arange(128)[:, None]
    qq = np.arange(256)[None, :]
    rel = kk - qq + 64
    band = np.abs(rel) <= 64
    outp = np.empty((128, 4, 3, 2, 256), np.float32)
    for ci, d in enumerate(CFGS):
        bucket = _t5_bucket(rel * d)
        g = rel_bias[bucket]
        for h in range(8):
            m = g[:, :, h].copy()
            m[~band] = NEG
            outp[:, h // 2, ci, h % 2, :] = m
    return np.ascontiguousarray(outp.reshape(128, 4, 1536))


_NC_CACHE = {}


def _run(inputs, NSB):
    x = np.asarray(inputs["x"], np.float32)
    B, S, _ = x.shape
    per_core = NSB * SBT
    cps = S // per_core
    n_cores = B * cps
    if NSB not in _NC_CACHE:
        _NC_CACHE[NSB] = build_nc(NSB)
    nc = _NC_CACHE[NSB]
    biasM = _bias_layout(inputs["rel_bias"])
    ident = np.eye(128, dtype=np.float32)
    f = lambda k: np.ascontiguousarray(np.asarray(inputs[k], np.float32))
    w_in = f("w_in")[0]
    w_mkv = f("w_mem_kv")[0]
    w_out = f("w_out")[0]
    sg_w = f("sg_w")[0]
    sg_b = f("sg_b")[0]
    in_maps = []
    for c in range(n_cores):
        b, part = divmod(c, cps)
        t0 = part * per_core
        slab = np.zeros((per_core + 2 * HALO, DM), np.float32)
        lo, hi = t0 - HALO, t0 + per_core + HALO
        slo, shi = max(lo, 0), min(hi, S)
        slab[slo - lo:shi - lo] = x[b, slo:shi]
        em = np.zeros((128, 2 * NSB), np.float32)
        for s in range(NSB):
            g0 = t0 + s * SBT
            if g0 == 0:
                em[0:64, 2 * s] = NEG
            if g0 + SBT == S:
                em[64:128, 2 * s + 1] = NEG
        in_maps.append({
            "xs": slab, "mem": f("mem")[b], "w_in": w_in, "w_mem_kv": w_mkv, "w_out": w_out,
            "norm_g": f("norm_g").reshape(1, DM), "mem_norm_g": f("mem_norm_g").reshape(1, DM),
            "final_norm_g": f("final_norm_g").reshape(1, DM),
            "sg_ln_g": f("sg_ln_g").reshape(1, 512), "sg_ln_b": f("sg_ln_b").reshape(1, 512),
            "sg_w": sg_w, "sg_b": sg_b, "biasM": biasM, "emask": em, "ident": ident,
        })
    res = run_bass_kernel_spmd(nc, in_maps, core_ids=list(range(n_cores)))
    outp = np.empty((B, S, DM), np.float32)
    for c in range(n_cores):
        b, part = divmod(c, cps)
        t0 = part * per_core
        outp[b, t0:t0 + per_core] = res.results[c]["out"]
    return outp


def kernel(**inputs):
    return _run(inputs, 2)
```

```python
import contextlib
import math
import numpy as np
import ml_dtypes
import concourse.bass as bass
import concourse.mybir as mybir
from concourse.bass_utils import run_bass_kernel_spmd

F32 = mybir.dt.float32
BF16 = mybir.dt.bfloat16
AF = mybir.ActivationFunctionType
ALU = mybir.AluOpType

ENGS = ("pe", "act", "dve", "pool", "sp")
NDMA = 12
DM = 1024
SBT = 2048
HALO = 1024
EPS = 1e-6
NEG = -30000.0
CFGS = (1, 4, 16)


class Sched:
    def __init__(self, nc):
        self.nc = nc
        self.ops = {e: [] for e in ENGS}
        self.last_writer = {}
        self.readers = {}

    def _add(self, eng, fn, reads, writes, dma, extra=None, nobar=False):
        idx = len(self.ops[eng])
        me = (eng, idx)
        deps = set()
        raw = set()
        for k in reads:
            w = self.last_writer.get(k)
            if w is not None:
                deps.add(w)
                raw.add(w)
        for k in writes:
            w = self.last_writer.get(k)
            if w is not None:
                deps.add(w)
            for r in self.readers.get(k, ()):
                deps.add(r)
        deps.discard(me)
        if not dma:
            if eng == "pe":
                deps = {d for d in deps if d[0] != "pe" or self.ops["pe"][d[1]]["dma"]}
        if extra:
            deps |= set(extra)
        self.ops[eng].append(dict(fn=fn, deps=deps, dma=dma, signal=False, nobar=nobar))
        for k in reads:
            self.readers.setdefault(k, []).append(me)
        for k in writes:
            self.last_writer[k] = me
            self.readers[k] = []
        return me

    def op(self, eng, fn, reads=(), writes=()):
        return self._add(eng, fn, tuple(reads), tuple(writes), False)

    def dma(self, q, fn, reads=(), writes=(), nobar=False):
        return self._add(q, fn, tuple(reads), tuple(writes), True, nobar=nobar)

    def barrier(self):
        deps = set()
        for e in ENGS:
            ndma = 0
            seen_c = False
            for i in range(len(self.ops[e]) - 1, -1, -1):
                o = self.ops[e][i]
                if o["dma"]:
                    if ndma < NDMA:
                        if not o["nobar"]:
                            deps.add((e, i))
                        ndma += 1
                elif o["fn"] is not None and not seen_c:
                    deps.add((e, i))
                    seen_c = True
                if seen_c and ndma >= NDMA:
                    break
        for e in ENGS:
            self._add(e, None, (), (), False, extra={d for d in deps})

    def emit(self):
        nc = self.nc
        for e in ENGS:
            for o in self.ops[e]:
                for (de, di) in o["deps"]:
                    self.ops[de][di]["signal"] = True
        for e in ENGS:
            c = 0
            nd = 0
            for o in self.ops[e]:
                if o["dma"]:
                    o["dslot"] = nd % NDMA
                    o["dval"] = 16 * (nd // NDMA + 1)
                    nd += 1
                elif o["signal"]:
                    assert o["fn"] is not None
                    c += 1
                    o["cnt"] = c
        with contextlib.ExitStack() as st:
            csem = {e: st.enter_context(nc.semaphore("c_" + e)) for e in ENGS}
            dsem = {e: [st.enter_context(nc.semaphore("d_%s_%d" % (e, i))) for i in range(NDMA)]
                    for e in ("sp", "pool", "act")}
            block = st.enter_context(nc.Block())

            def run(e, eng):
                waited = {}
                for o in self.ops[e]:
                    need = {}
                    for (de, di) in o["deps"]:
                        d = self.ops[de][di]
                        if d["dma"]:
                            sem, val = dsem[de][d["dslot"]], d["dval"]
                        else:
                            sem, val = csem[de], d["cnt"]
                        k = id(sem)
                        if k not in need or need[k][1] < val:
                            need[k] = (sem, val)
                    if o["dma"]:
                        sem = dsem[e][o["dslot"]]
                        if o["dval"] > 16:
                            k = id(sem)
                            if k not in need or need[k][1] < o["dval"] - 16:
                                need[k] = (sem, o["dval"] - 16)
                    for k, (sem, val) in need.items():
                        if waited.get(k, 0) >= val:
                            continue
                        waited[k] = val
                        eng.wait_ge(sem, val)
                    if o["fn"] is None:
                        continue
                    ins = o["fn"](eng)
                    if o["dma"]:
                        ins.then_inc(dsem[e][o["dslot"]], 16)
                    elif o["signal"]:
                        ins.then_inc(csem[e], 1)

            @block.sync
            def _(eng):
                run("sp", eng)

            @block.tensor
            def _(eng):
                run("pe", eng)

            @block.scalar
            def _(eng):
                run("act", eng)

            @block.vector
            def _(eng):
                run("dve", eng)

            @block.gpsimd
            def _(eng):
                run("pool", eng)


class Arena:
    def __init__(self, t, n):
        self.t, self.n, self.off = t, n, 0

    def reset(self, off=0):
        self.off = off

    def alloc(self, cols):
        v = self.t[:, self.off:self.off + cols]
        self.off += cols
        assert self.off <= self.n, (self.off, self.n)
        return v


def build_nc(NSB):
    nc = bass.Bass("TRN2", target_bir_lowering=False)
    NTOK = NSB * SBT

    def din(name, shape):
        return nc.dram_tensor(name, list(shape), F32, kind="ExternalInput").ap()

    xs = din("xs", [NTOK + 2 * HALO, DM])
    mem = din("mem", [256, DM])
    w_in = din("w_in", [DM, 4608])
    w_mkv = din("w_mem_kv", [DM, 1024])
    w_out = din("w_out", [1536, DM])
    norm_g = din("norm_g", [1, DM])
    mem_norm_g = din("mem_norm_g", [1, DM])
    final_g = din("final_norm_g", [1, DM])
    sg_ln_g = din("sg_ln_g", [1, 512])
    sg_ln_b = din("sg_ln_b", [1, 512])
    sg_w = din("sg_w", [8, 128, 128])
    sg_b = din("sg_b", [8, 128])
    biasM = din("biasM", [128, 4, 1536])
    emask = din("emask", [128, 2 * NSB])
    ident_d = din("ident", [128, 128])
    out = nc.dram_tensor("out", [NTOK, DM], F32, kind="ExternalOutput").ap()

    wb_in = nc.dram_tensor("wb_in", [DM, 4608], BF16, kind="Internal").ap()
    wb_out = nc.dram_tensor("wb_out", [1536, DM], BF16, kind="Internal").ap()
    wb_bias = nc.dram_tensor("wb_bias", [128, 4, 1536], BF16, kind="Internal").ap()
    kv_scr = nc.dram_tensor("kv_scr", [4, 2, 128, 2048], BF16, kind="Internal").ap()
    w_in_v = wb_in.rearrange("(kc p) c -> p kc c", p=128)
    w_mkv_v = w_mkv.rearrange("(kc p) c -> p kc c", p=128)
    w_out_v = wb_out.rearrange("(kc p) c -> p kc c", p=128)

    with contextlib.ExitStack() as st:
        def sb(name, shape, dt):
            return st.enter_context(nc.sbuf_tensor("s_" + name, list(shape), dt))

        HCOLS = 55296
        FCOLS = 10240
        hT_own = sb("hT_own", [128, 8, SBT], BF16)
        attT = sb("attT", [128, 4, SBT], BF16)
        ident = sb("ident", [128, 128], BF16)
        ones = sb("ones", [128, 128], BF16)
        WsT = sb("WsT", [128, 8, 128], BF16)
        Bp = sb("Bp", [128, 4, 128], F32)
        lng = sb("lng", [128, 4], F32)
        kmT = sb("kmT", [128, 4, 256], BF16)
        vm = sb("vm", [128, 2, 512], BF16)
        emk = sb("emk", [128, 2 * NSB], F32)
        small = sb("small", [128, 64], F32)
        bigH_t = sb("bigH", [128, HCOLS], BF16)
        bigF_t = sb("bigF", [128, FCOLS], F32)
        pball = st.enter_context(nc.psum_tensor("pball", [128, 8, 512], F32))
        AH = Arena(bigH_t, HCOLS)
        AF_ = Arena(bigF_t, FCOLS)

        S = Sched(nc)
        PB = lambda k: ("pb", k)

        def bank(k):
            return pball[:, k, :]

        def bank_bf(k):
            return pball[:, k, :].bitcast(BF16)

        def ACT(out_, in_, func, rd, wr, **kw):
            S.op("act", lambda e: e.activation(out=out_, in_=in_, func=func, **kw), rd, wr)

        def MM(out_, lhsT, rhs, start, stop, rd, wr):
            S.op("pe", lambda e: e.matmul(out_, lhsT, rhs, start=start, stop=stop), rd, wr)

        def TR(out_, in_, rd, wr):
            S.op("pe", lambda e: e.transpose(out_, in_, ident[:]), list(rd) + ["ident"], wr)

        def TT(eng, out_, in0, in1, op, rd, wr):
            S.op(eng, lambda e: e.tensor_tensor(out=out_, in0=in0, in1=in1, op=op), rd, wr)

        def STT(eng, out_, in0, scalar, in1, op0, op1, rd, wr):
            S.op(eng, lambda e: e.scalar_tensor_tensor(out=out_, in0=in0, scalar=scalar, in1=in1,
                                                       op0=op0, op1=op1), rd, wr)

        def TS(eng, out_, in0, s1, s2, op0, op1, rd, wr):
            if op1 is None:
                S.op(eng, lambda e: e.tensor_scalar(out=out_, in0=in0, scalar1=s1, scalar2=None,
                                                    op0=op0), rd, wr)
            else:
                S.op(eng, lambda e: e.tensor_scalar(out=out_, in0=in0, scalar1=s1, scalar2=s2,
                                                    op0=op0, op1=op1), rd, wr)

        def CP(eng, out_, in_, rd, wr):
            S.op(eng, lambda e: e.tensor_copy(out=out_, in_=in_), rd, wr)

        def RCP(out_, in_, rd, wr):
            S.op("dve", lambda e: e.reciprocal(out=out_, in_=in_), rd, wr)

        def RCPF(out_, in_, rd, wr):
            S.op("dve", lambda e: e.reciprocal_approx_fast(out=out_, in_=in_), rd, wr)

        def MSET(eng, ap, val, wr):
            S.op(eng, lambda e: e.memset(ap, val), (), wr)

        def DMA(q, out_, in_, rd, wr, nobar=False):
            S.dma(q, lambda e: e.dma_start(out=out_, in_=in_), rd, wr, nobar=nobar)

        small_i = [0]

        def scol():
            i = small_i[0] % 64
            small_i[0] += 1
            return small[:, i:i + 1], ("small", i)

        def rms_rows(x_ap, x_key, sq_ap, sq_key):
            ss, kss = scol()
            ACT(sq_ap, x_ap, AF.Square, [x_key], [sq_key, kss], accum_out=ss)
            sd, ksd = scol()
            ACT(sd, ss, AF.Sqrt, [kss], [ksd], bias=EPS, scale=1.0 / DM)
            rs, krs = scol()
            RCP(rs, sd, [ksd], [krs])
            return rs, krs

        evac_rr = [0]

        def norm_transpose_pipe(n, src_fn, gbc, gkey, xa, hb, sq, dst_fn, tag="", tiles=None):
            pend = {}

            def stage1(t):
                i = t % len(xa)
                DMA("sp", xa[i], src_fn(t), [], [(tag + "xa", i)])
                ss, kss = scol()
                ACT(sq, xa[i], AF.Square, [(tag + "xa", i)], [tag + "sq", kss], accum_out=ss)
                sd, ksd = scol()
                ACT(sd, ss, AF.Sqrt, [kss], [ksd], bias=EPS, scale=1.0 / DM)
                pend[t] = (sd, ksd)

            def stage2(t):
                i = t % len(xa)
                ih = t % len(hb)
                sd, ksd = pend.pop(t)
                rs, krs = scol()
                RCP(rs, sd, [ksd], [krs])
                STT("dve", hb[ih], xa[i], rs, gbc, ALU.mult, ALU.mult,
                    [(tag + "xa", i), krs, gkey], [(tag + "hb", ih)])

            def stage3(t):
                i = t % len(hb)
                k = t % 3
                pv = bank_bf(k).rearrange("p (a b) -> p a b", b=128)
                for kc in range(8):
                    TR(pv[:, kc, :], hb[i][:, kc * 128:(kc + 1) * 128], [(tag + "hb", i)], [PB(k)])

            def stage4(t):
                k = t % 3
                pv = bank_bf(k).rearrange("p (a b) -> p a b", b=128)
                dst, dkey = dst_fn(t)
                eng = "act" if (evac_rr[0] % 2 == 0) else "dve"
                evac_rr[0] += 1
                if eng == "act":
                    ACT(dst, pv, AF.Copy, [PB(k)], [dkey])
                else:
                    CP("dve", dst, pv, [PB(k)], [dkey])

            tl = list(range(n)) if tiles is None else list(tiles)
            n = len(tl)
            for i in range(n + 3):
                if i < n:
                    stage1(tl[i])
                if 0 <= i - 1 < n:
                    stage2(tl[i - 1])
                if 0 <= i - 2 < n:
                    stage3(tl[i - 2])
                if 0 <= i - 3 < n:
                    stage4(tl[i - 3])

        DMA("pool", ident[:], ident_d, [], ["ident"])
        MSET("pool", ones[:], 1.0, ["ones"])
        DMA("sp", emk[:], emask, [], ["emk"])
        WBIN = [("wbin", kc) for kc in range(8)]
        WBOUT = [("wbout", kc) for kc in range(12)]
        WBIN2 = [("wbin2", kc) for kc in range(8)]
        w_in_f = w_in.rearrange("(kc p) c -> p kc c", p=128)

        def convert_hp(hp):
            c0 = hp * 128
            for grp in range(4):
                c = grp * 512 + c0
                DMA("pool", w_in_v[:, :, c:c + 128], w_in_f[:, :, c:c + 128], [], [("wbin_hp", hp, grp)],
                    nobar=True)
            DMA("pool", wb_bias[:, hp, :], biasM[:, hp, :], [], [("wbb", hp)], nobar=True)

        convert_hp(0)

        def convert_rest():
            for kc in range(8):
                DMA("pool", wb_in[kc * 128:(kc + 1) * 128, 2048:4608],
                    w_in[kc * 128:(kc + 1) * 128, 2048:4608], [], [("wbin2", kc)], nobar=True)
            for kc in range(12):
                DMA("pool", wb_out[kc * 128:(kc + 1) * 128, :], w_out[kc * 128:(kc + 1) * 128, :], [],
                    [("wbout", kc)], nobar=True)

        sst = {}

        def emit_setup_loads():
            sst["xa"] = [AF_.alloc(1024), AF_.alloc(1024)]
            sst["gbc"] = AF_.alloc(1024)
            sst["hb"] = [AH.alloc(1024), AH.alloc(1024)]
            sst["sq"] = AH.alloc(1024)
            sst["wm"] = AH.alloc(8192).rearrange("p (a b) -> p a b", b=1024)
            sst["memnT"] = AH.alloc(2048).rearrange("p (a b) -> p a b", b=256)
            sst["lnb_bc"] = AH.alloc(512)
            sst["sgw"] = AH.alloc(1024).rearrange("p (a b) -> p a b", b=128)
            sst["sgbb"] = AF_.alloc(512).rearrange("p (a b) -> p a b", b=128)
            DMA("pool", sst["gbc"], mem_norm_g.partition_broadcast(128)[:, 0, :], [], ["gbc_s"])
            sst["wm_loads"] = lambda: [DMA("pool", sst["wm"][:, kc, :], w_mkv_v[:, kc, :], [("hT", 16 + 2 * kc)],
                                           [("wm", kc)]) for kc in range(8)]
            DMA("pool", sst["sgw"], sg_w.rearrange("g p q -> p g q"), [], ["sgw"])
            DMA("pool", sst["lnb_bc"], sg_ln_b.partition_broadcast(128)[:, 0, :], [], ["lnb"])
            for j in range(4):
                DMA("pool", lng[:, j:j + 1], sg_ln_g[0:1, j * 128:(j + 1) * 128].rearrange("a p -> p a"),
                    [], ["lng"])
                for hh in range(2):
                    g = 2 * j + hh
                    DMA("pool", sst["sgbb"][hh * 64:(hh + 1) * 64, j, :],
                        sg_b[g:g + 1, :].partition_broadcast(64)[:, 0, :], [], ["sgbb"])

        def emit_setup():
            xa, gbc, hb, sq = sst["xa"], sst["gbc"], sst["hb"], sst["sq"]
            wm, memnT, lnb_bc, sgw, sgbb = (sst["wm"], sst["memnT"], sst["lnb_bc"], sst["sgw"],
                                            sst["sgbb"])
            norm_transpose_pipe(2, lambda t: mem[t * 128:(t + 1) * 128, :], gbc, "gbc_s", xa, hb, sq,
                                lambda t: (memnT[:, :, t * 128:(t + 1) * 128], ("memnT", t)), tag="s_")
            wmk = [("wm", kc) for kc in range(8)]
            mnk = [("memnT", 0), ("memnT", 1)]
            for h in range(4):
                k = 2 + h % 2
                for kc in range(8):
                    MM(bank(k)[:, 0:256], wm[:, kc, h * 128:(h + 1) * 128], memnT[:, kc, :],
                       kc == 0, kc == 7, wmk + mnk, [PB(k)])
                ACT(kmT[:, h, :], bank(k)[:, 0:256], AF.Copy, [PB(k)], ["kmT"])
            for mt in range(2):
                k = 4 + mt
                for kc in range(8):
                    MM(bank(k), memnT[:, kc, mt * 128:(mt + 1) * 128], wm[:, kc, 512:1024],
                       kc == 0, kc == 7, wmk + mnk, [PB(k)])
                ACT(vm[:, mt, :], bank(k), AF.Copy, [PB(k)], ["vm"])
            pv6 = bank_bf(6).rearrange("p (a b) -> p a b", b=128)
            for g in range(8):
                TR(pv6[:, g, :], sgw[:, g, :], ["sgw"], [PB(6)])
            CP("dve", WsT[:], pv6, [PB(6)], ["WsT"])
            for j in range(4):
                for hh in range(2):
                    g = 2 * j + hh
                    MM(bank(7)[hh * 64:(hh + 1) * 64, j * 128:(j + 1) * 128],
                       lnb_bc[:, g * 64:(g + 1) * 64], WsT[:, g, :], True, True,
                       ["lnb", "WsT"], [PB(7)])
            TT("dve", Bp[:], bank(7).rearrange("p (a b) -> p a b", b=128), sgbb, ALU.add,
               [PB(7), "sgbb"], ["Bp"])


        AH.reset()
        hT_halo = AH.alloc(16384).rearrange("p (a b) -> p a b", b=2048)
        wsets = []
        for i in range(2):
            ws = dict(
                wq=AH.alloc(1024).rearrange("p (a b) -> p a b", b=128),
                wk=AH.alloc(1024).rearrange("p (a b) -> p a b", b=128),
                wv=AH.alloc(1024).rearrange("p (a b) -> p a b", b=128),
                wg=AH.alloc(1024).rearrange("p (a b) -> p a b", b=128),
                mbf=AH.alloc(1536), i=i)
            ws["mb"] = ws["mbf"].rearrange("p (c h w) -> p c h w", c=3, h=2)
            wsets.append(ws)
        AB_H0 = AH.off

        def load_wset(ws, hp):
            c0 = hp * 128
            i = ws["i"]
            DMA("sp", ws["wk"], w_in_v[:, :, 512 + c0:512 + c0 + 128], [("wbin_hp", hp, 1)], [("wk", i)])
            DMA("sp", ws["wv"], w_in_v[:, :, 1024 + c0:1024 + c0 + 128], [("wbin_hp", hp, 2)], [("wv", i)])
            DMA("sp", ws["wq"], w_in_v[:, :, c0:c0 + 128], [("wbin_hp", hp, 0)], [("wq", i)])
            DMA("sp", ws["wg"], w_in_v[:, :, 1536 + c0:1536 + c0 + 128], [("wbin_hp", hp, 3)], [("wg", i)])
            DMA("sp", ws["mbf"], wb_bias[:, hp, :], [("wbb", hp)], [("mb", i)])

        def hT_blk(b):
            if b in (0, 1):
                v = hT_halo[:, :, b * 512:(b + 1) * 512]
            elif b in (6, 7):
                v = hT_halo[:, :, 1024 + (b - 6) * 512:1024 + (b - 5) * 512]
            else:
                v = hT_own[:, :, (b - 2) * 512:(b - 1) * 512]
            return v, [("hT", 4 * b + i) for i in range(4)]

        ALLT = []
        for ci, d in enumerate(CFGS):
            A = HALO // d
            nq = SBT // (128 * d)
            for r in range(d):
                for j in range(nq + 1):
                    ALLT.append((ci, d, r, j, nq, A))
        NT = len(ALLT)

        for s in range(NSB):
            AH.reset(AB_H0); AF_.reset()
            hb = [AH.alloc(1024) for _ in range(4)]
            sq = AH.alloc(1024)
            xa = [AF_.alloc(1024) for _ in range(4)]
            gbc = AF_.alloc(1024)
            DMA("sp", gbc, norm_g.partition_broadcast(128)[:, 0, :], [], ["gbc"])
            if s == 0:
                emit_setup_loads()

            if s == 0:
                pass

            def dstA(t):
                v, _ = hT_blk(t // 4)
                o = (t % 4) * 128
                return v[:, :, o:o + 128], ("hT", t)
            tiles_a = None if s == 0 else list(range(8, 32))
            norm_transpose_pipe(32, lambda t: xs[s * SBT + t * 128:s * SBT + (t + 1) * 128, :],
                                gbc, "gbc", xa, hb, sq, dstA, tiles=tiles_a)
            if s == 0:
                sst["wm_loads"]()
                emit_setup()
            load_wset(wsets[0], 0)
            S.barrier()

            AH.reset(AB_H0); AF_.reset()
            Q2 = AF_.alloc(2048).bitcast(BF16).rearrange("p (h t) -> p h t", h=2)
            QA = Q2[:, 0, :]
            QB = Q2[:, 1, :]
            KT = AH.alloc(4096)
            VT = AH.alloc(4096)
            Vt_flat = AH.alloc(NT * 256)
            Vt = Vt_flat.rearrange("p (a b) -> p a b", b=256)
            PT = [AH.alloc(512).rearrange("p (a b) -> p a b", b=256) for _ in range(3)]
            GT = AF_.alloc(2048)
            acc = AF_.alloc(4096).rearrange("p (a b) -> p a b", b=2048)
            rl = AF_.alloc(2048)
            MSET("pool", Vt[:, :, 64:192], 1.0, ["Vt1"])
            MSET("pool", QA[64:128, :], 0.0, ["QTz0"])
            MSET("pool", QB[0:64, :], 0.0, ["QTz1"])
            KTK = [("KT", b_) for b_ in range(8)]
            VTK = [("VT", b_) for b_ in range(8)]
            QTK = [("QT", b_, h_) for b_ in range(4) for h_ in range(2)] + ["QTz0", "QTz1"]
            GTK = [("GT", b_) for b_ in range(4)]
            if s == 0:
                for hp_ in (1, 2, 3):
                    convert_hp(hp_)
                convert_rest()
            ACCK = [("acc", i) for i in range(16)]

            for hp in range(4):
                ws = wsets[hp % 2]
                wi = ws["i"]
                if hp + 1 < 4:
                    load_wset(wsets[(hp + 1) % 2], hp + 1)
                pk = 0
                if s >= 1:
                    DMA("sp", KT[:, 0:2048], kv_scr[hp, 0], [("kvs", hp, 0)], [("KT", b_) for b_ in range(4)])
                    DMA("sp", VT[:, 0:2048], kv_scr[hp, 1], [("kvs", hp, 1)], [("VT", b_) for b_ in range(4)])
                for b in list(range(8)) + [12, 13, 14, 15]:
                    if b < 8:
                        hv, hk = hT_blk(b)
                        jobs = [(("wk", wi), ws["wk"], KT[:, b * 512:(b + 1) * 512], ("KT", b), AF.Copy, 1.0),
                                (("wv", wi), ws["wv"], VT[:, b * 512:(b + 1) * 512], ("VT", b), AF.Copy, 1.0)]
                        if s >= 1 and b < 4:
                            jobs = []
                        if 2 <= b <= 5:
                            o = (b - 2) * 512
                            jobs += [(("wq", wi), ws["wq"], None, ("QT", b - 2), AF.Copy, 0.125)]
                    else:
                        hv, hk = hT_blk(b - 10)
                        o = (b - 12) * 512
                        jobs = [(("wg", wi), ws["wg"], GT[:, o:o + 512], ("GT", b - 12), AF.Silu, 1.0)]
                    for (wkey, wt, dst, dkey, fn, sc) in jobs:
                        k = pk % 8
                        pk += 1
                        for kc in range(8):
                            MM(bank(k), wt[:, kc, :], hv[:, kc, :], kc == 0, kc == 7,
                               [wkey] + hk, [PB(k)])
                        if dst is None:
                            ACT(QA[0:64, o:o + 512], bank(k)[0:64, :], fn, [PB(k)], [dkey + (0,)], scale=sc)
                            ACT(QB[64:128, o:o + 512], bank(k)[64:128, :], fn, [PB(k)], [dkey + (1,)], scale=sc)
                        else:
                            ACT(dst, bank(k), fn, [PB(k)], [dkey], scale=sc)
                if s + 1 < NSB:
                    DMA("sp", kv_scr[hp, 0], KT[:, 2048:4096], [("KT", b_) for b_ in range(4, 8)],
                        [("kvs", hp, 0)])
                    DMA("sp", kv_scr[hp, 1], VT[:, 2048:4096], [("VT", b_) for b_ in range(4, 8)],
                        [("kvs", hp, 1)])
                for g0 in range(0, NT, 8):
                    grp = ALLT[g0:g0 + 8]
                    k = (g0 // 8) % 8
                    pvv = bank_bf(k).rearrange("p (a b) -> p a b", b=128)
                    for i, (ci, d, r, j, nq, A) in enumerate(grp):
                        u0 = r + d * (A + 128 * j - 64)
                        TR(pvv[:, i, :], VT[:, u0:u0 + d * 127 + 1:d], VTK, [PB(k)])
                    n = len(grp)
                    CP("dve", Vt[:, g0:g0 + n, 0:64], pvv[:, 0:n, 0:64], [PB(k)], [("Vt", g0, 0)])
                    CP("dve", Vt[:, g0:g0 + n, 192:256], pvv[:, 0:n, 64:128], [PB(k)], [("Vt", g0, 1)])

                def geom(ti):
                    ci, d, r, j, nq, A = ALLT[ti]
                    halves = []
                    if j >= 1:
                        halves.append((0, j - 1))
                    if j <= nq - 1:
                        halves.append((1, j))
                    return ci, d, r, j, nq, A, halves

                def stage1(ti):
                    ci, d, r, j, nq, A, halves = geom(ti)
                    u0 = r + d * (A + 128 * j - 64)
                    W = 128 * len(halves)
                    mcol0 = 128 * halves[0][0]
                    q0 = r + d * 128 * halves[0][1]
                    sk = ti % 4
                    so = pball[:, sk, 0:2 * W]
                    MM(so, ident[:], ws["mb"][:, ci, :, mcol0:mcol0 + W],
                       True, False, ["ident", ("mb", wi)], [PB(sk)])
                    MM(so, KT[:, u0:u0 + d * 127 + 1:d], Q2[:, :, q0:q0 + d * (W - 1) + 1:d],
                       False, True, KTK + QTK, [PB(sk)])

                def stage2(ti):
                    ci, d, r, j, nq, A, halves = geom(ti)
                    W = 128 * len(halves)
                    sk = ti % 4
                    if j == 0:
                        bias = emk[:, 2 * s:2 * s + 1]
                    elif j == nq:
                        bias = emk[:, 2 * s + 1:2 * s + 2]
                    else:
                        bias = 0.0
                    ACT(PT[ti % 3][:, :, 0:W],
                        pball[:, sk, 0:2 * W].rearrange("p (h w) -> p h w", h=2), AF.Exp,
                        [PB(sk), "emk"], [("PT", ti % 3)], bias=bias)

                st3 = dict(ctr=0, prev=None, cur=None)

                def stage3(ti):
                    ci, d, r, j, nq, A, halves = geom(ti)
                    pt = PT[ti % 3]
                    for hi_, (hf, cj) in enumerate(halves):
                        if hf == 1:
                            slot = st3["ctr"] % 2
                            st3["ctr"] += 1
                            st3["cur"] = slot
                            opening = True
                        else:
                            slot = st3["prev"]
                            opening = False
                        ko = 4 + 2 * slot
                        cs = hi_ * 128
                        vk = ["Vt1", ("Vt", (ti // 8) * 8, 0), ("Vt", (ti // 8) * 8, 1)]
                        MM(pball[:, ko, 0:128], Vt[:, ti, 0:128], pt[:, 0, cs:cs + 128],
                           opening, not opening, vk + [("PT", ti % 3)], [PB(ko)])
                        MM(pball[:, ko + 1, 0:128], Vt[:, ti, 128:256], pt[:, 1, cs:cs + 128],
                           opening, not opening, vk + [("PT", ti % 3)], [PB(ko + 1)])
                        if not opening:
                            t0 = r + d * 128 * cj
                            dst = acc[:, :, t0:t0 + d * 127 + 1:d]
                            src = pball[:, ko:ko + 2, 0:128]
                            blk0 = (d * 128 * cj) // 128
                            aks = ACCK[blk0:blk0 + d]
                            if ci == 0:
                                CP("dve", dst, src, [PB(ko), PB(ko + 1)], aks)
                            else:
                                TT("dve", dst, src, dst, ALU.add, [PB(ko), PB(ko + 1)] + aks, aks)
                    if j <= nq - 1:
                        st3["prev"] = st3["cur"]

                for i in range(NT + 2):
                    if i < NT:
                        stage1(i)
                    if 0 <= i - 1 < NT:
                        stage2(i - 1)
                    if 0 <= i - 2 < NT:
                        stage3(i - 2)
                if hp == 3:
                    S.barrier()
                RCP(rl[0:64, :], acc[64:128, 0, :], ACCK + ["FT"], ["rl"])
                RCP(rl[64:128, :], acc[0:64, 1, :], ACCK + ["FT"], ["rl"])
                TT("pool", rl, rl, GT, ALU.mult, ["rl", "FT"] + GTK, ["rl"])
                TT("pool", attT[0:64, hp, :], acc[0:64, 0, :], rl[0:64, :], ALU.mult,
                   ACCK + ["rl", "FT"], [("attT", hp)])
                TT("pool", attT[64:128, hp, :], acc[64:128, 1, :], rl[64:128, :], ALU.mult,
                   ACCK + ["rl", "FT"], [("attT", hp)])

            AH.reset(); AF_.reset()
            wC = [AH.alloc(4096).rearrange("p (a b) -> p a b", b=512) for _ in range(5)]
            wo = AH.alloc(12288).rearrange("p (a b) -> p a b", b=1024)
            uT = AH.alloc(2048).rearrange("p (a b) -> p a b", b=512)
            sgT = AH.alloc(2048).rearrange("p (a b) -> p a b", b=512)
            sgmT = AH.alloc(2048).rearrange("p (a b) -> p a b", b=512)
            vvn = [AH.alloc(512) for _ in range(4)]
            ysgT = AH.alloc(2048).rearrange("p (a b) -> p a b", b=512)
            qmT = AH.alloc(2048).rearrange("p (a b) -> p a b", b=512)
            PTm = [AH.alloc(512) for _ in range(4)]
            ymemT = AH.alloc(2048).rearrange("p (a b) -> p a b", b=512)
            sqj = AH.alloc(1024)
            vv = [AF_.alloc(512) for _ in range(4)]
            rlm = [AF_.alloc(512), AF_.alloc(512)]
            xr = [AF_.alloc(1024) for _ in range(4)]
            fgbc = AF_.alloc(1024)
            ms = [AF_.alloc(512).rearrange("p (a b) -> p a b", b=128) for _ in range(2)]
            usg = AH.alloc(2048).rearrange("p (a b) -> p a b", b=512)

            cgrp = [2048, 2560, 3072, 3584, 4096]
            for gi in (1, 0, 2, 4, 3):
                cc = cgrp[gi]
                for k0 in (0, 4):
                    DMA("sp", wC[gi][:, k0:k0 + 4, :], w_in_v[:, k0:k0 + 4, cc:cc + 512], WBIN2,
                        [("wC", gi, kc) for kc in range(k0, k0 + 4)])
            for k0 in (0, 4, 8):
                DMA("sp", wo[:, k0:k0 + 4, :], w_out_v[:, k0:k0 + 4, :], WBOUT,
                    [("wo", kc) for kc in range(k0, k0 + 4)])
            DMA("sp", fgbc, final_g.partition_broadcast(128)[:, 0, :], [], ["fgbc", "FT"])
            w_u, w_v2, w_gs, w_qm, w_gm = wC
            free_banks = list(range(8))

            def balloc():
                assert free_banks, "PSUM banks exhausted"
                return free_banks.pop(0)

            def bfree(k):
                free_banks.append(k)

            def balloc_pair():
                best = None
                for a_ in free_banks:
                    if a_ < 7 and (a_ + 1) in free_banks:
                        sc = max(free_banks.index(a_), free_banks.index(a_ + 1))
                        if best is None or sc < best[0]:
                            best = (sc, a_)
                assert best is not None, "no adjacent PSUM bank pair free"
                a_ = best[1]
                free_banks.remove(a_)
                free_banks.remove(a_ + 1)
                return a_

            bst = [dict() for _ in range(4)]

            def groups_P(bb):
                hv = hT_own[:, :, bb * 512:(bb + 1) * 512]
                hk = [("hT", 8 + 4 * bb + i) for i in range(4)]
                stt_ = [dict() for _ in range(4)]
                bst[bb]["stt"] = stt_
                gl = []

                def vv_group(tt):
                    d_ = stt_[tt]
                    k = balloc()
                    for kc in range(8):
                        MM(bank(k), hv[:, kc, tt * 128:(tt + 1) * 128], w_v2[:, kc, :],
                           kc == 0, kc == 7, [("wC", 1, kc)] + hk, [PB(k)])
                    d_["sm"], d_["ksm"] = scol()
                    ACT(vv[tt], bank(k), AF.Gelu_apprx_tanh, [PB(k)],
                        [("vv", tt), d_["ksm"]], accum_out=d_["sm"])
                    bfree(k)

                def fm_group(wt, gi, dst, dkey, fn, ft):
                    k = balloc()
                    for kc in range(8):
                        MM(bank(k), wt[:, kc, ft * 128:(ft + 1) * 128], hv[:, kc, :],
                           kc == 0, kc == 7, [("wC", gi, kc)] + hk, [PB(k)])
                    ACT(dst[:, ft, :], bank(k), fn, [PB(k)], [(dkey, ft)])
                    bfree(k)

                for tt in range(4):
                    gl.append(lambda tt=tt: vv_group(tt))
                for (wt, gi, dst, dkey, fn) in ((w_u, 0, uT, "uT", AF.Gelu_apprx_tanh),
                                                (w_gs, 2, sgT, "sgT", AF.Silu),
                                                (w_qm, 3, qmT, "qmT", AF.Copy),
                                                (w_gm, 4, sgmT, "sgmT", AF.Silu)):
                    for ft in range(4):
                        gl.append(lambda wt=wt, gi=gi, dst=dst, dkey=dkey, fn=fn, ft=ft:
                                  fm_group(wt, gi, dst, dkey, fn, ft))
                return gl

            filler = []

            def fill(n):
                for _ in range(n):
                    if filler:
                        filler.pop(0)()

            def stage_M(bb):
                ft_ = ["FT"] if bb == 0 else []
                stt_ = bst[bb]["stt"]
                for tt in range(4):
                    d_ = stt_[tt]
                    d_["sq2"], d_["ksq"] = scol()
                    ACT(sqj[:, 0:512], vv[tt], AF.Square, [("vv", tt)], ["sqj", d_["ksq"]],
                        accum_out=d_["sq2"])
                fen, kfen = scol()
                ACT(fen, stt_[3]["sq2"], AF.Copy, [stt_[i_]["ksq"] for i_ in range(4)], [kfen])
                for tt in range(4):
                    d_ = stt_[tt]
                    d_["mean"], d_["kmean"] = scol()
                    TS("dve", d_["mean"], d_["sm"], 1.0 / 512, None, ALU.mult, None,
                       [d_["ksm"], kfen], [d_["kmean"]])
                for tt in range(4):
                    d_ = stt_[tt]
                    d_["m2"], d_["km2"] = scol()
                    TT("dve", d_["m2"], d_["mean"], d_["mean"], ALU.mult, [d_["kmean"]], [d_["km2"]])
                for tt in range(4):
                    d_ = stt_[tt]
                    d_["var"], d_["kvar"] = scol()
                    STT("dve", d_["var"], d_["sq2"], 1.0 / 512, d_["m2"], ALU.mult, ALU.subtract,
                        [d_["ksq"], d_["km2"], kfen], [d_["kvar"]])
                for tt in range(4):
                    d_ = stt_[tt]
                    d_["sd"], d_["ksd"] = scol()
                    ACT(d_["sd"], d_["var"], AF.Sqrt, [d_["kvar"]], [d_["ksd"]], bias=EPS, scale=1.0)
                for tt in range(4):
                    d_ = stt_[tt]
                    d_["rs"], d_["krs"] = scol()
                    RCP(d_["rs"], d_["sd"], [d_["ksd"]], [d_["krs"]])
                for tt in range(4):
                    d_ = stt_[tt]
                    d_["nb"], d_["knb"] = scol()
                    STT("dve", d_["nb"], d_["mean"], -1.0, d_["rs"], ALU.mult, ALU.mult,
                        [d_["kmean"], d_["krs"]], [d_["knb"]])
                for tt in range(4):
                    d_ = stt_[tt]
                    ACT(vvn[tt], vv[tt], AF.Identity, [("vv", tt), d_["krs"], d_["knb"]],
                        [("vvn", tt)], bias=d_["nb"], scale=d_["rs"])
                TT("pool", usg, uT, sgT, ALU.mult,
                   [("uT", f_) for f_ in range(4)] + [("sgT", f_) for f_ in range(4)], ["usg"])
                fill(8)
                mixb = []
                for tt in range(4):
                    km_ = balloc()
                    mixb.append(km_)
                    for j in range(4):
                        for hh in range(2):
                            g = 2 * j + hh
                            MM(bank(km_)[hh * 64:(hh + 1) * 64, j * 128:(j + 1) * 128],
                               vvn[tt][:, g * 64:(g + 1) * 64], WsT[:, g, :], True, True,
                               [("vvn", tt), "WsT"], [PB(km_)])
                fill(4)
                for tt in range(4):
                    i = tt % 2
                    bmv = bank(mixb[tt]).rearrange("p (a b) -> p a b", b=128)
                    for j in range(4):
                        STT("dve", ms[i][:, j, :], bmv[:, j, :], lng[:, j:j + 1], Bp[:, j, :],
                            ALU.mult, ALU.add, [PB(mixb[tt]), "lng", "Bp"], [("ms", i, j)] + ft_)
                    bfree(mixb[tt])
                    TT("pool", ysgT[:, :, tt * 128:(tt + 1) * 128], ms[i],
                       usg[:, :, tt * 128:(tt + 1) * 128], ALU.mult,
                       [("ms", i, j_) for j_ in range(4)] + ["usg"], [("ysgT", tt)])
                sbk = {}

                def scores(h):
                    for mt in range(2):
                        k = balloc()
                        sbk[(h, mt)] = k
                        MM(bank(k), kmT[:, h, mt * 128:(mt + 1) * 128], qmT[:, h, :], True, True,
                           ["kmT", ("qmT", h)], [PB(k)])

                scores(0)
                scores(1)
                for h in range(4):
                    if h + 2 < 4:
                        scores(h + 2)
                    for mt in range(2):
                        pi = (2 * h + mt) % 4
                        ACT(PTm[pi], bank(sbk[(h, mt)]), AF.Exp, [PB(sbk[(h, mt)])], [("PTm", pi)],
                            scale=1.0 / math.sqrt(128.0))
                        bfree(sbk[(h, mt)])
                    kO = balloc()
                    kL = balloc()
                    for mt in range(2):
                        pi = (2 * h + mt) % 4
                        MM(bank(kL), ones[:], PTm[pi], mt == 0, mt == 1,
                           ["ones", ("PTm", pi)], [PB(kL)])
                    for mt in range(2):
                        pi = (2 * h + mt) % 4
                        MM(bank(kO), vm[:, mt, h * 128:(h + 1) * 128], PTm[pi], mt == 0, mt == 1,
                           ["vm", ("PTm", pi)], [PB(kO)])
                    i = h % 2
                    RCP(rlm[i], bank(kL), [PB(kL)], [("rlm", i)] + ft_)
                    bfree(kL)
                    TT("pool", rlm[i], rlm[i], sgmT[:, h, :], ALU.mult, [("rlm", i), ("sgmT", h)],
                       [("rlm", i)])
                    TT("dve", ymemT[:, h, :], bank(kO), rlm[i], ALU.mult, [PB(kO), ("rlm", i)],
                       [("ymemT", h)])
                    bfree(kO)
                    if h == 1:
                        fill(4)
                    if h == 3:
                        fill(4)

            def stage_O(bb):
                ft_ = ["FT"] if bb == 0 else []
                for tt in range(4):
                    own0 = bb * 512 + tt * 128
                    r0 = s * SBT + HALO + own0
                    DMA("sp", xr[tt], xs[r0:r0 + 128, :], [], [("xr", tt)] + ft_)
                for tt in range(4):
                    own0 = bb * 512 + tt * 128
                    k0_ = balloc_pair()
                    ky = [k0_, k0_ + 1]
                    for half in range(2):
                        for kc in range(12):
                            if kc < 4:
                                lt, lk = attT[:, kc, own0:own0 + 128], ("attT", kc)
                            elif kc < 8:
                                lt, lk = ysgT[:, kc - 4, tt * 128:(tt + 1) * 128], ("ysgT", tt)
                            else:
                                lt, lk = ymemT[:, kc - 8, tt * 128:(tt + 1) * 128], ("ymemT", kc - 8)
                            MM(bank(ky[half]), lt, wo[:, kc, half * 512:(half + 1) * 512],
                               kc == 0, kc == 11, [lk, ("wo", kc)], [PB(ky[half])])
                    xv = xr[tt].rearrange("p (a b) -> p a b", b=512)
                    TT("dve", xv, pball[:, ky[0]:ky[0] + 2, :], xv, ALU.add,
                       [PB(ky[0]), PB(ky[1]), ("xr", tt)], [("xr", tt)])
                    bfree(ky[0])
                    bfree(ky[1])
                rss = []
                for tt in range(4):
                    ss, kss = scol()
                    ACT(sqj, xr[tt], AF.Square, [("xr", tt)], ["sqj", kss], accum_out=ss)
                    rss.append([ss, kss])
                for tt in range(4):
                    sd, ksd = scol()
                    ACT(sd, rss[tt][0], AF.Sqrt, [rss[tt][1]], [ksd], bias=EPS, scale=1.0 / DM)
                    rss[tt] += [sd, ksd]
                for tt in range(4):
                    rs, krs = scol()
                    RCP(rs, rss[tt][2], [rss[tt][3]], [krs])
                    rss[tt] += [rs, krs]
                for tt in range(4):
                    rs, krs = rss[tt][4], rss[tt][5]
                    STT("dve", xr[tt], xr[tt], rs, fgbc, ALU.mult, ALU.mult,
                        [("xr", tt), krs, "fgbc"], [("xr", tt)])
                    o0 = s * SBT + bb * 512 + tt * 128
                    DMA("sp", out[o0:o0 + 128, :], xr[tt], [("xr", tt)], [("out", s, bb, tt)])

            filler.extend(groups_P(0))
            fill(20)
            for bb in range(4):
                if bb + 1 < 4:
                    filler.extend(groups_P(bb + 1))
                stage_M(bb)
                fill(20)
                stage_O(bb)
            S.barrier()
        S.emit()
    return nc


def _t5_bucket(rel):
    nb_, md = 32, 1024
    half = nb_ // 2
    max_exact = half // 2
    ret = (rel > 0).astype(np.int32) * half
    n = np.abs(rel)
    large = max_exact + (np.log(np.maximum(n, 1).astype(np.float32) / max_exact)
                         / math.log(md / max_exact) * (half - max_exact)).astype(np.int32)
    large = np.minimum(large, half - 1)
    return (ret + np.where(n < max_exact, n, large)).astype(np.int32)


def _bias_layout(rel_bias):
    rel_bias = np.asarray(rel_bias, np.float32)
    kk = np.arange(128)[:, None]
    qq = np.arange(256)[None, :]
    rel = kk - qq + 64
    band = np.abs(rel) <= 64
    outp = np.empty((128, 4, 3, 2, 256), np.float32)
    for ci, d in enumerate(CFGS):
        bucket = _t5_bucket(rel * d)
        g = rel_bias[bucket]
        for h in range(8):
            m = g[:, :, h].copy()
            m[~band] = NEG
            outp[:, h // 2, ci, h % 2, :] = m
    return np.ascontiguousarray(outp.reshape(128, 4, 1536))


_NC_CACHE = {}


def _run(inputs, NSB):
    x = np.asarray(inputs["x"], np.float32)
    B, S, _ = x.shape
    per_core = NSB * SBT
    cps = S // per_core
    n_cores = B * cps
    if NSB not in _NC_CACHE:
        _NC_CACHE[NSB] = build_nc(NSB)
    nc = _NC_CACHE[NSB]
    biasM = _bias_layout(inputs["rel_bias"])
    ident = np.eye(128, dtype=np.float32)
    f = lambda k: np.ascontiguousarray(np.asarray(inputs[k], np.float32))
    w_in = f("w_in")[0]
    w_mkv = f("w_mem_kv")[0]
    w_out = f("w_out")[0]
    sg_w = f("sg_w")[0]
    sg_b = f("sg_b")[0]
    in_maps = []
    for c in range(n_cores):
        b, part = divmod(c, cps)
        t0 = part * per_core
        slab = np.zeros((per_core + 2 * HALO, DM), np.float32)
        lo, hi = t0 - HALO, t0 + per_core + HALO
        slo, shi = max(lo, 0), min(hi, S)
        slab[slo - lo:shi - lo] = x[b, slo:shi]
        em = np.zeros((128, 2 * NSB), np.float32)
        for s in range(NSB):
            g0 = t0 + s * SBT
            if g0 == 0:
                em[0:64, 2 * s] = NEG
            if g0 + SBT == S:
                em[64:128, 2 * s + 1] = NEG
        in_maps.append({
            "xs": slab, "mem": f("mem")[b], "w_in": w_in, "w_mem_kv": w_mkv, "w_out": w_out,
            "norm_g": f("norm_g").reshape(1, DM), "mem_norm_g": f("mem_norm_g").reshape(1, DM),
            "final_norm_g": f("final_norm_g").reshape(1, DM),
            "sg_ln_g": f("sg_ln_g").reshape(1, 512), "sg_ln_b": f("sg_ln_b").reshape(1, 512),
            "sg_w": sg_w, "sg_b": sg_b, "biasM": biasM, "emask": em, "ident": ident,
        })
    res = run_bass_kernel_spmd(nc, in_maps, core_ids=list(range(n_cores)))
    outp = np.empty((B, S, DM), np.float32)
    for c in range(n_cores):
        b, part = divmod(c, cps)
        t0 = part * per_core
        outp[b, t0:t0 + per_core] = res.results[c]["out"]
    return outp


def kernel(**inputs):
    return _run(inputs, 2)
```

```python
import contextlib
import math
import numpy as np
import ml_dtypes
import concourse.bass as bass
import concourse.mybir as mybir
from concourse.bass_utils import run_bass_kernel_spmd

F32 = mybir.dt.float32
BF16 = mybir.dt.bfloat16
AF = mybir.ActivationFunctionType
ALU = mybir.AluOpType

ENGS = ("pe", "act", "dve", "pool", "sp")
NDMA = 12
DM = 1024
SBT = 2048
HALO = 1024
EPS = 1e-6
NEG = -30000.0
CFGS = (1, 4, 16)


class Sched:
    def __init__(self, nc):
        self.nc = nc
        self.ops = {e: [] for e in ENGS}
        self.last_writer = {}
        self.readers = {}

    def _add(self, eng, fn, reads, writes, dma, extra=None, nobar=False):
        idx = len(self.ops[eng])
        me = (eng, idx)
        deps = set()
        raw = set()
        for k in reads:
            w = self.last_writer.get(k)
            if w is not None:
                deps.add(w)
                raw.add(w)
        for k in writes:
            w = self.last_writer.get(k)
            if w is not None:
                deps.add(w)
            for r in self.readers.get(k, ()):
                deps.add(r)
        deps.discard(me)
        if not dma:
            if eng == "pe":
                deps = {d for d in deps if d[0] != "pe" or self.ops["pe"][d[1]]["dma"]}
        if extra:
            deps |= set(extra)
        self.ops[eng].append(dict(fn=fn, deps=deps, dma=dma, signal=False, nobar=nobar))
        for k in reads:
            self.readers.setdefault(k, []).append(me)
        for k in writes:
            self.last_writer[k] = me
            self.readers[k] = []
        return me

    def op(self, eng, fn, reads=(), writes=()):
        return self._add(eng, fn, tuple(reads), tuple(writes), False)

    def dma(self, q, fn, reads=(), writes=(), nobar=False):
        return self._add(q, fn, tuple(reads), tuple(writes), True, nobar=nobar)

    def barrier(self):
        deps = set()
        for e in ENGS:
            ndma = 0
            seen_c = False
            for i in range(len(self.ops[e]) - 1, -1, -1):
                o = self.ops[e][i]
                if o["dma"]:
                    if ndma < NDMA:
                        if not o["nobar"]:
                            deps.add((e, i))
                        ndma += 1
                elif o["fn"] is not None and not seen_c:
                    deps.add((e, i))
                    seen_c = True
                if seen_c and ndma >= NDMA:
                    break
        for e in ENGS:
            self._add(e, None, (), (), False, extra={d for d in deps})

    def emit(self):
        nc = self.nc
        for e in ENGS:
            for o in self.ops[e]:
                for (de, di) in o["deps"]:
                    self.ops[de][di]["signal"] = True
        for e in ENGS:
            c = 0
            nd = 0
            for o in self.ops[e]:
                if o["dma"]:
                    o["dslot"] = nd % NDMA
                    o["dval"] = 16 * (nd // NDMA + 1)
                    nd += 1
                elif o["signal"]:
                    assert o["fn"] is not None
                    c += 1
                    o["cnt"] = c
        with contextlib.ExitStack() as st:
            csem = {e: st.enter_context(nc.semaphore("c_" + e)) for e in ENGS}
            dsem = {e: [st.enter_context(nc.semaphore("d_%s_%d" % (e, i))) for i in range(NDMA)]
                    for e in ("sp", "pool", "act")}
            block = st.enter_context(nc.Block())

            def run(e, eng):
                waited = {}
                for o in self.ops[e]:
                    need = {}
                    for (de, di) in o["deps"]:
                        d = self.ops[de][di]
                        if d["dma"]:
                            sem, val = dsem[de][d["dslot"]], d["dval"]
                        else:
                            sem, val = csem[de], d["cnt"]
                        k = id(sem)
                        if k not in need or need[k][1] < val:
                            need[k] = (sem, val)
                    if o["dma"]:
                        sem = dsem[e][o["dslot"]]
                        if o["dval"] > 16:
                            k = id(sem)
                            if k not in need or need[k][1] < o["dval"] - 16:
                                need[k] = (sem, o["dval"] - 16)
                    for k, (sem, val) in need.items():
                        if waited.get(k, 0) >= val:
                            continue
                        waited[k] = val
                        eng.wait_ge(sem, val)
                    if o["fn"] is None:
                        continue
                    ins = o["fn"](eng)
                    if o["dma"]:
                        ins.then_inc(dsem[e][o["dslot"]], 16)
                    elif o["signal"]:
                        ins.then_inc(csem[e], 1)

            @block.sync
            def _(eng):
                run("sp", eng)

            @block.tensor
            def _(eng):
                run("pe", eng)

            @block.scalar
            def _(eng):
                run("act", eng)

            @block.vector
            def _(eng):
                run("dve", eng)

            @block.gpsimd
            def _(eng):
                run("pool", eng)


class Arena:
    def __init__(self, t, n):
        self.t, self.n, self.off = t, n, 0

    def reset(self, off=0):
        self.off = off

    def alloc(self, cols):
        v = self.t[:, self.off:self.off + cols]
        self.off += cols
        assert self.off <= self.n, (self.off, self.n)
        return v


def build_nc(NSB):
    nc = bass.Bass("TRN2", target_bir_lowering=False)
    NTOK = NSB * SBT

    def din(name, shape):
        return nc.dram_tensor(name, list(shape), F32, kind="ExternalInput").ap()

    xs = din("xs", [NTOK + 2 * HALO, DM])
    mem = din("mem", [256, DM])
    w_in = din("w_in", [DM, 4608])
    w_mkv = din("w_mem_kv", [DM, 1024])
    w_out = din("w_out", [1536, DM])
    norm_g = din("norm_g", [1, DM])
    mem_norm_g = din("mem_norm_g", [1, DM])
    final_g = din("final_norm_g", [1, DM])
    sg_ln_g = din("sg_ln_g", [1, 512])
    sg_ln_b = din("sg_ln_b", [1, 512])
    sg_w = din("sg_w", [8, 128, 128])
    sg_b = din("sg_b", [8, 128])
    biasM = din("biasM", [128, 4, 1536])
    emask = din("emask", [128, 2 * NSB])
    ident_d = din("ident", [128, 128])
    out = nc.dram_tensor("out", [NTOK, DM], F32, kind="ExternalOutput").ap()

    wb_in = nc.dram_tensor("wb_in", [DM, 4608], BF16, kind="Internal").ap()
    wb_out = nc.dram_tensor("wb_out", [1536, DM], BF16, kind="Internal").ap()
    wb_bias = nc.dram_tensor("wb_bias", [128, 4, 1536], BF16, kind="Internal").ap()
    kv_scr = nc.dram_tensor("kv_scr", [4, 2, 128, 2048], BF16, kind="Internal").ap()
    w_in_v = wb_in.rearrange("(kc p) c -> p kc c", p=128)
    w_mkv_v = w_mkv.rearrange("(kc p) c -> p kc c", p=128)
    w_out_v = wb_out.rearrange("(kc p) c -> p kc c", p=128)

    with contextlib.ExitStack() as st:
        def sb(name, shape, dt):
            return st.enter_context(nc.sbuf_tensor("s_" + name, list(shape), dt))

        HCOLS = 55296
        FCOLS = 10240
        hT_own = sb("hT_own", [128, 8, SBT], BF16)
        attT = sb("attT", [128, 4, SBT], BF16)
        ident = sb("ident", [128, 128], BF16)
        ones = sb("ones", [128, 128], BF16)
        WsT = sb("WsT", [128, 8, 128], BF16)
        Bp = sb("Bp", [128, 4, 128], F32)
        lng = sb("lng", [128, 4], F32)
        kmT = sb("kmT", [128, 4, 256], BF16)
        vm = sb("vm", [128, 2, 512], BF16)
        emk = sb("emk", [128, 2 * NSB], F32)
        small = sb("small", [128, 64], F32)
        bigH_t = sb("bigH", [128, HCOLS], BF16)
        bigF_t = sb("bigF", [128, FCOLS], F32)
        pball = st.enter_context(nc.psum_tensor("pball", [128, 8, 512], F32))
        AH = Arena(bigH_t, HCOLS)
        AF_ = Arena(bigF_t, FCOLS)

        S = Sched(nc)
        PB = lambda k: ("pb", k)

        def bank(k):
            return pball[:, k, :]

        def bank_bf(k):
            return pball[:, k, :].bitcast(BF16)

        def ACT(out_, in_, func, rd, wr, **kw):
            S.op("act", lambda e: e.activation(out=out_, in_=in_, func=func, **kw), rd, wr)

        def MM(out_, lhsT, rhs, start, stop, rd, wr):
            S.op("pe", lambda e: e.matmul(out_, lhsT, rhs, start=start, stop=stop), rd, wr)

        def TR(out_, in_, rd, wr):
            S.op("pe", lambda e: e.transpose(out_, in_, ident[:]), list(rd) + ["ident"], wr)

        def TT(eng, out_, in0, in1, op, rd, wr):
            S.op(eng, lambda e: e.tensor_tensor(out=out_, in0=in0, in1=in1, op=op), rd, wr)

        def STT(eng, out_, in0, scalar, in1, op0, op1, rd, wr):
            S.op(eng, lambda e: e.scalar_tensor_tensor(out=out_, in0=in0, scalar=scalar, in1=in1,
                                                       op0=op0, op1=op1), rd, wr)

        def TS(eng, out_, in0, s1, s2, op0, op1, rd, wr):
            if op1 is None:
                S.op(eng, lambda e: e.tensor_scalar(out=out_, in0=in0, scalar1=s1, scalar2=None,
                                                    op0=op0), rd, wr)
            else:
                S.op(eng, lambda e: e.tensor_scalar(out=out_, in0=in0, scalar1=s1, scalar2=s2,
                                                    op0=op0, op1=op1), rd, wr)

        def CP(eng, out_, in_, rd, wr):
            S.op(eng, lambda e: e.tensor_copy(out=out_, in_=in_), rd, wr)

        def RCP(out_, in_, rd, wr):
            S.op("dve", lambda e: e.reciprocal(out=out_, in_=in_), rd, wr)

        def RCPF(out_, in_, rd, wr):
            S.op("dve", lambda e: e.reciprocal_approx_fast(out=out_, in_=in_), rd, wr)

        def MSET(eng, ap, val, wr):
            S.op(eng, lambda e: e.memset(ap, val), (), wr)

        def DMA(q, out_, in_, rd, wr, nobar=False):
            S.dma(q, lambda e: e.dma_start(out=out_, in_=in_), rd, wr, nobar=nobar)

        small_i = [0]

        def scol():
            i = small_i[0] % 64
            small_i[0] += 1
            return small[:, i:i + 1], ("small", i)

        def rms_rows(x_ap, x_key, sq_ap, sq_key):
            ss, kss = scol()
            ACT(sq_ap, x_ap, AF.Square, [x_key], [sq_key, kss], accum_out=ss)
            sd, ksd = scol()
            ACT(sd, ss, AF.Sqrt, [kss], [ksd], bias=EPS, scale=1.0 / DM)
            rs, krs = scol()
            RCP(rs, sd, [ksd], [krs])
            return rs, krs

        evac_rr = [0]

        def norm_transpose_pipe(n, src_fn, gbc, gkey, xa, hb, sq, dst_fn, tag="", tiles=None):
            pend = {}

            def stage1(t):
                i = t % len(xa)
                DMA("sp", xa[i], src_fn(t), [], [(tag + "xa", i)])
                ss, kss = scol()
                ACT(sq, xa[i], AF.Square, [(tag + "xa", i)], [tag + "sq", kss], accum_out=ss)
                sd, ksd = scol()
                ACT(sd, ss, AF.Sqrt, [kss], [ksd], bias=EPS, scale=1.0 / DM)
                pend[t] = (sd, ksd)

            def stage2(t):
                i = t % len(xa)
                ih = t % len(hb)
                sd, ksd = pend.pop(t)
                rs, krs = scol()
                RCP(rs, sd, [ksd], [krs])
                STT("dve", hb[ih], xa[i], rs, gbc, ALU.mult, ALU.mult,
                    [(tag + "xa", i), krs, gkey], [(tag + "hb", ih)])

            def stage3(t):
                i = t % len(hb)
                k = t % 3
                pv = bank_bf(k).rearrange("p (a b) -> p a b", b=128)
                for kc in range(8):
                    TR(pv[:, kc, :], hb[i][:, kc * 128:(kc + 1) * 128], [(tag + "hb", i)], [PB(k)])

            def stage4(t):
                k = t % 3
                pv = bank_bf(k).rearrange("p (a b) -> p a b", b=128)
                dst, dkey = dst_fn(t)
                eng = "act" if (evac_rr[0] % 2 == 0) else "dve"
                evac_rr[0] += 1
                if eng == "act":
                    ACT(dst, pv, AF.Copy, [PB(k)], [dkey])
                else:
                    CP("dve", dst, pv, [PB(k)], [dkey])

            tl = list(range(n)) if tiles is None else list(tiles)
            n = len(tl)
            for i in range(n + 3):
                if i < n:
                    stage1(tl[i])
                if 0 <= i - 1 < n:
                    stage2(tl[i - 1])
                if 0 <= i - 2 < n:
                    stage3(tl[i - 2])
                if 0 <= i - 3 < n:
                    stage4(tl[i - 3])

        DMA("pool", ident[:], ident_d, [], ["ident"])
        MSET("pool", ones[:], 1.0, ["ones"])
        DMA("sp", emk[:], emask, [], ["emk"])
        WBIN = [("wbin", kc) for kc in range(8)]
        WBOUT = [("wbout", kc) for kc in range(12)]
        WBIN2 = [("wbin2", kc) for kc in range(8)]
        w_in_f = w_in.rearrange("(kc p) c -> p kc c", p=128)

        def convert_hp(hp):
            c0 = hp * 128
            for grp in range(4):
                c = grp * 512 + c0
                DMA("pool", w_in_v[:, :, c:c + 128], w_in_f[:, :, c:c + 128], [], [("wbin_hp", hp, grp)],
                    nobar=True)
            DMA("pool", wb_bias[:, hp, :], biasM[:, hp, :], [], [("wbb", hp)], nobar=True)

        convert_hp(0)

        def convert_rest():
            for kc in range(8):
                DMA("pool", wb_in[kc * 128:(kc + 1) * 128, 2048:4608],
                    w_in[kc * 128:(kc + 1) * 128, 2048:4608], [], [("wbin2", kc)], nobar=True)
            for kc in range(12):
                DMA("pool", wb_out[kc * 128:(kc + 1) * 128, :], w_out[kc * 128:(kc + 1) * 128, :], [],
                    [("wbout", kc)], nobar=True)

        sst = {}

        def emit_setup_loads():
            sst["xa"] = [AF_.alloc(1024), AF_.alloc(1024)]
            sst["gbc"] = AF_.alloc(1024)
            sst["hb"] = [AH.alloc(1024), AH.alloc(1024)]
            sst["sq"] = AH.alloc(1024)
            sst["wm"] = AH.alloc(8192).rearrange("p (a b) -> p a b", b=1024)
            sst["memnT"] = AH.alloc(2048).rearrange("p (a b) -> p a b", b=256)
            sst["lnb_bc"] = AH.alloc(512)
            sst["sgw"] = AH.alloc(1024).rearrange("p (a b) -> p a b", b=128)
            sst["sgbb"] = AF_.alloc(512).rearrange("p (a b) -> p a b", b=128)
            DMA("pool", sst["gbc"], mem_norm_g.partition_broadcast(128)[:, 0, :], [], ["gbc_s"])
            sst["wm_loads"] = lambda: [DMA("pool", sst["wm"][:, kc, :], w_mkv_v[:, kc, :], [("hT", 16 + 2 * kc)],
                                           [("wm", kc)]) for kc in range(8)]
            DMA("pool", sst["sgw"], sg_w.rearrange("g p q -> p g q"), [], ["sgw"])
            DMA("pool", sst["lnb_bc"], sg_ln_b.partition_broadcast(128)[:, 0, :], [], ["lnb"])
            for j in range(4):
                DMA("pool", lng[:, j:j + 1], sg_ln_g[0:1, j * 128:(j + 1) * 128].rearrange("a p -> p a"),
                    [], ["lng"])
                for hh in range(2):
                    g = 2 * j + hh
                    DMA("pool", sst["sgbb"][hh * 64:(hh + 1) * 64, j, :],
                        sg_b[g:g + 1, :].partition_broadcast(64)[:, 0, :], [], ["sgbb"])

        def emit_setup():
            xa, gbc, hb, sq = sst["xa"], sst["gbc"], sst["hb"], sst["sq"]
            wm, memnT, lnb_bc, sgw, sgbb = (sst["wm"], sst["memnT"], sst["lnb_bc"], sst["sgw"],
                                            sst["sgbb"])
            norm_transpose_pipe(2, lambda t: mem[t * 128:(t + 1) * 128, :], gbc, "gbc_s", xa, hb, sq,
                                lambda t: (memnT[:, :, t * 128:(t + 1) * 128], ("memnT", t)), tag="s_")
            wmk = [("wm", kc) for kc in range(8)]
            mnk = [("memnT", 0), ("memnT", 1)]
            for h in range(4):
                k = 2 + h % 2
                for kc in range(8):
                    MM(bank(k)[:, 0:256], wm[:, kc, h * 128:(h + 1) * 128], memnT[:, kc, :],
                       kc == 0, kc == 7, wmk + mnk, [PB(k)])
                ACT(kmT[:, h, :], bank(k)[:, 0:256], AF.Copy, [PB(k)], ["kmT"])
            for mt in range(2):
                k = 4 + mt
                for kc in range(8):
                    MM(bank(k), memnT[:, kc, mt * 128:(mt + 1) * 128], wm[:, kc, 512:1024],
                       kc == 0, kc == 7, wmk + mnk, [PB(k)])
                ACT(vm[:, mt, :], bank(k), AF.Copy, [PB(k)], ["vm"])
            pv6 = bank_bf(6).rearrange("p (a b) -> p a b", b=128)
            for g in range(8):
                TR(pv6[:, g, :], sgw[:, g, :], ["sgw"], [PB(6)])
            CP("dve", WsT[:], pv6, [PB(6)], ["WsT"])
            for j in range(4):
                for hh in range(2):
                    g = 2 * j + hh
                    MM(bank(7)[hh * 64:(hh + 1) * 64, j * 128:(j + 1) * 128],
                       lnb_bc[:, g * 64:(g + 1) * 64], WsT[:, g, :], True, True,
                       ["lnb", "WsT"], [PB(7)])
            TT("dve", Bp[:], bank(7).rearrange("p (a b) -> p a b", b=128), sgbb, ALU.add,
               [PB(7), "sgbb"], ["Bp"])


        AH.reset()
        hT_halo = AH.alloc(16384).rearrange("p (a b) -> p a b", b=2048)
        wsets = []
        for i in range(2):
            ws = dict(
                wq=AH.alloc(1024).rearrange("p (a b) -> p a b", b=128),
                wk=AH.alloc(1024).rearrange("p (a b) -> p a b", b=128),
                wv=AH.alloc(1024).rearrange("p (a b) -> p a b", b=128),
                wg=AH.alloc(1024).rearrange("p (a b) -> p a b", b=128),
                mbf=AH.alloc(1536), i=i)
            ws["mb"] = ws["mbf"].rearrange("p (c h w) -> p c h w", c=3, h=2)
            wsets.append(ws)
        AB_H0 = AH.off

        def load_wset(ws, hp):
            c0 = hp * 128
            i = ws["i"]
            DMA("sp", ws["wk"], w_in_v[:, :, 512 + c0:512 + c0 + 128], [("wbin_hp", hp, 1)], [("wk", i)])
            DMA("sp", ws["wv"], w_in_v[:, :, 1024 + c0:1024 + c0 + 128], [("wbin_hp", hp, 2)], [("wv", i)])
            DMA("sp", ws["wq"], w_in_v[:, :, c0:c0 + 128], [("wbin_hp", hp, 0)], [("wq", i)])
            DMA("sp", ws["wg"], w_in_v[:, :, 1536 + c0:1536 + c0 + 128], [("wbin_hp", hp, 3)], [("wg", i)])
            DMA("sp", ws["mbf"], wb_bias[:, hp, :], [("wbb", hp)], [("mb", i)])

        def hT_blk(b):
            if b in (0, 1):
                v = hT_halo[:, :, b * 512:(b + 1) * 512]
            elif b in (6, 7):
                v = hT_halo[:, :, 1024 + (b - 6) * 512:1024 + (b - 5) * 512]
            else:
                v = hT_own[:, :, (b - 2) * 512:(b - 1) * 512]
            return v, [("hT", 4 * b + i) for i in range(4)]

        ALLT = []
        for ci, d in enumerate(CFGS):
            A = HALO // d
            nq = SBT // (128 * d)
            for r in range(d):
                for j in range(nq + 1):
                    ALLT.append((ci, d, r, j, nq, A))
        NT = len(ALLT)

        for s in range(NSB):
            AH.reset(AB_H0); AF_.reset()
            hb = [AH.alloc(1024) for _ in range(4)]
            sq = AH.alloc(1024)
            xa = [AF_.alloc(1024) for _ in range(4)]
            gbc = AF_.alloc(1024)
            DMA("sp", gbc, norm_g.partition_broadcast(128)[:, 0, :], [], ["gbc"])
            if s == 0:
                emit_setup_loads()

            if s == 0:
                pass

            def dstA(t):
                v, _ = hT_blk(t // 4)
                o = (t % 4) * 128
                return v[:, :, o:o + 128], ("hT", t)
            tiles_a = None if s == 0 else list(range(8, 32))
            norm_transpose_pipe(32, lambda t: xs[s * SBT + t * 128:s * SBT + (t + 1) * 128, :],
                                gbc, "gbc", xa, hb, sq, dstA, tiles=tiles_a)
            if s == 0:
                sst["wm_loads"]()
                emit_setup()
            load_wset(wsets[0], 0)
            S.barrier()

            AH.reset(AB_H0); AF_.reset()
            Q2 = AF_.alloc(2048).bitcast(BF16).rearrange("p (h t) -> p h t", h=2)
            QA = Q2[:, 0, :]
            QB = Q2[:, 1, :]
            KT = AH.alloc(4096)
            VT = AH.alloc(4096)
            Vt_flat = AH.alloc(NT * 256)
            Vt = Vt_flat.rearrange("p (a b) -> p a b", b=256)
            PT = [AH.alloc(512).rearrange("p (a b) -> p a b", b=256) for _ in range(3)]
            GT = AF_.alloc(2048)
            acc = AF_.alloc(4096).rearrange("p (a b) -> p a b", b=2048)
            rl = AF_.alloc(2048)
            MSET("pool", Vt[:, :, 64:192], 1.0, ["Vt1"])
            MSET("pool", QA[64:128, :], 0.0, ["QTz0"])
            MSET("pool", QB[0:64, :], 0.0, ["QTz1"])
            KTK = [("KT", b_) for b_ in range(8)]
            VTK = [("VT", b_) for b_ in range(8)]
            QTK = [("QT", b_, h_) for b_ in range(4) for h_ in range(2)] + ["QTz0", "QTz1"]
            GTK = [("GT", b_) for b_ in range(4)]
            if s == 0:
                for hp_ in (1, 2, 3):
                    convert_hp(hp_)
                convert_rest()
            ACCK = [("acc", i) for i in range(16)]

            for hp in range(4):
                ws = wsets[hp % 2]
                wi = ws["i"]
                if hp + 1 < 4:
                    load_wset(wsets[(hp + 1) % 2], hp + 1)
                pk = 0
                if s >= 1:
                    DMA("sp", KT[:, 0:2048], kv_scr[hp, 0], [("kvs", hp, 0)], [("KT", b_) for b_ in range(4)])
                    DMA("sp", VT[:, 0:2048], kv_scr[hp, 1], [("kvs", hp, 1)], [("VT", b_) for b_ in range(4)])
                for b in list(range(8)) + [12, 13, 14, 15]:
                    if b < 8:
                        hv, hk = hT_blk(b)
                        jobs = [(("wk", wi), ws["wk"], KT[:, b * 512:(b + 1) * 512], ("KT", b), AF.Copy, 1.0),
                                (("wv", wi), ws["wv"], VT[:, b * 512:(b + 1) * 512], ("VT", b), AF.Copy, 1.0)]
                        if s >= 1 and b < 4:
                            jobs = []
                        if 2 <= b <= 5:
                            o = (b - 2) * 512
                            jobs += [(("wq", wi), ws["wq"], None, ("QT", b - 2), AF.Copy, 0.125)]
                    else:
                        hv, hk = hT_blk(b - 10)
                        o = (b - 12) * 512
                        jobs = [(("wg", wi), ws["wg"], GT[:, o:o + 512], ("GT", b - 12), AF.Silu, 1.0)]
                    for (wkey, wt, dst, dkey, fn, sc) in jobs:
                        k = pk % 8
                        pk += 1
                        for kc in range(8):
                            MM(bank(k), wt[:, kc, :], hv[:, kc, :], kc == 0, kc == 7,
                               [wkey] + hk, [PB(k)])
                        if dst is None:
                            ACT(QA[0:64, o:o + 512], bank(k)[0:64, :], fn, [PB(k)], [dkey + (0,)], scale=sc)
                            ACT(QB[64:128, o:o + 512], bank(k)[64:128, :], fn, [PB(k)], [dkey + (1,)], scale=sc)
                        else:
                            ACT(dst, bank(k), fn, [PB(k)], [dkey], scale=sc)
                if s + 1 < NSB:
                    DMA("sp", kv_scr[hp, 0], KT[:, 2048:4096], [("KT", b_) for b_ in range(4, 8)],
                        [("kvs", hp, 0)])
                    DMA("sp", kv_scr[hp, 1], VT[:, 2048:4096], [("VT", b_) for b_ in range(4, 8)],
                        [("kvs", hp, 1)])
                for g0 in range(0, NT, 8):
                    grp = ALLT[g0:g0 + 8]
                    k = (g0 // 8) % 8
                    pvv = bank_bf(k).rearrange("p (a b) -> p a b", b=128)
                    for i, (ci, d, r, j, nq, A) in enumerate(grp):
                        u0 = r + d * (A + 128 * j - 64)
                        TR(pvv[:, i, :], VT[:, u0:u0 + d * 127 + 1:d], VTK, [PB(k)])
                    n = len(grp)
                    CP("dve", Vt[:, g0:g0 + n, 0:64], pvv[:, 0:n, 0:64], [PB(k)], [("Vt", g0, 0)])
                    CP("dve", Vt[:, g0:g0 + n, 192:256], pvv[:, 0:n, 64:128], [PB(k)], [("Vt", g0, 1)])

                def geom(ti):
                    ci, d, r, j, nq, A = ALLT[ti]
                    halves = []
                    if j >= 1:
                        halves.append((0, j - 1))
                    if j <= nq - 1:
                        halves.append((1, j))
                    return ci, d, r, j, nq, A, halves

                def stage1(ti):
                    ci, d, r, j, nq, A, halves = geom(ti)
                    u0 = r + d * (A + 128 * j - 64)
                    W = 128 * len(halves)
                    mcol0 = 128 * halves[0][0]
                    q0 = r + d * 128 * halves[0][1]
                    sk = ti % 4
                    so = pball[:, sk, 0:2 * W]
                    MM(so, ident[:], ws["mb"][:, ci, :, mcol0:mcol0 + W],
                       True, False, ["ident", ("mb", wi)], [PB(sk)])
                    MM(so, KT[:, u0:u0 + d * 127 + 1:d], Q2[:, :, q0:q0 + d * (W - 1) + 1:d],
                       False, True, KTK + QTK, [PB(sk)])

                def stage2(ti):
                    ci, d, r, j, nq, A, halves = geom(ti)
                    W = 128 * len(halves)
                    sk = ti % 4
                    if j == 0:
                        bias = emk[:, 2 * s:2 * s + 1]
                    elif j == nq:
                        bias = emk[:, 2 * s + 1:2 * s + 2]
                    else:
                        bias = 0.0
                    ACT(PT[ti % 3][:, :, 0:W],
                        pball[:, sk, 0:2 * W].rearrange("p (h w) -> p h w", h=2), AF.Exp,
                        [PB(sk), "emk"], [("PT", ti % 3)], bias=bias)

                st3 = dict(ctr=0, prev=None, cur=None)

                def stage3(ti):
                    ci, d, r, j, nq, A, halves = geom(ti)
                    pt = PT[ti % 3]
                    for hi_, (hf, cj) in enumerate(halves):
                        if hf == 1:
                            slot = st3["ctr"] % 2
                            st3["ctr"] += 1
                            st3["cur"] = slot
                            opening = True
                        else:
                            slot = st3["prev"]
                            opening = False
                        ko = 4 + 2 * slot
                        cs = hi_ * 128
                        vk = ["Vt1", ("Vt", (ti // 8) * 8, 0), ("Vt", (ti // 8) * 8, 1)]
                        MM(pball[:, ko, 0:128], Vt[:, ti, 0:128], pt[:, 0, cs:cs + 128],
                           opening, not opening, vk + [("PT", ti % 3)], [PB(ko)])
                        MM(pball[:, ko + 1, 0:128], Vt[:, ti, 128:256], pt[:, 1, cs:cs + 128],
                           opening, not opening, vk + [("PT", ti % 3)], [PB(ko + 1)])
                        if not opening:
                            t0 = r + d * 128 * cj
                            dst = acc[:, :, t0:t0 + d * 127 + 1:d]
                            src = pball[:, ko:ko + 2, 0:128]
                            blk0 = (d * 128 * cj) // 128
                            aks = ACCK[blk0:blk0 + d]
                            if ci == 0:
                                CP("dve", dst, src, [PB(ko), PB(ko + 1)], aks)
                            else:
                                TT("dve", dst, src, dst, ALU.add, [PB(ko), PB(ko + 1)] + aks, aks)
                    if j <= nq - 1:
                        st3["prev"] = st3["cur"]

                for i in range(NT + 2):
                    if i < NT:
                        stage1(i)
                    if 0 <= i - 1 < NT:
                        stage2(i - 1)
                    if 0 <= i - 2 < NT:
                        stage3(i - 2)
                if hp == 3:
                    S.barrier()
                TT("pool", acc[0:64, 0, :], acc[0:64, 0, :], GT[0:64, :], ALU.mult,
                   ACCK + ["FT"] + GTK, ["accgA"])
                TT("pool", acc[64:128, 1, :], acc[64:128, 1, :], GT[64:128, :], ALU.mult,
                   ACCK + ["FT"] + GTK, ["accgB"])
                RCP(rl[0:64, :], acc[64:128, 0, :], ACCK + ["FT"], ["rlA"])
                RCP(rl[64:128, :], acc[0:64, 1, :], ACCK + ["FT"], ["rlB"])
                TT("pool", attT[0:64, hp, :], acc[0:64, 0, :], rl[0:64, :], ALU.mult,
                   ACCK + ["accgA", "rlA", "FT"], [("attT", hp)])
                TT("pool", attT[64:128, hp, :], acc[64:128, 1, :], rl[64:128, :], ALU.mult,
                   ACCK + ["accgB", "rlB", "FT"], [("attT", hp)])

            AH.reset(); AF_.reset()
            wC = [AH.alloc(4096).rearrange("p (a b) -> p a b", b=512) for _ in range(5)]
            wo = AH.alloc(12288).rearrange("p (a b) -> p a b", b=1024)
            uT = AH.alloc(2048).rearrange("p (a b) -> p a b", b=512)
            sgT = AH.alloc(2048).rearrange("p (a b) -> p a b", b=512)
            sgmT = AH.alloc(2048).rearrange("p (a b) -> p a b", b=512)
            vvn = [AH.alloc(512) for _ in range(4)]
            ysgT = AH.alloc(2048).rearrange("p (a b) -> p a b", b=512)
            qmT = AH.alloc(2048).rearrange("p (a b) -> p a b", b=512)
            PTm = [AH.alloc(512) for _ in range(4)]
            ymemT = AH.alloc(2048).rearrange("p (a b) -> p a b", b=512)
            sqj = AH.alloc(1024)
            vv = [AF_.alloc(512) for _ in range(4)]
            rlm = [AF_.alloc(512), AF_.alloc(512)]
            xr = [AF_.alloc(1024) for _ in range(4)]
            fgbc = AF_.alloc(1024)
            ms = [AF_.alloc(512).rearrange("p (a b) -> p a b", b=128) for _ in range(2)]
            usg = AH.alloc(2048).rearrange("p (a b) -> p a b", b=512)

            cgrp = [2048, 2560, 3072, 3584, 4096]
            for gi in (1, 0, 2, 4, 3):
                cc = cgrp[gi]
                for k0 in (0, 4):
                    DMA("sp", wC[gi][:, k0:k0 + 4, :], w_in_v[:, k0:k0 + 4, cc:cc + 512], WBIN2,
                        [("wC", gi, kc) for kc in range(k0, k0 + 4)])
            for k0 in (0, 4, 8):
                DMA("sp", wo[:, k0:k0 + 4, :], w_out_v[:, k0:k0 + 4, :], WBOUT,
                    [("wo", kc) for kc in range(k0, k0 + 4)])
            DMA("sp", fgbc, final_g.partition_broadcast(128)[:, 0, :], [], ["fgbc", "FT"])
            w_u, w_v2, w_gs, w_qm, w_gm = wC
            free_banks = list(range(8))

            def balloc():
                assert free_banks, "PSUM banks exhausted"
                return free_banks.pop(0)

            def bfree(k):
                free_banks.append(k)

            def balloc_pair():
                best = None
                for a_ in free_banks:
                    if a_ < 7 and (a_ + 1) in free_banks:
                        sc = max(free_banks.index(a_), free_banks.index(a_ + 1))
                        if best is None or sc < best[0]:
                            best = (sc, a_)
                assert best is not None, "no adjacent PSUM bank pair free"
                a_ = best[1]
                free_banks.remove(a_)
                free_banks.remove(a_ + 1)
                return a_

            bst = [dict() for _ in range(4)]

            def groups_P(bb):
                hv = hT_own[:, :, bb * 512:(bb + 1) * 512]
                hk = [("hT", 8 + 4 * bb + i) for i in range(4)]
                stt_ = [dict() for _ in range(4)]
                bst[bb]["stt"] = stt_
                gl = []

                def vv_group(tt):
                    d_ = stt_[tt]
                    k = balloc()
                    for kc in range(8):
                        MM(bank(k), hv[:, kc, tt * 128:(tt + 1) * 128], w_v2[:, kc, :],
                           kc == 0, kc == 7, [("wC", 1, kc)] + hk, [PB(k)])
                    d_["sm"], d_["ksm"] = scol()
                    ACT(vv[tt], bank(k), AF.Gelu_apprx_tanh, [PB(k)],
                        [("vv", tt), d_["ksm"]], accum_out=d_["sm"])
                    bfree(k)

                def fm_group(wt, gi, dst, dkey, fn, ft):
                    k = balloc()
                    for kc in range(8):
                        MM(bank(k), wt[:, kc, ft * 128:(ft + 1) * 128], hv[:, kc, :],
                           kc == 0, kc == 7, [("wC", gi, kc)] + hk, [PB(k)])
                    ACT(dst[:, ft, :], bank(k), fn, [PB(k)], [(dkey, ft)])
                    bfree(k)

                for tt in range(4):
                    gl.append(lambda tt=tt: vv_group(tt))
                for (wt, gi, dst, dkey, fn) in ((w_u, 0, uT, "uT", AF.Gelu_apprx_tanh),
                                                (w_gs, 2, sgT, "sgT", AF.Silu),
                                                (w_qm, 3, qmT, "qmT", AF.Copy),
                                                (w_gm, 4, sgmT, "sgmT", AF.Silu)):
                    for ft in range(4):
                        gl.append(lambda wt=wt, gi=gi, dst=dst, dkey=dkey, fn=fn, ft=ft:
                                  fm_group(wt, gi, dst, dkey, fn, ft))
                return gl

            filler = []

            def fill(n):
                for _ in range(n):
                    if filler:
                        filler.pop(0)()

            def stage_M(bb):
                ft_ = ["FT"] if bb == 0 else []
                stt_ = bst[bb]["stt"]
                for tt in range(4):
                    d_ = stt_[tt]
                    d_["sq2"], d_["ksq"] = scol()
                    ACT(sqj[:, 0:512], vv[tt], AF.Square, [("vv", tt)], ["sqj", d_["ksq"]],
                        accum_out=d_["sq2"])
                fen, kfen = scol()
                ACT(fen, stt_[3]["sq2"], AF.Copy, [stt_[i_]["ksq"] for i_ in range(4)], [kfen])
                for tt in range(4):
                    d_ = stt_[tt]
                    d_["mean"], d_["kmean"] = scol()
                    TS("dve", d_["mean"], d_["sm"], 1.0 / 512, None, ALU.mult, None,
                       [d_["ksm"], kfen], [d_["kmean"]])
                for tt in range(4):
                    d_ = stt_[tt]
                    d_["m2"], d_["km2"] = scol()
                    TT("dve", d_["m2"], d_["mean"], d_["mean"], ALU.mult, [d_["kmean"]], [d_["km2"]])
                for tt in range(4):
                    d_ = stt_[tt]
                    d_["var"], d_["kvar"] = scol()
                    STT("dve", d_["var"], d_["sq2"], 1.0 / 512, d_["m2"], ALU.mult, ALU.subtract,
                        [d_["ksq"], d_["km2"], kfen], [d_["kvar"]])
                for tt in range(4):
                    d_ = stt_[tt]
                    d_["sd"], d_["ksd"] = scol()
                    ACT(d_["sd"], d_["var"], AF.Sqrt, [d_["kvar"]], [d_["ksd"]], bias=EPS, scale=1.0)
                for tt in range(4):
                    d_ = stt_[tt]
                    d_["rs"], d_["krs"] = scol()
                    RCP(d_["rs"], d_["sd"], [d_["ksd"]], [d_["krs"]])
                for tt in range(4):
                    d_ = stt_[tt]
                    d_["nb"], d_["knb"] = scol()
                    STT("dve", d_["nb"], d_["mean"], -1.0, d_["rs"], ALU.mult, ALU.mult,
                        [d_["kmean"], d_["krs"]], [d_["knb"]])
                for tt in range(4):
                    d_ = stt_[tt]
                    ACT(vvn[tt], vv[tt], AF.Identity, [("vv", tt), d_["krs"], d_["knb"]],
                        [("vvn", tt)], bias=d_["nb"], scale=d_["rs"])
                TT("pool", usg, uT, sgT, ALU.mult,
                   [("uT", f_) for f_ in range(4)] + [("sgT", f_) for f_ in range(4)], ["usg"])
                fill(8)
                mixb = []
                for tt in range(4):
                    km_ = balloc()
                    mixb.append(km_)
                    for j in range(4):
                        for hh in range(2):
                            g = 2 * j + hh
                            MM(bank(km_)[hh * 64:(hh + 1) * 64, j * 128:(j + 1) * 128],
                               vvn[tt][:, g * 64:(g + 1) * 64], WsT[:, g, :], True, True,
                               [("vvn", tt), "WsT"], [PB(km_)])
                fill(4)
                for tt in range(4):
                    i = tt % 2
                    bmv = bank(mixb[tt]).rearrange("p (a b) -> p a b", b=128)
                    for j in range(4):
                        STT("dve", ms[i][:, j, :], bmv[:, j, :], lng[:, j:j + 1], Bp[:, j, :],
                            ALU.mult, ALU.add, [PB(mixb[tt]), "lng", "Bp"], [("ms", i, j)] + ft_)
                    bfree(mixb[tt])
                    TT("pool", ysgT[:, :, tt * 128:(tt + 1) * 128], ms[i],
                       usg[:, :, tt * 128:(tt + 1) * 128], ALU.mult,
                       [("ms", i, j_) for j_ in range(4)] + ["usg"], [("ysgT", tt)])
                sbk = {}

                def scores(h):
                    for mt in range(2):
                        k = balloc()
                        sbk[(h, mt)] = k
                        MM(bank(k), kmT[:, h, mt * 128:(mt + 1) * 128], qmT[:, h, :], True, True,
                           ["kmT", ("qmT", h)], [PB(k)])

                scores(0)
                scores(1)
                for h in range(4):
                    if h + 2 < 4:
                        scores(h + 2)
                    for mt in range(2):
                        pi = (2 * h + mt) % 4
                        ACT(PTm[pi], bank(sbk[(h, mt)]), AF.Exp, [PB(sbk[(h, mt)])], [("PTm", pi)],
                            scale=1.0 / math.sqrt(128.0))
                        bfree(sbk[(h, mt)])
                    kO = balloc()
                    kL = balloc()
                    for mt in range(2):
                        pi = (2 * h + mt) % 4
                        MM(bank(kL), ones[:], PTm[pi], mt == 0, mt == 1,
                           ["ones", ("PTm", pi)], [PB(kL)])
                    for mt in range(2):
                        pi = (2 * h + mt) % 4
                        MM(bank(kO), vm[:, mt, h * 128:(h + 1) * 128], PTm[pi], mt == 0, mt == 1,
                           ["vm", ("PTm", pi)], [PB(kO)])
                    i = h % 2
                    RCP(rlm[i], bank(kL), [PB(kL)], [("rlm", i)] + ft_)
                    bfree(kL)
                    TT("pool", rlm[i], rlm[i], sgmT[:, h, :], ALU.mult, [("rlm", i), ("sgmT", h)],
                       [("rlm", i)])
                    TT("dve", ymemT[:, h, :], bank(kO), rlm[i], ALU.mult, [PB(kO), ("rlm", i)],
                       [("ymemT", h)])
                    bfree(kO)
                    if h == 1:
                        fill(4)
                    if h == 3:
                        fill(4)

            def stage_O(bb):
                ft_ = ["FT"] if bb == 0 else []
                for tt in range(4):
                    own0 = bb * 512 + tt * 128
                    r0 = s * SBT + HALO + own0
                    DMA("sp", xr[tt], xs[r0:r0 + 128, :], [], [("xr", tt)] + ft_)
                for tt in range(4):
                    own0 = bb * 512 + tt * 128
                    k0_ = balloc_pair()
                    ky = [k0_, k0_ + 1]
                    for half in range(2):
                        for kc in range(12):
                            if kc < 4:
                                lt, lk = attT[:, kc, own0:own0 + 128], ("attT", kc)
                            elif kc < 8:
                                lt, lk = ysgT[:, kc - 4, tt * 128:(tt + 1) * 128], ("ysgT", tt)
                            else:
                                lt, lk = ymemT[:, kc - 8, tt * 128:(tt + 1) * 128], ("ymemT", kc - 8)
                            MM(bank(ky[half]), lt, wo[:, kc, half * 512:(half + 1) * 512],
                               kc == 0, kc == 11, [lk, ("wo", kc)], [PB(ky[half])])
                    xv = xr[tt].rearrange("p (a b) -> p a b", b=512)
                    TT("dve", xv, pball[:, ky[0]:ky[0] + 2, :], xv, ALU.add,
                       [PB(ky[0]), PB(ky[1]), ("xr", tt)], [("xr", tt)])
                    bfree(ky[0])
                    bfree(ky[1])
                rss = []
                for tt in range(4):
                    ss, kss = scol()
                    ACT(sqj, xr[tt], AF.Square, [("xr", tt)], ["sqj", kss], accum_out=ss)
                    rss.append([ss, kss])
                for tt in range(4):
                    sd, ksd = scol()
                    ACT(sd, rss[tt][0], AF.Sqrt, [rss[tt][1]], [ksd], bias=EPS, scale=1.0 / DM)
                    rss[tt] += [sd, ksd]
                for tt in range(4):
                    rs, krs = scol()
                    RCP(rs, rss[tt][2], [rss[tt][3]], [krs])
                    rss[tt] += [rs, krs]
                for tt in range(4):
                    rs, krs = rss[tt][4], rss[tt][5]
                    STT("dve", xr[tt], xr[tt], rs, fgbc, ALU.mult, ALU.mult,
                        [("xr", tt), krs, "fgbc"], [("xr", tt)])
                    o0 = s * SBT + bb * 512 + tt * 128
                    DMA("sp", out[o0:o0 + 128, :], xr[tt], [("xr", tt)], [("out", s, bb, tt)])

            filler.extend(groups_P(0))
            fill(20)
            for bb in range(4):
                if bb + 1 < 4:
                    filler.extend(groups_P(bb + 1))
                stage_M(bb)
                fill(20)
                stage_O(bb)
            S.barrier()
        S.emit()
    return nc


def _t5_bucket(rel):
    nb_, md = 32, 1024
    half = nb_ // 2
    max_exact = half // 2
    ret = (rel > 0).astype(np.int32) * half
    n = np.abs(rel)
    large = max_exact + (np.log(np.maximum(n, 1).astype(np.float32) / max_exact)
                         / math.log(md / max_exact) * (half - max_exact)).astype(np.int32)
    large = np.minimum(large, half - 1)
    return (ret + np.where(n < max_exact, n, large)).astype(np.int32)


def _bias_layout(rel_bias):
    rel_bias = np.asarray(rel_bias, np.float32)
    kk = np.arange(128)[:, None]
    qq = np.arange(256)[None, :]
    rel = kk - qq + 64
    band = np.abs(rel) <= 64
    outp = np.empty((128, 4, 3, 2, 256), np.float32)
    for ci, d in enumerate(CFGS):
        bucket = _t5_bucket(rel * d)
        g = rel_bias[bucket]
        for h in range(8):
            m = g[:, :, h].copy()
            m[~band] = NEG
            outp[:, h // 2, ci, h % 2, :] = m
    return np.ascontiguousarray(outp.reshape(128, 4, 1536))


_NC_CACHE = {}


def _run(inputs, NSB):
    x = np.asarray(inputs["x"], np.float32)
    B, S, _ = x.shape
    per_core = NSB * SBT
    cps = S // per_core
    n_cores = B * cps
    if NSB not in _NC_CACHE:
        _NC_CACHE[NSB] = build_nc(NSB)
    nc = _NC_CACHE[NSB]
    biasM = _bias_layout(inputs["rel_bias"])
    ident = np.eye(128, dtype=np.float32)
    f = lambda k: np.ascontiguousarray(np.asarray(inputs[k], np.float32))
    w_in = f("w_in")[0]
    w_mkv = f("w_mem_kv")[0]
    w_out = f("w_out")[0]
    sg_w = f("sg_w")[0]
    sg_b = f("sg_b")[0]
    in_maps = []
    for c in range(n_cores):
        b, part = divmod(c, cps)
        t0 = part * per_core
        slab = np.zeros((per_core + 2 * HALO, DM), np.float32)
        lo, hi = t0 - HALO, t0 + per_core + HALO
        slo, shi = max(lo, 0), min(hi, S)
        slab[slo - lo:shi - lo] = x[b, slo:shi]
        em = np.zeros((128, 2 * NSB), np.float32)
        for s in range(NSB):
            g0 = t0 + s * SBT
            if g0 == 0:
                em[0:64, 2 * s] = NEG
            if g0 + SBT == S:
                em[64:128, 2 * s + 1] = NEG
        in_maps.append({
            "xs": slab, "mem": f("mem")[b], "w_in": w_in, "w_mem_kv": w_mkv, "w_out": w_out,
            "norm_g": f("norm_g").reshape(1, DM), "mem_norm_g": f("mem_norm_g").reshape(1, DM),
            "final_norm_g": f("final_norm_g").reshape(1, DM),
            "sg_ln_g": f("sg_ln_g").reshape(1, 512), "sg_ln_b": f("sg_ln_b").reshape(1, 512),
            "sg_w": sg_w, "sg_b": sg_b, "biasM": biasM, "emask": em, "ident": ident,
        })
    res = run_bass_kernel_spmd(nc, in_maps, core_ids=list(range(n_cores)))
    outp = np.empty((B, S, DM), np.float32)
    for c in range(n_cores):
        b, part = divmod(c, cps)
        t0 = part * per_core
        outp[b, t0:t0 + per_core] = res.results[c]["out"]
    return outp


def kernel(**inputs):
    return _run(inputs, 2)
```

```python
import contextlib
import math
import numpy as np
import ml_dtypes
import concourse.bass as bass
import concourse.mybir as mybir
from concourse.bass_utils import run_bass_kernel_spmd

F32 = mybir.dt.float32
BF16 = mybir.dt.bfloat16
AF = mybir.ActivationFunctionType
ALU = mybir.AluOpType

ENGS = ("pe", "act", "dve", "pool", "sp")
NDMA = 12
DM = 1024
SBT = 2048
HALO = 1024
EPS = 1e-6
NEG = -30000.0
CFGS = (1, 4, 16)


class Sched:
    def __init__(self, nc):
        self.nc = nc
        self.ops = {e: [] for e in ENGS}
        self.last_writer = {}
        self.readers = {}

    def _add(self, eng, fn, reads, writes, dma, extra=None, nobar=False):
        idx = len(self.ops[eng])
        me = (eng, idx)
        deps = set()
        raw = set()
        for k in reads:
            w = self.last_writer.get(k)
            if w is not None:
                deps.add(w)
                raw.add(w)
        for k in writes:
            w = self.last_writer.get(k)
            if w is not None:
                deps.add(w)
            for r in self.readers.get(k, ()):
                deps.add(r)
        deps.discard(me)
        if not dma:
            if eng == "pe":
                deps = {d for d in deps if d[0] != "pe" or self.ops["pe"][d[1]]["dma"]}
        if extra:
            deps |= set(extra)
        self.ops[eng].append(dict(fn=fn, deps=deps, dma=dma, signal=False, nobar=nobar))
        for k in reads:
            self.readers.setdefault(k, []).append(me)
        for k in writes:
            self.last_writer[k] = me
            self.readers[k] = []
        return me

    def op(self, eng, fn, reads=(), writes=()):
        return self._add(eng, fn, tuple(reads), tuple(writes), False)

    def dma(self, q, fn, reads=(), writes=(), nobar=False):
        return self._add(q, fn, tuple(reads), tuple(writes), True, nobar=nobar)

    def barrier(self):
        deps = set()
        for e in ENGS:
            ndma = 0
            seen_c = False
            for i in range(len(self.ops[e]) - 1, -1, -1):
                o = self.ops[e][i]
                if o["dma"]:
                    if ndma < NDMA:
                        if not o["nobar"]:
                            deps.add((e, i))
                        ndma += 1
                elif o["fn"] is not None and not seen_c:
                    deps.add((e, i))
                    seen_c = True
                if seen_c and ndma >= NDMA:
                    break
        for e in ENGS:
            self._add(e, None, (), (), False, extra={d for d in deps})

    def emit(self):
        nc = self.nc
        for e in ENGS:
            for o in self.ops[e]:
                for (de, di) in o["deps"]:
                    self.ops[de][di]["signal"] = True
        for e in ENGS:
            c = 0
            nd = 0
            for o in self.ops[e]:
                if o["dma"]:
                    o["dslot"] = nd % NDMA
                    o["dval"] = 16 * (nd // NDMA + 1)
                    nd += 1
                elif o["signal"]:
                    assert o["fn"] is not None
                    c += 1
                    o["cnt"] = c
        with contextlib.ExitStack() as st:
            csem = {e: st.enter_context(nc.semaphore("c_" + e)) for e in ENGS}
            dsem = {e: [st.enter_context(nc.semaphore("d_%s_%d" % (e, i))) for i in range(NDMA)]
                    for e in ("sp", "pool", "act")}
            block = st.enter_context(nc.Block())

            def run(e, eng):
                waited = {}
                for o in self.ops[e]:
                    need = {}
                    for (de, di) in o["deps"]:
                        d = self.ops[de][di]
                        if d["dma"]:
                            sem, val = dsem[de][d["dslot"]], d["dval"]
                        else:
                            sem, val = csem[de], d["cnt"]
                        k = id(sem)
                        if k not in need or need[k][1] < val:
                            need[k] = (sem, val)
                    if o["dma"]:
                        sem = dsem[e][o["dslot"]]
                        if o["dval"] > 16:
                            k = id(sem)
                            if k not in need or need[k][1] < o["dval"] - 16:
                                need[k] = (sem, o["dval"] - 16)
                    for k, (sem, val) in need.items():
                        if waited.get(k, 0) >= val:
                            continue
                        waited[k] = val
                        eng.wait_ge(sem, val)
                    if o["fn"] is None:
                        continue
                    ins = o["fn"](eng)
                    if o["dma"]:
                        ins.then_inc(dsem[e][o["dslot"]], 16)
                    elif o["signal"]:
                        ins.then_inc(csem[e], 1)

            @block.sync
            def _(eng):
                run("sp", eng)

            @block.tensor
            def _(eng):
                run("pe", eng)

            @block.scalar
            def _(eng):
                run("act", eng)

            @block.vector
            def _(eng):
                run("dve", eng)

            @block.gpsimd
            def _(eng):
                run("pool", eng)


class Arena:
    def __init__(self, t, n):
        self.t, self.n, self.off = t, n, 0

    def reset(self, off=0):
        self.off = off

    def alloc(self, cols):
        v = self.t[:, self.off:self.off + cols]
        self.off += cols
        assert self.off <= self.n, (self.off, self.n)
        return v


def build_nc(NSB):
    nc = bass.Bass("TRN2", target_bir_lowering=False)
    NTOK = NSB * SBT

    def din(name, shape):
        return nc.dram_tensor(name, list(shape), F32, kind="ExternalInput").ap()

    xs = din("xs", [NTOK + 2 * HALO, DM])
    mem = din("mem", [256, DM])
    w_in = din("w_in", [DM, 4608])
    w_mkv = din("w_mem_kv", [DM, 1024])
    w_out = din("w_out", [1536, DM])
    norm_g = din("norm_g", [1, DM])
    mem_norm_g = din("mem_norm_g", [1, DM])
    final_g = din("final_norm_g", [1, DM])
    sg_ln_g = din("sg_ln_g", [1, 512])
    sg_ln_b = din("sg_ln_b", [1, 512])
    sg_w = din("sg_w", [8, 128, 128])
    sg_b = din("sg_b", [8, 128])
    biasM = din("biasM", [128, 4, 1536])
    emask = din("emask", [128, 2 * NSB])
    ident_d = din("ident", [128, 128])
    out = nc.dram_tensor("out", [NTOK, DM], F32, kind="ExternalOutput").ap()

    wb_in = nc.dram_tensor("wb_in", [DM, 4608], BF16, kind="Internal").ap()
    wb_out = nc.dram_tensor("wb_out", [1536, DM], BF16, kind="Internal").ap()
    wb_bias = nc.dram_tensor("wb_bias", [128, 4, 1536], BF16, kind="Internal").ap()
    kv_scr = nc.dram_tensor("kv_scr", [4, 2, 128, 2048], BF16, kind="Internal").ap()
    w_in_v = wb_in.rearrange("(kc p) c -> p kc c", p=128)
    w_mkv_v = w_mkv.rearrange("(kc p) c -> p kc c", p=128)
    w_out_v = wb_out.rearrange("(kc p) c -> p kc c", p=128)

    with contextlib.ExitStack() as st:
        def sb(name, shape, dt):
            return st.enter_context(nc.sbuf_tensor("s_" + name, list(shape), dt))

        HCOLS = 55296
        FCOLS = 10240
        hT_own = sb("hT_own", [128, 8, SBT], BF16)
        attT = sb("attT", [128, 4, SBT], BF16)
        ident = sb("ident", [128, 128], BF16)
        ones = sb("ones", [128, 128], BF16)
        WsT = sb("WsT", [128, 8, 128], BF16)
        Bp = sb("Bp", [128, 4, 128], F32)
        lng = sb("lng", [128, 4], F32)
        kmT = sb("kmT", [128, 4, 256], BF16)
        vm = sb("vm", [128, 2, 512], BF16)
        emk = sb("emk", [128, 2 * NSB], F32)
        small = sb("small", [128, 64], F32)
        bigH_t = sb("bigH", [128, HCOLS], BF16)
        bigF_t = sb("bigF", [128, FCOLS], F32)
        pball = st.enter_context(nc.psum_tensor("pball", [128, 8, 512], F32))
        AH = Arena(bigH_t, HCOLS)
        AF_ = Arena(bigF_t, FCOLS)

        S = Sched(nc)
        PB = lambda k: ("pb", k)

        def bank(k):
            return pball[:, k, :]

        def bank_bf(k):
            return pball[:, k, :].bitcast(BF16)

        def ACT(out_, in_, func, rd, wr, **kw):
            S.op("act", lambda e: e.activation(out=out_, in_=in_, func=func, **kw), rd, wr)

        def MM(out_, lhsT, rhs, start, stop, rd, wr):
            S.op("pe", lambda e: e.matmul(out_, lhsT, rhs, start=start, stop=stop), rd, wr)

        def TR(out_, in_, rd, wr):
            S.op("pe", lambda e: e.transpose(out_, in_, ident[:]), list(rd) + ["ident"], wr)

        def TT(eng, out_, in0, in1, op, rd, wr):
            S.op(eng, lambda e: e.tensor_tensor(out=out_, in0=in0, in1=in1, op=op), rd, wr)

        def STT(eng, out_, in0, scalar, in1, op0, op1, rd, wr):
            S.op(eng, lambda e: e.scalar_tensor_tensor(out=out_, in0=in0, scalar=scalar, in1=in1,
                                                       op0=op0, op1=op1), rd, wr)

        def TS(eng, out_, in0, s1, s2, op0, op1, rd, wr):
            if op1 is None:
                S.op(eng, lambda e: e.tensor_scalar(out=out_, in0=in0, scalar1=s1, scalar2=None,
                                                    op0=op0), rd, wr)
            else:
                S.op(eng, lambda e: e.tensor_scalar(out=out_, in0=in0, scalar1=s1, scalar2=s2,
                                                    op0=op0, op1=op1), rd, wr)

        def CP(eng, out_, in_, rd, wr):
            S.op(eng, lambda e: e.tensor_copy(out=out_, in_=in_), rd, wr)

        def RCP(out_, in_, rd, wr):
            S.op("dve", lambda e: e.reciprocal(out=out_, in_=in_), rd, wr)

        def RCPF(out_, in_, rd, wr):
            S.op("dve", lambda e: e.reciprocal_approx_fast(out=out_, in_=in_), rd, wr)

        def MSET(eng, ap, val, wr):
            S.op(eng, lambda e: e.memset(ap, val), (), wr)

        def DMA(q, out_, in_, rd, wr, nobar=False):
            S.dma(q, lambda e: e.dma_start(out=out_, in_=in_), rd, wr, nobar=nobar)

        small_i = [0]

        def scol():
            i = small_i[0] % 64
            small_i[0] += 1
            return small[:, i:i + 1], ("small", i)

        def rms_rows(x_ap, x_key, sq_ap, sq_key):
            ss, kss = scol()
            ACT(sq_ap, x_ap, AF.Square, [x_key], [sq_key, kss], accum_out=ss)
            sd, ksd = scol()
            ACT(sd, ss, AF.Sqrt, [kss], [ksd], bias=EPS, scale=1.0 / DM)
            rs, krs = scol()
            RCP(rs, sd, [ksd], [krs])
            return rs, krs

        evac_rr = [0]

        def norm_transpose_pipe(n, src_fn, gbc, gkey, xa, hb, sq, dst_fn, tag="", tiles=None):
            pend = {}

            def stage1(t):
                i = t % len(xa)
                DMA("sp", xa[i], src_fn(t), [], [(tag + "xa", i)])
                ss, kss = scol()
                ACT(sq, xa[i], AF.Square, [(tag + "xa", i)], [tag + "sq", kss], accum_out=ss)
                sd, ksd = scol()
                ACT(sd, ss, AF.Sqrt, [kss], [ksd], bias=EPS, scale=1.0 / DM)
                pend[t] = (sd, ksd)

            def stage2(t):
                i = t % len(xa)
                ih = t % len(hb)
                sd, ksd = pend.pop(t)
                rs, krs = scol()
                RCP(rs, sd, [ksd], [krs])
                STT("dve", hb[ih], xa[i], rs, gbc, ALU.mult, ALU.mult,
                    [(tag + "xa", i), krs, gkey], [(tag + "hb", ih)])

            def stage3(t):
                i = t % len(hb)
                k = t % 3
                pv = bank_bf(k).rearrange("p (a b) -> p a b", b=128)
                for kc in range(8):
                    TR(pv[:, kc, :], hb[i][:, kc * 128:(kc + 1) * 128], [(tag + "hb", i)], [PB(k)])

            def stage4(t):
                k = t % 3
                pv = bank_bf(k).rearrange("p (a b) -> p a b", b=128)
                dst, dkey = dst_fn(t)
                eng = "act" if (evac_rr[0] % 2 == 0) else "dve"
                evac_rr[0] += 1
                if eng == "act":
                    ACT(dst, pv, AF.Copy, [PB(k)], [dkey])
                else:
                    CP("dve", dst, pv, [PB(k)], [dkey])

            tl = list(range(n)) if tiles is None else list(tiles)
            n = len(tl)
            for i in range(n + 3):
                if i < n:
                    stage1(tl[i])
                if 0 <= i - 1 < n:
                    stage2(tl[i - 1])
                if 0 <= i - 2 < n:
                    stage3(tl[i - 2])
                if 0 <= i - 3 < n:
                    stage4(tl[i - 3])

        DMA("pool", ident[:], ident_d, [], ["ident"])
        MSET("pool", ones[:], 1.0, ["ones"])
        DMA("sp", emk[:], emask, [], ["emk"])
        WBIN = [("wbin", kc) for kc in range(8)]
        WBOUT = [("wbout", kc) for kc in range(12)]
        WBIN2 = [("wbin2", kc) for kc in range(8)]
        w_in_f = w_in.rearrange("(kc p) c -> p kc c", p=128)

        def convert_hp(hp):
            c0 = hp * 128
            for grp in range(4):
                c = grp * 512 + c0
                DMA("pool", w_in_v[:, :, c:c + 128], w_in_f[:, :, c:c + 128], [], [("wbin_hp", hp, grp)],
                    nobar=True)
            DMA("pool", wb_bias[:, hp, :], biasM[:, hp, :], [], [("wbb", hp)], nobar=True)

        convert_hp(0)

        def convert_rest():
            for kc in range(8):
                DMA("pool", wb_in[kc * 128:(kc + 1) * 128, 2048:4608],
                    w_in[kc * 128:(kc + 1) * 128, 2048:4608], [], [("wbin2", kc)], nobar=True)
            for kc in range(12):
                DMA("pool", wb_out[kc * 128:(kc + 1) * 128, :], w_out[kc * 128:(kc + 1) * 128, :], [],
                    [("wbout", kc)], nobar=True)

        sst = {}

        def emit_setup_loads():
            sst["xa"] = [AF_.alloc(1024), AF_.alloc(1024)]
            sst["gbc"] = AF_.alloc(1024)
            sst["hb"] = [AH.alloc(1024), AH.alloc(1024)]
            sst["sq"] = AH.alloc(1024)
            sst["wm"] = AH.alloc(8192).rearrange("p (a b) -> p a b", b=1024)
            sst["memnT"] = AH.alloc(2048).rearrange("p (a b) -> p a b", b=256)
            sst["lnb_bc"] = AH.alloc(512)
            sst["sgw"] = AH.alloc(1024).rearrange("p (a b) -> p a b", b=128)
            sst["sgbb"] = AF_.alloc(512).rearrange("p (a b) -> p a b", b=128)
            DMA("pool", sst["gbc"], mem_norm_g.partition_broadcast(128)[:, 0, :], [], ["gbc_s"])
            sst["wm_loads"] = lambda: [DMA("pool", sst["wm"][:, kc, :], w_mkv_v[:, kc, :], [("hT", 16 + 2 * kc)],
                                           [("wm", kc)]) for kc in range(8)]
            DMA("pool", sst["sgw"], sg_w.rearrange("g p q -> p g q"), [], ["sgw"])
            DMA("pool", sst["lnb_bc"], sg_ln_b.partition_broadcast(128)[:, 0, :], [], ["lnb"])
            for j in range(4):
                DMA("pool", lng[:, j:j + 1], sg_ln_g[0:1, j * 128:(j + 1) * 128].rearrange("a p -> p a"),
                    [], ["lng"])
                for hh in range(2):
                    g = 2 * j + hh
                    DMA("pool", sst["sgbb"][hh * 64:(hh + 1) * 64, j, :],
                        sg_b[g:g + 1, :].partition_broadcast(64)[:, 0, :], [], ["sgbb"])

        def emit_setup():
            xa, gbc, hb, sq = sst["xa"], sst["gbc"], sst["hb"], sst["sq"]
            wm, memnT, lnb_bc, sgw, sgbb = (sst["wm"], sst["memnT"], sst["lnb_bc"], sst["sgw"],
                                            sst["sgbb"])
            norm_transpose_pipe(2, lambda t: mem[t * 128:(t + 1) * 128, :], gbc, "gbc_s", xa, hb, sq,
                                lambda t: (memnT[:, :, t * 128:(t + 1) * 128], ("memnT", t)), tag="s_")
            wmk = [("wm", kc) for kc in range(8)]
            mnk = [("memnT", 0), ("memnT", 1)]
            for h in range(4):
                k = 2 + h % 2
                for kc in range(8):
                    MM(bank(k)[:, 0:256], wm[:, kc, h * 128:(h + 1) * 128], memnT[:, kc, :],
                       kc == 0, kc == 7, wmk + mnk, [PB(k)])
                ACT(kmT[:, h, :], bank(k)[:, 0:256], AF.Copy, [PB(k)], ["kmT"])
            for mt in range(2):
                k = 4 + mt
                for kc in range(8):
                    MM(bank(k), memnT[:, kc, mt * 128:(mt + 1) * 128], wm[:, kc, 512:1024],
                       kc == 0, kc == 7, wmk + mnk, [PB(k)])
                ACT(vm[:, mt, :], bank(k), AF.Copy, [PB(k)], ["vm"])
            pv6 = bank_bf(6).rearrange("p (a b) -> p a b", b=128)
            for g in range(8):
                TR(pv6[:, g, :], sgw[:, g, :], ["sgw"], [PB(6)])
            CP("dve", WsT[:], pv6, [PB(6)], ["WsT"])
            for j in range(4):
                for hh in range(2):
                    g = 2 * j + hh
                    MM(bank(7)[hh * 64:(hh + 1) * 64, j * 128:(j + 1) * 128],
                       lnb_bc[:, g * 64:(g + 1) * 64], WsT[:, g, :], True, True,
                       ["lnb", "WsT"], [PB(7)])
            TT("dve", Bp[:], bank(7).rearrange("p (a b) -> p a b", b=128), sgbb, ALU.add,
               [PB(7), "sgbb"], ["Bp"])


        AH.reset()
        hT_halo = AH.alloc(16384).rearrange("p (a b) -> p a b", b=2048)
        wsets = []
        for i in range(2):
            ws = dict(
                wq=AH.alloc(1024).rearrange("p (a b) -> p a b", b=128),
                wk=AH.alloc(1024).rearrange("p (a b) -> p a b", b=128),
                wv=AH.alloc(1024).rearrange("p (a b) -> p a b", b=128),
                wg=AH.alloc(1024).rearrange("p (a b) -> p a b", b=128),
                mbf=AH.alloc(1536), i=i)
            ws["mb"] = ws["mbf"].rearrange("p (c h w) -> p c h w", c=3, h=2)
            wsets.append(ws)
        AB_H0 = AH.off

        def load_wset(ws, hp):
            c0 = hp * 128
            i = ws["i"]
            DMA("sp", ws["wk"], w_in_v[:, :, 512 + c0:512 + c0 + 128], [("wbin_hp", hp, 1)], [("wk", i)])
            DMA("sp", ws["wv"], w_in_v[:, :, 1024 + c0:1024 + c0 + 128], [("wbin_hp", hp, 2)], [("wv", i)])
            DMA("sp", ws["wq"], w_in_v[:, :, c0:c0 + 128], [("wbin_hp", hp, 0)], [("wq", i)])
            DMA("sp", ws["wg"], w_in_v[:, :, 1536 + c0:1536 + c0 + 128], [("wbin_hp", hp, 3)], [("wg", i)])
            DMA("sp", ws["mbf"], wb_bias[:, hp, :], [("wbb", hp)], [("mb", i)])

        def hT_blk(b):
            if b in (0, 1):
                v = hT_halo[:, :, b * 512:(b + 1) * 512]
            elif b in (6, 7):
                v = hT_halo[:, :, 1024 + (b - 6) * 512:1024 + (b - 5) * 512]
            else:
                v = hT_own[:, :, (b - 2) * 512:(b - 1) * 512]
            return v, [("hT", 4 * b + i) for i in range(4)]

        ALLT = []
        for ci, d in enumerate(CFGS):
            A = HALO // d
            nq = SBT // (128 * d)
            for r in range(d):
                for j in range(nq + 1):
                    ALLT.append((ci, d, r, j, nq, A))
        NT = len(ALLT)

        for s in range(NSB):
            AH.reset(AB_H0); AF_.reset()
            hb = [AH.alloc(1024) for _ in range(4)]
            sq = AH.alloc(1024)
            xa = [AF_.alloc(1024) for _ in range(4)]
            gbc = AF_.alloc(1024)
            DMA("sp", gbc, norm_g.partition_broadcast(128)[:, 0, :], [], ["gbc"])
            if s == 0:
                emit_setup_loads()

            if s == 0:
                pass

            def dstA(t):
                v, _ = hT_blk(t // 4)
                o = (t % 4) * 128
                return v[:, :, o:o + 128], ("hT", t)
            tiles_a = None if s == 0 else list(range(8, 32))
            norm_transpose_pipe(32, lambda t: xs[s * SBT + t * 128:s * SBT + (t + 1) * 128, :],
                                gbc, "gbc", xa, hb, sq, dstA, tiles=tiles_a)
            if s == 0:
                sst["wm_loads"]()
                emit_setup()
            load_wset(wsets[0], 0)
            S.barrier()

            AH.reset(AB_H0); AF_.reset()
            Q2 = AF_.alloc(2048).bitcast(BF16).rearrange("p (h t) -> p h t", h=2)
            QA = Q2[:, 0, :]
            QB = Q2[:, 1, :]
            KT = AH.alloc(4096)
            VT = AH.alloc(4096)
            Vt_flat = AH.alloc(NT * 256)
            Vt = Vt_flat.rearrange("p (a b) -> p a b", b=256)
            PT = [AH.alloc(512).rearrange("p (a b) -> p a b", b=256) for _ in range(3)]
            GT = AF_.alloc(2048)
            acc = AF_.alloc(4096).rearrange("p (a b) -> p a b", b=2048)
            rl = AF_.alloc(2048)
            MSET("pool", Vt[:, :, 64:192], 1.0, ["Vt1"])
            MSET("pool", QA[64:128, :], 0.0, ["QTz0"])
            MSET("pool", QB[0:64, :], 0.0, ["QTz1"])
            KTK = [("KT", b_) for b_ in range(8)]
            VTK = [("VT", b_) for b_ in range(8)]
            QTK = [("QT", b_, h_) for b_ in range(4) for h_ in range(2)] + ["QTz0", "QTz1"]
            GTK = [("GT", b_) for b_ in range(4)]
            if s == 0:
                for hp_ in (1, 2, 3):
                    convert_hp(hp_)
                convert_rest()
            ACCK = [("acc", i) for i in range(16)]

            for hp in range(4):
                ws = wsets[hp % 2]
                wi = ws["i"]
                if hp + 1 < 4:
                    load_wset(wsets[(hp + 1) % 2], hp + 1)
                pk = 0
                if s >= 1:
                    DMA("sp", KT[:, 0:2048], kv_scr[hp, 0], [("kvs", hp, 0)], [("KT", b_) for b_ in range(4)])
                    DMA("sp", VT[:, 0:2048], kv_scr[hp, 1], [("kvs", hp, 1)], [("VT", b_) for b_ in range(4)])
                for b in list(range(8)) + [12, 13, 14, 15]:
                    if b < 8:
                        hv, hk = hT_blk(b)
                        jobs = [(("wk", wi), ws["wk"], KT[:, b * 512:(b + 1) * 512], ("KT", b), AF.Copy, 1.0),
                                (("wv", wi), ws["wv"], VT[:, b * 512:(b + 1) * 512], ("VT", b), AF.Copy, 1.0)]
                        if s >= 1 and b < 4:
                            jobs = []
                        if 2 <= b <= 5:
                            o = (b - 2) * 512
                            jobs += [(("wq", wi), ws["wq"], None, ("QT", b - 2), AF.Copy, 0.125)]
                    else:
                        hv, hk = hT_blk(b - 10)
                        o = (b - 12) * 512
                        jobs = [(("wg", wi), ws["wg"], GT[:, o:o + 512], ("GT", b - 12), AF.Silu, 1.0)]
                    for (wkey, wt, dst, dkey, fn, sc) in jobs:
                        k = pk % 8
                        pk += 1
                        for kc in range(8):
                            MM(bank(k), wt[:, kc, :], hv[:, kc, :], kc == 0, kc == 7,
                               [wkey] + hk, [PB(k)])
                        if dst is None:
                            ACT(QA[0:64, o:o + 512], bank(k)[0:64, :], fn, [PB(k)], [dkey + (0,)], scale=sc)
                            ACT(QB[64:128, o:o + 512], bank(k)[64:128, :], fn, [PB(k)], [dkey + (1,)], scale=sc)
                        else:
                            ACT(dst, bank(k), fn, [PB(k)], [dkey], scale=sc)
                if s + 1 < NSB:
                    DMA("sp", kv_scr[hp, 0], KT[:, 2048:4096], [("KT", b_) for b_ in range(4, 8)],
                        [("kvs", hp, 0)])
                    DMA("sp", kv_scr[hp, 1], VT[:, 2048:4096], [("VT", b_) for b_ in range(4, 8)],
                        [("kvs", hp, 1)])
                for g0 in range(0, NT, 8):
                    grp = ALLT[g0:g0 + 8]
                    k = (g0 // 8) % 8
                    pvv = bank_bf(k).rearrange("p (a b) -> p a b", b=128)
                    for i, (ci, d, r, j, nq, A) in enumerate(grp):
                        u0 = r + d * (A + 128 * j - 64)
                        TR(pvv[:, i, :], VT[:, u0:u0 + d * 127 + 1:d], VTK, [PB(k)])
                    n = len(grp)
                    CP("dve", Vt[:, g0:g0 + n, 0:64], pvv[:, 0:n, 0:64], [PB(k)], [("Vt", g0, 0)])
                    CP("dve", Vt[:, g0:g0 + n, 192:256], pvv[:, 0:n, 64:128], [PB(k)], [("Vt", g0, 1)])

                def geom(ti):
                    ci, d, r, j, nq, A = ALLT[ti]
                    halves = []
                    if j >= 1:
                        halves.append((0, j - 1))
                    if j <= nq - 1:
                        halves.append((1, j))
                    return ci, d, r, j, nq, A, halves

                def stage1(ti):
                    ci, d, r, j, nq, A, halves = geom(ti)
                    u0 = r + d * (A + 128 * j - 64)
                    W = 128 * len(halves)
                    mcol0 = 128 * halves[0][0]
                    q0 = r + d * 128 * halves[0][1]
                    sk = ti % 4
                    so = pball[:, sk, 0:2 * W]
                    MM(so, ident[:], ws["mb"][:, ci, :, mcol0:mcol0 + W],
                       True, False, ["ident", ("mb", wi)], [PB(sk)])
                    MM(so, KT[:, u0:u0 + d * 127 + 1:d], Q2[:, :, q0:q0 + d * (W - 1) + 1:d],
                       False, True, KTK + QTK, [PB(sk)])

                def stage2(ti):
                    ci, d, r, j, nq, A, halves = geom(ti)
                    W = 128 * len(halves)
                    sk = ti % 4
                    if j == 0:
                        bias = emk[:, 2 * s:2 * s + 1]
                    elif j == nq:
                        bias = emk[:, 2 * s + 1:2 * s + 2]
                    else:
                        bias = 0.0
                    ACT(PT[ti % 3][:, :, 0:W],
                        pball[:, sk, 0:2 * W].rearrange("p (h w) -> p h w", h=2), AF.Exp,
                        [PB(sk), "emk"], [("PT", ti % 3)], bias=bias)

                st3 = dict(ctr=0, prev=None, cur=None)

                def stage3(ti):
                    ci, d, r, j, nq, A, halves = geom(ti)
                    pt = PT[ti % 3]
                    for hi_, (hf, cj) in enumerate(halves):
                        if hf == 1:
                            slot = st3["ctr"] % 2
                            st3["ctr"] += 1
                            st3["cur"] = slot
                            opening = True
                        else:
                            slot = st3["prev"]
                            opening = False
                        ko = 4 + 2 * slot
                        cs = hi_ * 128
                        vk = ["Vt1", ("Vt", (ti // 8) * 8, 0), ("Vt", (ti // 8) * 8, 1)]
                        MM(pball[:, ko, 0:128], Vt[:, ti, 0:128], pt[:, 0, cs:cs + 128],
                           opening, not opening, vk + [("PT", ti % 3)], [PB(ko)])
                        MM(pball[:, ko + 1, 0:128], Vt[:, ti, 128:256], pt[:, 1, cs:cs + 128],
                           opening, not opening, vk + [("PT", ti % 3)], [PB(ko + 1)])
                        if not opening:
                            t0 = r + d * 128 * cj
                            dst = acc[:, :, t0:t0 + d * 127 + 1:d]
                            src = pball[:, ko:ko + 2, 0:128]
                            blk0 = (d * 128 * cj) // 128
                            aks = ACCK[blk0:blk0 + d]
                            if ci == 0:
                                CP("dve", dst, src, [PB(ko), PB(ko + 1)], aks)
                            else:
                                TT("dve", dst, src, dst, ALU.add, [PB(ko), PB(ko + 1)] + aks, aks)
                    if j <= nq - 1:
                        st3["prev"] = st3["cur"]

                for i in range(NT + 2):
                    if i < NT:
                        stage1(i)
                    if 0 <= i - 1 < NT:
                        stage2(i - 1)
                    if 0 <= i - 2 < NT:
                        stage3(i - 2)
                if hp == 3:
                    S.barrier()
                TT("pool", acc[0:64, 0, :], acc[0:64, 0, :], GT[0:64, :], ALU.mult,
                   ACCK + ["FT"] + GTK, ["accgA"])
                TT("pool", acc[64:128, 1, :], acc[64:128, 1, :], GT[64:128, :], ALU.mult,
                   ACCK + ["FT"] + GTK, ["accgB"])
                RCP(rl[0:64, :], acc[64:128, 0, :], ACCK + ["FT"], ["rlA"])
                RCP(rl[64:128, :], acc[0:64, 1, :], ACCK + ["FT"], ["rlB"])
                TT("pool", attT[0:64, hp, :], acc[0:64, 0, :], rl[0:64, :], ALU.mult,
                   ACCK + ["accgA", "rlA", "FT"], [("attT", hp)])
                TT("pool", attT[64:128, hp, :], acc[64:128, 1, :], rl[64:128, :], ALU.mult,
                   ACCK + ["accgB", "rlB", "FT"], [("attT", hp)])

            AH.reset(); AF_.reset()
            wC = [AH.alloc(4096).rearrange("p (a b) -> p a b", b=512) for _ in range(5)]
            wo = AH.alloc(12288).rearrange("p (a b) -> p a b", b=1024)
            uT = AH.alloc(2048).rearrange("p (a b) -> p a b", b=512)
            sgT = AH.alloc(2048).rearrange("p (a b) -> p a b", b=512)
            sgmT = AH.alloc(2048).rearrange("p (a b) -> p a b", b=512)
            vvn = [AH.alloc(512) for _ in range(4)]
            ysgT = AH.alloc(2048).rearrange("p (a b) -> p a b", b=512)
            qmT = AH.alloc(2048).rearrange("p (a b) -> p a b", b=512)
            PTm = [AH.alloc(512) for _ in range(4)]
            ymemT = AH.alloc(2048).rearrange("p (a b) -> p a b", b=512)
            sqj = AH.alloc(1024)
            vv = [AF_.alloc(512) for _ in range(4)]
            rlm = [AF_.alloc(512), AF_.alloc(512)]
            xr = [AF_.alloc(1024) for _ in range(4)]
            fgbc = AF_.alloc(1024)
            ms = [AF_.alloc(512).rearrange("p (a b) -> p a b", b=128) for _ in range(2)]
            tO = [AF_.alloc(512) for _ in range(2)]
            usg = AH.alloc(2048).rearrange("p (a b) -> p a b", b=512)

            cgrp = [2048, 2560, 3072, 3584, 4096]
            for gi in (1, 0, 2, 4, 3):
                cc = cgrp[gi]
                for k0 in (0, 4):
                    DMA("sp", wC[gi][:, k0:k0 + 4, :], w_in_v[:, k0:k0 + 4, cc:cc + 512], WBIN2,
                        [("wC", gi, kc) for kc in range(k0, k0 + 4)])
            for k0 in (0, 4, 8):
                DMA("sp", wo[:, k0:k0 + 4, :], w_out_v[:, k0:k0 + 4, :], WBOUT,
                    [("wo", kc) for kc in range(k0, k0 + 4)])
            DMA("sp", fgbc, final_g.partition_broadcast(128)[:, 0, :], [], ["fgbc", "FT"])
            w_u, w_v2, w_gs, w_qm, w_gm = wC
            free_banks = list(range(8))

            def balloc():
                assert free_banks, "PSUM banks exhausted"
                return free_banks.pop(0)

            def bfree(k):
                free_banks.append(k)

            def balloc_pair():
                best = None
                for a_ in free_banks:
                    if a_ < 7 and (a_ + 1) in free_banks:
                        sc = max(free_banks.index(a_), free_banks.index(a_ + 1))
                        if best is None or sc < best[0]:
                            best = (sc, a_)
                assert best is not None, "no adjacent PSUM bank pair free"
                a_ = best[1]
                free_banks.remove(a_)
                free_banks.remove(a_ + 1)
                return a_

            bst = [dict() for _ in range(4)]

            def groups_P(bb):
                hv = hT_own[:, :, bb * 512:(bb + 1) * 512]
                hk = [("hT", 8 + 4 * bb + i) for i in range(4)]
                stt_ = [dict() for _ in range(4)]
                bst[bb]["stt"] = stt_
                gl = []

                def vv_group(tt):
                    d_ = stt_[tt]
                    k = balloc()
                    for kc in range(8):
                        MM(bank(k), hv[:, kc, tt * 128:(tt + 1) * 128], w_v2[:, kc, :],
                           kc == 0, kc == 7, [("wC", 1, kc)] + hk, [PB(k)])
                    d_["sm"], d_["ksm"] = scol()
                    ACT(vv[tt], bank(k), AF.Gelu_apprx_tanh, [PB(k)],
                        [("vv", tt), d_["ksm"]], accum_out=d_["sm"])
                    bfree(k)

                def fm_group(wt, gi, dst, dkey, fn, ft):
                    k = balloc()
                    for kc in range(8):
                        MM(bank(k), wt[:, kc, ft * 128:(ft + 1) * 128], hv[:, kc, :],
                           kc == 0, kc == 7, [("wC", gi, kc)] + hk, [PB(k)])
                    ACT(dst[:, ft, :], bank(k), fn, [PB(k)], [(dkey, ft)])
                    bfree(k)

                for tt in range(4):
                    gl.append(lambda tt=tt: vv_group(tt))
                for (wt, gi, dst, dkey, fn) in ((w_u, 0, uT, "uT", AF.Gelu_apprx_tanh),
                                                (w_gs, 2, sgT, "sgT", AF.Silu),
                                                (w_qm, 3, qmT, "qmT", AF.Copy),
                                                (w_gm, 4, sgmT, "sgmT", AF.Silu)):
                    for ft in range(4):
                        gl.append(lambda wt=wt, gi=gi, dst=dst, dkey=dkey, fn=fn, ft=ft:
                                  fm_group(wt, gi, dst, dkey, fn, ft))
                return gl

            filler = []

            def fill(n):
                for _ in range(n):
                    if filler:
                        filler.pop(0)()

            def stage_M(bb):
                ft_ = ["FT"] if bb == 0 else []
                stt_ = bst[bb]["stt"]
                for tt in range(4):
                    d_ = stt_[tt]
                    d_["sq2"], d_["ksq"] = scol()
                    ACT(sqj[:, 0:512], vv[tt], AF.Square, [("vv", tt)], ["sqj", d_["ksq"]],
                        accum_out=d_["sq2"])
                fen, kfen = scol()
                ACT(fen, stt_[3]["sq2"], AF.Copy, [stt_[i_]["ksq"] for i_ in range(4)], [kfen])
                for tt in range(4):
                    d_ = stt_[tt]
                    d_["mean"], d_["kmean"] = scol()
                    TS("dve", d_["mean"], d_["sm"], 1.0 / 512, None, ALU.mult, None,
                       [d_["ksm"], kfen], [d_["kmean"]])
                for tt in range(4):
                    d_ = stt_[tt]
                    d_["m2"], d_["km2"] = scol()
                    TT("dve", d_["m2"], d_["mean"], d_["mean"], ALU.mult, [d_["kmean"]], [d_["km2"]])
                for tt in range(4):
                    d_ = stt_[tt]
                    d_["var"], d_["kvar"] = scol()
                    STT("dve", d_["var"], d_["sq2"], 1.0 / 512, d_["m2"], ALU.mult, ALU.subtract,
                        [d_["ksq"], d_["km2"], kfen], [d_["kvar"]])
                for tt in range(4):
                    d_ = stt_[tt]
                    d_["sd"], d_["ksd"] = scol()
                    ACT(d_["sd"], d_["var"], AF.Sqrt, [d_["kvar"]], [d_["ksd"]], bias=EPS, scale=1.0)
                for tt in range(4):
                    d_ = stt_[tt]
                    d_["rs"], d_["krs"] = scol()
                    RCP(d_["rs"], d_["sd"], [d_["ksd"]], [d_["krs"]])
                for tt in range(4):
                    d_ = stt_[tt]
                    d_["nb"], d_["knb"] = scol()
                    STT("dve", d_["nb"], d_["mean"], -1.0, d_["rs"], ALU.mult, ALU.mult,
                        [d_["kmean"], d_["krs"]], [d_["knb"]])
                for tt in range(4):
                    d_ = stt_[tt]
                    ACT(vvn[tt], vv[tt], AF.Identity, [("vv", tt), d_["krs"], d_["knb"]],
                        [("vvn", tt)], bias=d_["nb"], scale=d_["rs"])
                TT("pool", usg, uT, sgT, ALU.mult,
                   [("uT", f_) for f_ in range(4)] + [("sgT", f_) for f_ in range(4)], ["usg"])
                fill(8)
                mixb = []
                for tt in range(4):
                    km_ = balloc()
                    mixb.append(km_)
                    for j in range(4):
                        for hh in range(2):
                            g = 2 * j + hh
                            MM(bank(km_)[hh * 64:(hh + 1) * 64, j * 128:(j + 1) * 128],
                               vvn[tt][:, g * 64:(g + 1) * 64], WsT[:, g, :], True, True,
                               [("vvn", tt), "WsT"], [PB(km_)])
                fill(4)
                for tt in range(4):
                    i = tt % 2
                    bmv = bank(mixb[tt]).rearrange("p (a b) -> p a b", b=128)
                    for j in range(4):
                        STT("dve", ms[i][:, j, :], bmv[:, j, :], lng[:, j:j + 1], Bp[:, j, :],
                            ALU.mult, ALU.add, [PB(mixb[tt]), "lng", "Bp"], [("ms", i, j)] + ft_)
                    bfree(mixb[tt])
                    TT("pool", ysgT[:, :, tt * 128:(tt + 1) * 128], ms[i],
                       usg[:, :, tt * 128:(tt + 1) * 128], ALU.mult,
                       [("ms", i, j_) for j_ in range(4)] + ["usg"], [("ysgT", tt)])
                sbk = {}

                def scores(h):
                    for mt in range(2):
                        k = balloc()
                        sbk[(h, mt)] = k
                        MM(bank(k), kmT[:, h, mt * 128:(mt + 1) * 128], qmT[:, h, :], True, True,
                           ["kmT", ("qmT", h)], [PB(k)])

                scores(0)
                scores(1)
                for h in range(4):
                    if h + 2 < 4:
                        scores(h + 2)
                    for mt in range(2):
                        pi = (2 * h + mt) % 4
                        ACT(PTm[pi], bank(sbk[(h, mt)]), AF.Exp, [PB(sbk[(h, mt)])], [("PTm", pi)],
                            scale=1.0 / math.sqrt(128.0))
                        bfree(sbk[(h, mt)])
                    kO = balloc()
                    kL = balloc()
                    for mt in range(2):
                        pi = (2 * h + mt) % 4
                        MM(bank(kL), ones[:], PTm[pi], mt == 0, mt == 1,
                           ["ones", ("PTm", pi)], [PB(kL)])
                    for mt in range(2):
                        pi = (2 * h + mt) % 4
                        MM(bank(kO), vm[:, mt, h * 128:(h + 1) * 128], PTm[pi], mt == 0, mt == 1,
                           ["vm", ("PTm", pi)], [PB(kO)])
                    i = h % 2
                    RCP(rlm[i], bank(kL), [PB(kL)], [("rlm", i)] + ft_)
                    bfree(kL)
                    TT("dve", tO[i], bank(kO), sgmT[:, h, :], ALU.mult, [PB(kO), ("sgmT", h)],
                       [("tO", i)] + ft_)
                    bfree(kO)
                    TT("dve", ymemT[:, h, :], tO[i], rlm[i], ALU.mult, [("tO", i), ("rlm", i)],
                       [("ymemT", h)])
                    if h == 1:
                        fill(4)
                    if h == 3:
                        fill(4)

            def stage_O(bb):
                ft_ = ["FT"] if bb == 0 else []
                for tt in range(4):
                    own0 = bb * 512 + tt * 128
                    r0 = s * SBT + HALO + own0
                    DMA("sp", xr[tt], xs[r0:r0 + 128, :], [], [("xr", tt)] + ft_)
                for tt in range(4):
                    own0 = bb * 512 + tt * 128
                    k0_ = balloc_pair()
                    ky = [k0_, k0_ + 1]
                    for half in range(2):
                        for kc in range(12):
                            if kc < 4:
                                lt, lk = attT[:, kc, own0:own0 + 128], ("attT", kc)
                            elif kc < 8:
                                lt, lk = ysgT[:, kc - 4, tt * 128:(tt + 1) * 128], ("ysgT", tt)
                            else:
                                lt, lk = ymemT[:, kc - 8, tt * 128:(tt + 1) * 128], ("ymemT", kc - 8)
                            MM(bank(ky[half]), lt, wo[:, kc, half * 512:(half + 1) * 512],
                               kc == 0, kc == 11, [lk, ("wo", kc)], [PB(ky[half])])
                    xv = xr[tt].rearrange("p (a b) -> p a b", b=512)
                    TT("dve", xv, pball[:, ky[0]:ky[0] + 2, :], xv, ALU.add,
                       [PB(ky[0]), PB(ky[1]), ("xr", tt)], [("xr", tt)])
                    bfree(ky[0])
                    bfree(ky[1])
                rss = []
                for tt in range(4):
                    ss, kss = scol()
                    ACT(sqj, xr[tt], AF.Square, [("xr", tt)], ["sqj", kss], accum_out=ss)
                    rss.append([ss, kss])
                for tt in range(4):
                    sd, ksd = scol()
                    ACT(sd, rss[tt][0], AF.Sqrt, [rss[tt][1]], [ksd], bias=EPS, scale=1.0 / DM)
                    rss[tt] += [sd, ksd]
                for tt in range(4):
                    rs, krs = scol()
                    RCP(rs, rss[tt][2], [rss[tt][3]], [krs])
                    rss[tt] += [rs, krs]
                for tt in range(4):
                    rs, krs = rss[tt][4], rss[tt][5]
                    STT("dve", xr[tt], xr[tt], rs, fgbc, ALU.mult, ALU.mult,
                        [("xr", tt), krs, "fgbc"], [("xr", tt)])
                    o0 = s * SBT + bb * 512 + tt * 128
                    DMA("sp", out[o0:o0 + 128, :], xr[tt], [("xr", tt)], [("out", s, bb, tt)])

            filler.extend(groups_P(0))
            fill(20)
            for bb in range(4):
                if bb + 1 < 4:
                    filler.extend(groups_P(bb + 1))
                stage_M(bb)
                fill(20)
                stage_O(bb)
            S.barrier()
        S.emit()
    return nc


def _t5_bucket(rel):
    nb_, md = 32, 1024
    half = nb_ // 2
    max_exact = half // 2
    ret = (rel > 0).astype(np.int32) * half
    n = np.abs(rel)
    large = max_exact + (np.log(np.maximum(n, 1).astype(np.float32) / max_exact)
                         / math.log(md / max_exact) * (half - max_exact)).astype(np.int32)
    large = np.minimum(large, half - 1)
    return (ret + np.where(n < max_exact, n, large)).astype(np.int32)


def _bias_layout(rel_bias):
    rel_bias = np.asarray(rel_bias, np.float32)
    kk = np.arange(128)[:, None]
    qq = np.arange(256)[None, :]
    rel = kk - qq + 64
    band = np.abs(rel) <= 64
    outp = np.empty((128, 4, 3, 2, 256), np.float32)
    for ci, d in enumerate(CFGS):
        bucket = _t5_bucket(rel * d)
        g = rel_bias[bucket]
        for h in range(8):
            m = g[:, :, h].copy()
            m[~band] = NEG
            outp[:, h // 2, ci, h % 2, :] = m
    return np.ascontiguousarray(outp.reshape(128, 4, 1536))


_NC_CACHE = {}


def _run(inputs, NSB):
    x = np.asarray(inputs["x"], np.float32)
    B, S, _ = x.shape
    per_core = NSB * SBT
    cps = S // per_core
    n_cores = B * cps
    if NSB not in _NC_CACHE:
        _NC_CACHE[NSB] = build_nc(NSB)
    nc = _NC_CACHE[NSB]
    biasM = _bias_layout(inputs["rel_bias"])
    ident = np.eye(128, dtype=np.float32)
    f = lambda k: np.ascontiguousarray(np.asarray(inputs[k], np.float32))
    w_in = f("w_in")[0]
    w_mkv = f("w_mem_kv")[0]
    w_out = f("w_out")[0]
    sg_w = f("sg_w")[0]
    sg_b = f("sg_b")[0]
    in_maps = []
    for c in range(n_cores):
        b, part = divmod(c, cps)
        t0 = part * per_core
        slab = np.zeros((per_core + 2 * HALO, DM), np.float32)
        lo, hi = t0 - HALO, t0 + per_core + HALO
        slo, shi = max(lo, 0), min(hi, S)
        slab[slo - lo:shi - lo] = x[b, slo:shi]
        em = np.zeros((128, 2 * NSB), np.float32)
        for s in range(NSB):
            g0 = t0 + s * SBT
            if g0 == 0:
                em[0:64, 2 * s] = NEG
            if g0 + SBT == S:
                em[64:128, 2 * s + 1] = NEG
        in_maps.append({
            "xs": slab, "mem": f("mem")[b], "w_in": w_in, "w_mem_kv": w_mkv, "w_out": w_out,
            "norm_g": f("norm_g").reshape(1, DM), "mem_norm_g": f("mem_norm_g").reshape(1, DM),
            "final_norm_g": f("final_norm_g").reshape(1, DM),
            "sg_ln_g": f("sg_ln_g").reshape(1, 512), "sg_ln_b": f("sg_ln_b").reshape(1, 512),
            "sg_w": sg_w, "sg_b": sg_b, "biasM": biasM, "emask": em, "ident": ident,
        })
    res = run_bass_kernel_spmd(nc, in_maps, core_ids=list(range(n_cores)))
    outp = np.empty((B, S, DM), np.float32)
    for c in range(n_cores):
        b, part = divmod(c, cps)
        t0 = part * per_core
        outp[b, t0:t0 + per_core] = res.results[c]["out"]
    return outp


def kernel(**inputs):
    return _run(inputs, 2)
```
